# Optimizing a Trainium2 kernel written in Bass

```python
import math, functools
import jax, jax.numpy as jnp
from jax import lax
import numpy as np

D_MODEL = 4096
BATCH = 4
SEQ = 4096
DEPTH = 1

MEM_LEN = 256
RMS_EPS = 1e-6
GLA_HEADS = 4
GLA_DV = D_MODEL // 2 // GLA_HEADS
GLA_DK = GLA_DV // 2
GLA_LOWRANK = 16
GLA_TAU = 16.0
GLA_CHUNK = 64
DIL_HEAD_DIM = 128
DIL_HEADS = D_MODEL // 2 // DIL_HEAD_DIM
DIL_CONFIGS = ((128, 1), (512, 4), (2048, 16))
REL_BUCKETS = 32
REL_MAX_DIST = 2048
XATTN_HEADS = 4
XATTN_HEAD_DIM = 128
XATTN_WIDTH = XATTN_HEADS * XATTN_HEAD_DIM
D_FF = ((8 * D_MODEL // 3 + 255) // 256) * 256
CONV_WIDTH = 3
MIX_WIDTH = GLA_HEADS * GLA_DV + DIL_HEADS * DIL_HEAD_DIM
NEG_INF = -1e30

kernel_name = 'hybrid_gla_dilated_parallel_heads'


def _proj_sizes():
    return (GLA_HEADS * GLA_DK,
            GLA_HEADS * GLA_DK,
            GLA_HEADS * GLA_DV,
            GLA_LOWRANK,
            GLA_HEADS * GLA_DV,
            DIL_HEADS * DIL_HEAD_DIM,
            DIL_HEADS * DIL_HEAD_DIM,
            DIL_HEADS * DIL_HEAD_DIM)


def rmsnorm(x, g):
    xf = x.astype(jnp.float32)
    y = xf * lax.rsqrt(jnp.mean(xf * xf, axis=-1, keepdims=True) + RMS_EPS)
    return (y * g.astype(jnp.float32)).astype(x.dtype)


def t5_bucket(dist):
    max_exact = REL_BUCKETS // 2
    d_f = jnp.maximum(dist, 1).astype(jnp.float32)
    large = max_exact + (jnp.log(d_f / max_exact) / math.log(REL_MAX_DIST / max_exact)
                         * (REL_BUCKETS - max_exact)).astype(jnp.int32)
    large = jnp.minimum(large, REL_BUCKETS - 1)
    return jnp.where(dist < max_exact, dist, large)


def gla_chunked(q, k, v, g_log):
    B, S, H, DK = q.shape
    DV = v.shape[-1]
    C = GLA_CHUNK
    N = S // C

    def chunk(t):
        return t.reshape(B, N, C, H, -1).transpose(1, 0, 3, 2, 4)

    q, k, v, g = map(chunk, (q * DK ** -0.5, k, v, g_log))
    b = jnp.cumsum(g, axis=3)
    b_last = b[:, :, :, -1:, :]
    b_mid = b[:, :, :, C // 2:C // 2 + 1, :]
    q_start = q * jnp.exp(b)
    k_end = k * jnp.exp(b_last - b)
    chunk_decay = jnp.exp(b_last[:, :, :, 0, :])

    def step(state, inp):
        q_c, k_c, v_c, dec = inp
        o_c = jnp.einsum('bhcd,bhde->bhce', q_c, state)
        state = state * dec[..., None] + jnp.einsum('bhcd,bhce->bhde', k_c, v_c)
        return state, o_c

    state0 = jnp.zeros((B, H, DK, DV), jnp.float32)
    _, o_inter = lax.scan(step, state0, (q_start, k_end, v, chunk_decay))

    att = jnp.einsum('nbhcd,nbhjd->nbhcj', q * jnp.exp(b - b_mid), k * jnp.exp(b_mid - b))
    att = jnp.where(jnp.tril(jnp.ones((C, C), bool)), att, 0.0)
    o = o_inter + jnp.einsum('nbhcj,nbhje->nbhce', att, v)
    return o.transpose(1, 0, 3, 2, 4).reshape(B, S, H, DV)


def dilated_branch(q, k, v, dilation, steps, rel_bias):
    B, S, H, E = q.shape
    L = S // dilation
    nb = -(-L // steps)
    Lp = nb * steps

    def to_blocks(t):
        t = t.reshape(B, L, dilation, H, E)
        t = jnp.pad(t, ((0, 0), (0, Lp - L), (0, 0), (0, 0), (0, 0)))
        return t.reshape(B, nb, steps, dilation, H, E)

    def with_prev(t):
        prev = jnp.pad(t, ((0, 0), (1, 0), (0, 0), (0, 0), (0, 0), (0, 0)))[:, :-1]
        return jnp.concatenate([prev, t], axis=2)

    qb, kb, vb = to_blocks(q), to_blocks(k), to_blocks(v)
    kw, vw = with_prev(kb), with_prev(vb)
    logits = jnp.einsum('bnqrhe,bnkrhe->bnrhqk', qb, kw,
                        preferred_element_type=jnp.float32) * (E ** -0.5)

    qi = jnp.arange(steps)[:, None]
    kj = jnp.arange(2 * steps)[None, :]
    rel = qi + steps - kj
    band = (rel >= 0) & (rel <= steps)
    key_exists = (jnp.arange(nb)[:, None, None] * steps + kj[None] - steps) >= 0
    valid = band[None] & key_exists
    bias = rel_bias[t5_bucket(jnp.clip(rel, 0, steps) * dilation)].astype(jnp.float32)
    logits = logits + jnp.transpose(bias, (2, 0, 1))[None, None, None]
    logits = jnp.where(valid[None, :, None, None], logits, NEG_INF)

    m = jnp.max(logits, axis=-1)
    p = jnp.exp(logits - m[..., None])
    s = jnp.sum(p, axis=-1)
    o = jnp.einsum('bnrhqk,bnkrhe->bnqrhe', p.astype(v.dtype), vw,
                   preferred_element_type=jnp.float32)

    o = o.reshape(B, Lp, dilation, H, E)[:, :L].reshape(B, S, H, E)
    m = jnp.transpose(m, (0, 1, 4, 2, 3)).reshape(B, Lp, dilation, H)[:, :L].reshape(B, S, H)
    s = jnp.transpose(s, (0, 1, 4, 2, 3)).reshape(B, Lp, dilation, H)[:, :L].reshape(B, S, H)
    return o, m, s


def dilated_attention(q, k, v, rel_bias):
    outs = [dilated_branch(q, k, v, d, w // d, rel_bias) for (w, d) in DIL_CONFIGS]
    m_max = functools.reduce(jnp.maximum, [m for _, m, _ in outs])
    num = jnp.zeros(q.shape, jnp.float32)
    den = jnp.zeros(q.shape[:-1], jnp.float32)
    for o, m, s in outs:
        wgt = jnp.exp(m - m_max)
        num = num + wgt[..., None] * o
        den = den + wgt * s
    return num / den[..., None]


def memory_cross_attention(hn, memn, w_q, w_k, w_v, w_o):
    B, S, _ = hn.shape
    M = memn.shape[1]
    q = (hn @ w_q).reshape(B, S, XATTN_HEADS, XATTN_HEAD_DIM)
    k = (memn @ w_k).reshape(B, M, XATTN_HEADS, XATTN_HEAD_DIM)
    v = (memn @ w_v).reshape(B, M, XATTN_HEADS, XATTN_HEAD_DIM)
    logits = jnp.einsum('bshe,bmhe->bhsm', q, k,
                        preferred_element_type=jnp.float32) * (XATTN_HEAD_DIM ** -0.5)
    p = jax.nn.softmax(logits, axis=-1)
    o = jnp.einsum('bhsm,bmhe->bshe', p.astype(v.dtype), v).reshape(B, S, XATTN_WIDTH)
    return o @ w_o


def causal_dwconv(u, w, b):
    K = w.shape[0]
    S = u.shape[1]
    up = jnp.pad(u, ((0, 0), (K - 1, 0), (0, 0)))
    y = b
    for i in range(K):
        y = y + up[:, i:i + S] * w[i]
    return y


def hybrid_layer(x, mem, rel_bias, norm_mix_g, w_in, gla_w_gate2, gla_b_gate, gla_norm_g,
                 w_out, norm_xattn_g, mem_norm_g, w_xq, w_xk, w_xv, w_xo, norm_ffn_g,
                 w_ffn_gate, w_ffn_up, ffn_conv_w, ffn_conv_b, w_ffn_down):
    B, S, _ = x.shape
    f32 = jnp.float32

    hn = rmsnorm(x, norm_mix_g)
    proj = hn @ w_in
    split_at = np.cumsum(_proj_sizes())[:-1].tolist()
    gq, gk, gv, glr, gr, dq, dk, dv = jnp.split(proj, split_at, axis=-1)

    g_log = jax.nn.log_sigmoid((glr @ gla_w_gate2 + gla_b_gate).astype(f32)) / GLA_TAU
    o_gla = gla_chunked(gq.reshape(B, S, GLA_HEADS, GLA_DK).astype(f32),
                        gk.reshape(B, S, GLA_HEADS, GLA_DK).astype(f32),
                        gv.reshape(B, S, GLA_HEADS, GLA_DV).astype(f32),
                        g_log.reshape(B, S, GLA_HEADS, GLA_DK))
    o_gla = rmsnorm(o_gla, gla_norm_g) * jax.nn.silu(
        gr.astype(f32)).reshape(B, S, GLA_HEADS, GLA_DV)
    o_gla = o_gla.reshape(B, S, GLA_HEADS * GLA_DV)

    o_dil = dilated_attention(dq.reshape(B, S, DIL_HEADS, DIL_HEAD_DIM),
                              dk.reshape(B, S, DIL_HEADS, DIL_HEAD_DIM),
                              dv.reshape(B, S, DIL_HEADS, DIL_HEAD_DIM), rel_bias)
    o_dil = o_dil.reshape(B, S, DIL_HEADS * DIL_HEAD_DIM)

    mix = jnp.concatenate([o_gla, o_dil], axis=-1).astype(x.dtype) @ w_out
    h = x + mix

    h = h + memory_cross_attention(rmsnorm(h, norm_xattn_g), rmsnorm(mem, mem_norm_g),
                                   w_xq, w_xk, w_xv, w_xo)

    hn = rmsnorm(h, norm_ffn_g)
    gate = causal_dwconv(hn @ w_ffn_gate, ffn_conv_w, ffn_conv_b)
    h = h + (jax.nn.silu(gate) * (hn @ w_ffn_up)) @ w_ffn_down
    return h


def setup_inputs(seed: int = 0) -> dict:
    key = jax.random.key(seed)
    ks = jax.random.split(key, 22)
    f32 = jnp.float32

    def nrm(k, shape, scale):
        return jax.random.normal(k, shape, f32) * scale

    def gain(k, shape):
        return 1.0 + 0.05 * jax.random.normal(k, shape, f32)

    L = DEPTH
    n_cols = sum(_proj_sizes())
    return {
        'x': nrm(ks[0], (BATCH, SEQ, D_MODEL), 1.0),
        'mem': nrm(ks[1], (BATCH, MEM_LEN, D_MODEL), 1.0),
        'rel_bias': nrm(ks[2], (REL_BUCKETS, DIL_HEADS), 0.5),
        'norm_mix_g': gain(ks[3], (L, D_MODEL)),
        'w_in': nrm(ks[4], (L, D_MODEL, n_cols), D_MODEL ** -0.5),
        'gla_w_gate2': nrm(ks[5], (L, GLA_LOWRANK, GLA_HEADS * GLA_DK), GLA_LOWRANK ** -0.5),
        'gla_b_gate': nrm(ks[6], (L, GLA_HEADS * GLA_DK), 0.1),
        'gla_norm_g': gain(ks[7], (L, GLA_DV)),
        'w_out': nrm(ks[8], (L, MIX_WIDTH, D_MODEL), MIX_WIDTH ** -0.5),
        'norm_xattn_g': gain(ks[9], (L, D_MODEL)),
        'mem_norm_g': gain(ks[10], (L, D_MODEL)),
        'w_xq': nrm(ks[11], (L, D_MODEL, XATTN_WIDTH), D_MODEL ** -0.5),
        'w_xk': nrm(ks[12], (L, D_MODEL, XATTN_WIDTH), D_MODEL ** -0.5),
        'w_xv': nrm(ks[13], (L, D_MODEL, XATTN_WIDTH), D_MODEL ** -0.5),
        'w_xo': nrm(ks[14], (L, XATTN_WIDTH, D_MODEL), XATTN_WIDTH ** -0.5),
        'norm_ffn_g': gain(ks[15], (L, D_MODEL)),
        'w_ffn_gate': nrm(ks[16], (L, D_MODEL, D_FF), D_MODEL ** -0.5),
        'w_ffn_up': nrm(ks[17], (L, D_MODEL, D_FF), D_MODEL ** -0.5),
        'ffn_conv_w': nrm(ks[18], (L, CONV_WIDTH, D_FF), CONV_WIDTH ** -0.5),
        'ffn_conv_b': nrm(ks[19], (L, D_FF), 0.02),
        'w_ffn_down': nrm(ks[20], (L, D_FF, D_MODEL), D_FF ** -0.5),
        'final_norm_g': gain(ks[21], (D_MODEL,)),
    }


def reference(x, mem, rel_bias, norm_mix_g, w_in, gla_w_gate2, gla_b_gate, gla_norm_g,
              w_out, norm_xattn_g, mem_norm_g, w_xq, w_xk, w_xv, w_xo, norm_ffn_g,
              w_ffn_gate, w_ffn_up, ffn_conv_w, ffn_conv_b, w_ffn_down, final_norm_g):
    h = x
    for i in range(DEPTH):
        h = hybrid_layer(h, mem, rel_bias, norm_mix_g[i], w_in[i], gla_w_gate2[i],
                         gla_b_gate[i], gla_norm_g[i], w_out[i], norm_xattn_g[i],
                         mem_norm_g[i], w_xq[i], w_xk[i], w_xv[i], w_xo[i], norm_ffn_g[i],
                         w_ffn_gate[i], w_ffn_up[i], ffn_conv_w[i], ffn_conv_b[i],
                         w_ffn_down[i])
    return rmsnorm(h, final_norm_g)
```

```python
import numpy as np
from contextlib import ExitStack
import concourse.bass as bass
import concourse.mybir as mybir
from concourse.bass_utils import run_bass_kernel_spmd

F32 = mybir.dt.float32
BF16 = mybir.dt.bfloat16
AF = mybir.ActivationFunctionType
ALU = mybir.AluOpType
AX = mybir.AxisListType

D = 4096
SEQ = 4096
LT = 32
QT0 = 15
NQT = LT - QT0
KC = D // 128
GH, GDK, GDV = 4, 256, 512
DH, DE = 16, 128
DFF = 11008
FC = DFF // 128
XW = 512
MEM = 256
NCOL = 12304
C_GQ, C_GK, C_GV, C_GLR, C_GR, C_DQ, C_DK, C_DV = 0, 1024, 2048, 4096, 4112, 6160, 8208, 10256
EPS = 1e-6
NEG = -30000.0


class Buf:
    def __init__(self, name, t=None):
        self.name = name
        self.t = t
        self.lw = None
        self.rd = []
        self.sem = None
        self.cnt = 0

    def __getitem__(self, k):
        return self.t[k]


class Eng:
    def __init__(self, name, same_raw):
        self.name = name
        self.ops = []
        self.cnt = 0
        self.waited = {}
        self.sem = None
        self.same_raw = same_raw


class Sched:
    def __init__(self, nc, es):
        self.nc = nc
        self.es = es
        self.pe = Eng("pe", False)
        self.act = Eng("act", True)
        self.dve = Eng("dve", True)
        self.pool = Eng("pool", True)
        self.sp = Eng("sp", False)
        self.engs = [self.pe, self.act, self.dve, self.pool, self.sp]
        for e in self.engs[:4]:
            e.sem = es.enter_context(nc.semaphore("sem_" + e.name))
        self.dsems = []
        self.free_dsems = []
        self.nsem = 0

    def dma_sem(self, barrier=True):
        if barrier and self.free_dsems:
            return self.free_dsems.pop()
        s = self.es.enter_context(self.nc.semaphore("dsem%d" % self.nsem))
        self.nsem += 1
        rec = [s, 0]
        if barrier:
            self.dsems.append(rec)
        return rec

    def _waits(self, eng, deps):
        ws = []
        for d in deps:
            if d is None:
                continue
            sem, val = d
            if sem is eng.sem and not eng.same_raw:
                continue
            k = id(sem)
            if eng.waited.get(k, 0) < val:
                eng.waited[k] = val
                ws.append((sem, val))
        return ws

    def op(self, eng, fn, reads=(), writes=(), signal=True):
        deps = []
        for b in reads:
            deps.append(b.lw)
        for b in writes:
            if b.lw is not None and b.lw[0] is not eng.sem:
                deps.append(b.lw)
            for r in b.rd:
                if r[0] is not eng.sem:
                    deps.append(r)
        ws = self._waits(eng, deps)
        val = eng.cnt + 1
        if signal:
            eng.cnt = val
            eng.ops.append((ws, fn, (eng.sem, 1)))
        else:
            eng.ops.append((ws, fn, None))
        for b in reads:
            b.rd.append((eng.sem, val))
        for b in writes:
            b.lw = (eng.sem, val)
            b.rd = []

    def dma(self, eng, fn, out, in_):
        deps = []
        if in_ is not None:
            deps.append(in_.lw)
        if out is not None:
            deps.append(out.lw)
            deps.extend(out.rd)
        ws = self._waits(eng, deps)
        tgt = out if out is not None else in_
        if tgt.sem is None:
            tgt.sem = self.dma_sem()
        tgt.sem[1] += 16
        key = (tgt.sem[0], tgt.sem[1])
        eng.ops.append((ws, fn, (tgt.sem[0], 16)))
        if out is not None:
            out.lw = key
            out.rd = []
        if in_ is not None:
            in_.rd.append(key)

    def barrier(self, release=()):
        for e in self.engs:
            deps = []
            for o in self.engs[:4]:
                if o is not e and o.cnt > 0:
                    deps.append((o.sem, o.cnt))
            for rec in self.dsems:
                if rec[1] > 0:
                    deps.append((rec[0], rec[1]))
            ws = self._waits(e, deps)
            if ws:
                e.ops.append((ws, None, None))
        for b in release:
            if b.sem is not None:
                self.free_dsems.append(b.sem)
                b.sem = None

    def emit(self, block):
        def run(eng, h):
            for ws, fn, inc in eng.ops:
                for sem, val in ws:
                    h.wait_ge(sem, val)
                if fn is not None:
                    ins = fn(h)
                    if inc is not None:
                        ins.then_inc(inc[0], inc[1])

        @block.tensor
        def _(h):
            run(self.pe, h)

        @block.scalar
        def _(h):
            run(self.act, h)

        @block.vector
        def _(h):
            run(self.dve, h)

        @block.gpsimd
        def _(h):
            run(self.pool, h)

        @block.sync
        def _(h):
            run(self.sp, h)


class Prog:
    def __init__(self, debug_outs=(), stages=None):
        self.nc = bass.Bass("TRN2", target_bir_lowering=False)
        self.es = ExitStack()
        self.debug_outs = set(debug_outs)
        self.stages = stages
        self.ins = {}
        self.scr = {}
        self.npsum = 0

    def inp(self, name, shape, dt=F32):
        t = self.nc.dram_tensor(name, list(shape), dt, kind="ExternalInput")
        self.ins[name] = t
        return t.ap()

    def scratch(self, name, shape, dt):
        kind = "ExternalOutput" if name in self.debug_outs else "Internal"
        t = self.nc.dram_tensor(name, list(shape), dt, kind=kind)
        self.scr[name] = t
        return t.ap()

    def sb(self, st, name, shape, dt):
        t = st.enter_context(self.nc.sbuf_tensor(name, list(shape), dt))
        return Buf(name, t)

    def ps(self, st, name, shape, dt):
        t = st.enter_context(self.nc.psum_tensor(name, list(shape), dt))
        return Buf(name, t)


def bcast_row(ap_row, nparts):
    return bass.AP(ap_row.tensor, ap_row.offset, [[0, nparts]] + [list(x) for x in ap_row.ap[1:]])


def build(debug_outs=(), stages=None, opts=None):
    P = Prog(debug_outs, stages)
    P.opts = opts or {}
    nc = P.nc
    es = P.es
    S = Sched(nc, es)
    pe, act, dve, pool, sp = S.pe, S.act, S.dve, S.pool, S.sp

    def want(name):
        return stages is None or name in stages

    x_loc = P.inp("x_loc", [SEQ, D])
    mem = P.inp("mem", [MEM, D])
    rel_bias = P.inp("rel_bias", [32, DH])
    norm_mix_g = P.inp("norm_mix_g", [1, D])
    w_in = P.inp("w_in", [D, NCOL])
    gla_w_gate2 = P.inp("gla_w_gate2", [16, 1024])
    gla_b_gate = P.inp("gla_b_gate", [1, 1024])
    gla_norm_g = P.inp("gla_norm_g", [1, GDV])
    w_out = P.inp("w_out", [D, D])
    norm_xattn_g = P.inp("norm_xattn_g", [1, D])
    mem_norm_g = P.inp("mem_norm_g", [1, D])
    w_xq = P.inp("w_xq", [D, XW])
    w_xk = P.inp("w_xk", [D, XW])
    w_xv = P.inp("w_xv", [D, XW])
    w_xo = P.inp("w_xo", [XW, D])
    norm_ffn_g = P.inp("norm_ffn_g", [1, D])
    w_ffn_gate = P.inp("w_ffn_gate", [D, DFF])
    w_ffn_up = P.inp("w_ffn_up", [D, DFF])
    ffn_conv_w = P.inp("ffn_conv_w", [3, DFF])
    ffn_conv_b = P.inp("ffn_conv_b", [1, DFF])
    w_ffn_down = P.inp("w_ffn_down", [DFF, D])
    final_norm_g = P.inp("final_norm_g", [1, D])
    c_ident = P.inp("c_ident", [128, 128])
    c_gmats = P.inp("c_gmats", [128, 3, 128])
    c_attmask = P.inp("c_attmask", [128, 128])
    c_antiI = P.inp("c_antiI", [128, 2, 256])
    c_onehot = P.inp("c_onehot", [32, 3, 384])
    c_bandneg = P.inp("c_bandneg", [DH, 3, 384])
    c_flags = P.inp("c_flags", [128, 2])
    out = nc.dram_tensor("out", [2048, D], F32, kind="ExternalOutput").ap()

    win_bf = P.scratch("win_bf", [D, NCOL], BF16)
    wout_bf = P.scratch("wout_bf", [D, D], BF16)
    wxq_bf = P.scratch("wxq_bf", [D, XW], BF16)
    wxk_bf = P.scratch("wxk_bf", [D, XW], BF16)
    wxv_bf = P.scratch("wxv_bf", [D, XW], BF16)
    wxo_bf = P.scratch("wxo_bf", [XW, D], BF16)
    wg_bf = P.scratch("wg_bf", [D, DFF], BF16)
    wu_bf = P.scratch("wu_bf", [D, DFF], BF16)
    wd_bf = P.scratch("wd_bf", [DFF, D], BF16)
    hnT = P.scratch("hnT", [8, 128, KC, 512], BF16)
    gqT = P.scratch("gqT", [1024, SEQ], BF16)
    gkT = P.scratch("gkT", [1024, SEQ], BF16)
    gv = P.scratch("gv", [SEQ, 2048], BF16)
    glrT = P.scratch("glrT", [16, SEQ], F32)
    gr = P.scratch("gr", [SEQ, 2048], F32)
    dq = P.scratch("dq", [SEQ, 2048], BF16)
    dk = P.scratch("dk", [SEQ, 2048], BF16)
    dv = P.scratch("dv", [SEQ, 2048], BF16)
    fext_d = P.scratch("fext_d", [3, DH, 384], F32)
    oc = [P.scratch("oc%d" % i, [SEQ, 2048], F32) for i in range(3)]
    msc = [P.scratch("msc%d" % i, [SEQ, 2, DH], F32) for i in range(3)]
    h1 = P.scratch("h1", [SEQ, D], F32)
    h2 = P.scratch("h2", [SEQ, D], F32)
    h3 = P.scratch("h3", [SEQ, D], F32)
    hn2T = P.scratch("hn2T", [5, 128, KC, 512], BF16)
    hn3T = P.scratch("hn3T", [5, 128, KC, 512], BF16)
    memT = P.scratch("memT", [1, 128, KC, 512], BF16)
    kxT = P.scratch("kxT", [128, 4, 256], BF16)
    vx = P.scratch("vx", [128, 2, 512], BF16)
    qxT = P.scratch("qxT", [5, 128, 4, 512], BF16)
    oxT = P.scratch("oxT", [5, 128, 4, 512], BF16)
    hnT32 = P.scratch("hnT32", [128, KC, 128], F32)
    qk32T = P.scratch("qk32T", [128, 16, 128], F32)
    mixT = P.scratch("mixT", [5, 128, KC, 512], BF16)

    with es:
        ident_f = P.sb(es, "ident_f", [128, 128], F32)
        ident_b = P.sb(es, "ident_b", [128, 128], BF16)
        S.dma(sp, lambda h: h.dma_start(out=ident_f[:], in_=c_ident), ident_f, None)
        S.op(dve, lambda h: h.tensor_copy(out=ident_b[:], in_=ident_f[:]), [ident_f], [ident_b])

        def cast_w(cb, dst, src, rows, rstep):
            for r0 in range(0, rows, rstep):
                r1 = min(rows, r0 + rstep)
                S.dma(pool, lambda h, r0=r0, r1=r1: h.dma_start(out=dst[r0:r1, :], in_=src[r0:r1, :],
                                                                max_dma_last_dim=8192), cb, None)
        CB = {}
        for nm in ("win", "mid", "gu", "wd"):
            CB[nm] = Buf("cast_" + nm)
            CB[nm].sem = S.dma_sem(barrier=False)
        if want("cast"):
            cast_w(CB["win"], win_bf, w_in, D, 512)
            cast_w(CB["mid"], wout_bf, w_out, D, 1024)
            cast_w(CB["mid"], wxq_bf, w_xq, D, 4096)
            cast_w(CB["mid"], wxk_bf, w_xk, D, 4096)
            cast_w(CB["mid"], wxv_bf, w_xv, D, 4096)
            cast_w(CB["mid"], wxo_bf, w_xo, XW, 512)

        def norm_stage(tag, src_rows, tiles, gain_row, dstT=None, dst_tok=None, dst_tok_dt=F32, f32T=None):
            with ExitStack() as st:
                gain = P.sb(st, tag + "_gain", [128, D], F32)
                S.dma(sp, lambda h: h.dma_start(out=gain[:], in_=bcast_row(gain_row, 128)), gain, None)
                xt = [P.sb(st, tag + "_x%d" % i, [128, D], F32) for i in range(2)]
                junk = P.sb(st, tag + "_junk", [128, D], BF16)
                hn = [P.sb(st, tag + "_hn%d" % i, [128, D], BF16 if dstT is not None else dst_tok_dt) for i in range(2)]
                ss = [P.sb(st, tag + "_ss%d" % i, [128, 4], F32) for i in range(2)]
                mhalf = P.sb(st, tag + "_mh", [128, 1], F32)
                S.op(dve, lambda h: h.memset(mhalf[:], -0.5), [], [mhalf])
                ng = 4
                grp = [P.sb(st, tag + "_grp%d" % i, [128, KC, 512], BF16) for i in range(2)] if dstT is not None else None
                pst = [P.ps(st, tag + "_pt%d" % i, [128, 1024], BF16) for i in range(2)] if dstT is not None else None
                allb = [gain, junk, mhalf] + xt + hn + ss + (grp or [])
                if f32T is not None:
                    hn32 = P.sb(st, tag + "_hn32", [128, D], F32)
                    hT32 = P.sb(st, tag + "_hT32", [128, KC, 128], F32)
                    ps32 = P.ps(st, tag + "_ps32", [128, 512], F32)
                    allb += [hn32, hT32]
                groups = tiles if (len(tiles) > 0 and isinstance(tiles[0], list)) else \
                    [tiles[i:i + ng] for i in range(0, len(tiles), ng)]
                it = 0
                for gi, g in enumerate(groups):
                    gb = grp[gi % 2] if grp else None
                    for ti, t in enumerate(g):
                        xb, hb, sb_ = xt[it % 2], hn[it % 2], ss[it % 2]
                        S.dma(sp, lambda h, xb=xb, t=t: h.dma_start(out=xb[:], in_=src_rows(t)), xb, None)
                        S.op(dve, lambda h, xb=xb, sb_=sb_: h.scalar_tensor_tensor(
                            out=junk[:], in0=xb[:], scalar=1.0, in1=xb[:], op0=ALU.mult, op1=ALU.mult,
                            accum_out=sb_[:, 0:1]), [xb], [junk, sb_])
                        S.op(dve, lambda h, sb_=sb_: h.tensor_scalar(out=sb_[:, 1:2], in0=sb_[:, 0:1], scalar1=1.0 / D,
                                                                     scalar2=EPS, op0=ALU.mult, op1=ALU.add), [sb_], [sb_])
                        S.op(act, lambda h, sb_=sb_: h.activation(out=sb_[:, 2:3], in_=sb_[:, 1:2], func=AF.Ln), [sb_], [sb_])
                        S.op(act, lambda h, sb_=sb_: h.activation(out=sb_[:, 2:3], in_=sb_[:, 2:3], func=AF.Exp, scale=-0.5),
                             [sb_], [sb_])
                        S.op(dve, lambda h, xb=xb, hb=hb, sb_=sb_: h.scalar_tensor_tensor(
                            out=hb[:], in0=xb[:], scalar=sb_[:, 2:3], in1=gain[:], op0=ALU.mult, op1=ALU.mult),
                            [xb, sb_, gain], [hb])
                        if f32T is not None and t == f32T[0]:
                            S.op(dve, lambda h, xb=xb, sb_=sb_: h.scalar_tensor_tensor(
                                out=hn32[:], in0=xb[:], scalar=sb_[:, 2:3], in1=gain[:], op0=ALU.mult, op1=ALU.mult),
                                [xb, sb_, gain], [hn32])
                            for kc4 in range(0, KC, 4):
                                for k in range(4):
                                    kc = kc4 + k
                                    S.op(pe, lambda h, kc=kc, k=k: h.transpose(
                                        out=ps32[:, k * 128:(k + 1) * 128], in_=hn32[:, kc * 128:(kc + 1) * 128],
                                        identity=ident_f[:]), [hn32, ident_f], [ps32], signal=(k == 3))
                                S.op(act, lambda h, kc4=kc4: h.copy(out=hT32[:, kc4:kc4 + 4, :],
                                                                   in_=ps32[:].rearrange("p (k t) -> p k t", k=4)),
                                     [ps32], [hT32])
                            S.dma(act, lambda h: h.dma_start(out=f32T[1], in_=hT32[:]), None, hT32)
                        if dst_tok is not None:
                            S.dma(act, lambda h, hb=hb, t=t: h.dma_start(out=dst_tok(t), in_=hb[:]), None, hb)
                        if dstT is not None:
                            for kc8 in range(0, KC, 8):
                                pb = pst[(kc8 // 8) % 2]
                                for k in range(8):
                                    kc = kc8 + k
                                    S.op(pe, lambda h, pb=pb, hb=hb, kc=kc, k=k: h.transpose(
                                        out=pb[:, k * 128:(k + 1) * 128], in_=hb[:, kc * 128:(kc + 1) * 128],
                                        identity=ident_b[:]), [hb, ident_b], [pb], signal=(k == 7))
                                eng = act if (kc8 // 8) % 2 == 0 else dve
                                if eng is act:
                                    S.op(act, lambda h, pb=pb, gb=gb, kc8=kc8, ti=ti: h.copy(
                                        out=gb[:, kc8:kc8 + 8, ti * 128:(ti + 1) * 128],
                                        in_=pb[:].rearrange("p (k t) -> p k t", k=8)), [pb], [gb])
                                else:
                                    S.op(dve, lambda h, pb=pb, gb=gb, kc8=kc8, ti=ti: h.tensor_copy(
                                        out=gb[:, kc8:kc8 + 8, ti * 128:(ti + 1) * 128],
                                        in_=pb[:].rearrange("p (k t) -> p k t", k=8)), [pb], [gb])
                        it += 1
                    if dstT is not None:
                        nt = len(g) * 128
                        S.dma(act, lambda h, gb=gb, g=g, nt=nt: h.dma_start(out=dstT(g), in_=gb[:, :, 0:nt]), None, gb)
                S.barrier(release=allb)

        if want("norm1"):
            norm_stage("n1", lambda t: x_loc[t * 128:(t + 1) * 128, :], list(range(LT)), norm_mix_g,
                       dstT=lambda g: hnT[g[0] // 4], f32T=(16, hnT32))

        def gemm_stage(tag, groups, xT_of, kcx, T, castbuf, blocks_of, extra_bufs=()):
            with ExitStack() as st:
                xT = [P.sb(st, tag + "_xT%d" % i, [128, kcx, T], BF16) for i in range(2)]
                wb = [P.sb(st, tag + "_wb%d" % i, [128, kcx, 512], BF16) for i in range(2)]
                psb = [P.ps(st, tag + "_ps%d" % i, [128, 512], F32) for i in range(6)]
                pi = 0
                wi = 0
                Tmax = T
                for gi, g in enumerate(groups):
                    if isinstance(g, tuple):
                        g, T = g
                    else:
                        T = Tmax
                    xb = xT[gi % 2]
                    S.dma(sp, lambda h, xb=xb, g=g, T=T: h.dma_start(out=xb[:, :, 0:T], in_=xT_of(g)), xb, None)
                    for blk in blocks_of(g):
                        w_dram, c0, ncols, mode, epi, fin = blk
                        wbb = wb[wi % 2]
                        wi += 1
                        src = w_dram[:, c0:c0 + ncols].rearrange("(kc p) n -> p kc n", p=128)
                        S.dma(sp, lambda h, wbb=wbb, src=src, ncols=ncols: h.dma_start(
                            out=wbb[:, :, 0:ncols], in_=src), wbb, castbuf)
                        if mode == "B":
                            nj = (ncols + 127) // 128
                            for j in range(nj):
                                mcols = min(128, ncols - j * 128)
                                pb = psb[pi % 6]
                                pi += 1
                                for kc in range(kcx):
                                    S.op(pe, lambda h, pb=pb, wbb=wbb, xb=xb, kc=kc, j=j, mcols=mcols, T=T: h.matmul(
                                        pb[0:mcols, 0:T], lhsT=wbb[:, kc, j * 128:j * 128 + mcols], rhs=xb[:, kc, 0:T],
                                        start=(kc == 0), stop=(kc == kcx - 1)), [wbb, xb], [pb], signal=(kc == kcx - 1))
                                epi(pb, j, g, blk)
                        else:
                            for m in range(T // 128):
                                pb = psb[pi % 6]
                                pi += 1
                                for kc in range(kcx):
                                    S.op(pe, lambda h, pb=pb, wbb=wbb, xb=xb, kc=kc, m=m, ncols=ncols: h.matmul(
                                        pb[:, 0:ncols], lhsT=xb[:, kc, m * 128:(m + 1) * 128], rhs=wbb[:, kc, 0:ncols],
                                        start=(kc == 0), stop=(kc == kcx - 1)), [wbb, xb], [pb], signal=(kc == kcx - 1))
                                epi(pb, m, g, blk)
                        if fin is not None:
                            fin(g, blk)
                S.barrier(release=xT + wb + list(extra_bufs))

        evac_rr = [0]

        def evac(out_ap_fn, pb, writes):
            evac_rr[0] += 1
            if evac_rr[0] % 2 == 0:
                S.op(act, lambda h: h.copy(out=out_ap_fn()[0], in_=out_ap_fn()[1]), [pb], writes)
            else:
                S.op(dve, lambda h: h.tensor_copy(out=out_ap_fn()[0], in_=out_ap_fn()[1]), [pb], writes)

        if want("proj"):
            with ExitStack() as st:
                stB = [P.sb(st, "pj_stB%d" % i, [128, 4, 512], BF16) for i in range(2)]
                stF = [P.sb(st, "pj_stF%d" % i, [128, 4, 512], F32) for i in range(2)]
                stL = P.sb(st, "pj_stL", [16, 512], F32)
                rr = {"B": 0, "F": 0}

                def mk_block(c0, ncols, mode, dst, dcol0, dt):
                    state = {}

                    def epi(pb, i, g, blk):
                        if i == 0:
                            key = "F" if dt is F32 else "B"
                            pool_ = stF if dt is F32 else stB
                            state["stg"] = pool_[rr[key] % 2]
                            rr[key] += 1
                        stg = state["stg"]
                        if mode == "B":
                            evac(lambda: (stg[:, i, :], pb[:, 0:512]), pb, [stg])
                        else:
                            evac(lambda: (stg[:, i, 0:ncols], pb[:, 0:ncols]), pb, [stg])

                    def fin(g, blk):
                        stg = state["stg"]
                        if mode == "B":
                            d = dst[dcol0:dcol0 + ncols, g * 512:(g + 1) * 512].rearrange("(j p) t -> p j t", p=128)
                            S.dma(act, lambda h: h.dma_start(out=d, in_=stg[:, 0:ncols // 128, :]), None, stg)
                        else:
                            d = dst[g * 512:(g + 1) * 512, dcol0:dcol0 + ncols].rearrange("(m p) c -> p m c", p=128)
                            S.dma(act, lambda h: h.dma_start(out=d, in_=stg[:, :, 0:ncols]), None, stg)
                    return (win_bf, c0, ncols, mode, epi, fin)

                def glr_block():
                    def epi(pb, i, g, blk):
                        S.op(act, lambda h: h.copy(out=stL[:, :], in_=pb[0:16, 0:512]), [pb], [stL])

                    def fin(g, blk):
                        S.dma(act, lambda h: h.dma_start(out=glrT[:, g * 512:(g + 1) * 512], in_=stL[:, :]), None, stL)
                    return (win_bf, C_GLR, 16, "B", epi, fin)

                kv_blocks = []
                q_blocks = []
                for b in range(2):
                    kv_blocks.append(mk_block(C_GK + b * 512, 512, "B", gkT, b * 512, BF16))
                    q_blocks.append(mk_block(C_GQ + b * 512, 512, "B", gqT, b * 512, BF16))
                kv_blocks.append(glr_block())
                for b in range(4):
                    kv_blocks.append(mk_block(C_GV + b * 512, 512, "A", gv, b * 512, BF16))
                    kv_blocks.append(mk_block(C_DK + b * 512, 512, "A", dk, b * 512, BF16))
                    kv_blocks.append(mk_block(C_DV + b * 512, 512, "A", dv, b * 512, BF16))
                    q_blocks.append(mk_block(C_GR + b * 512, 512, "A", gr, b * 512, F32))
                    q_blocks.append(mk_block(C_DQ + b * 512, 512, "A", dq, b * 512, BF16))

                pj_groups = P.opts.get("pj_groups", list(range(8)))
                gemm_stage("pj", pj_groups, lambda g: hnT[g], KC, 512, CB["win"],
                           lambda g: kv_blocks + (q_blocks if g >= 3 else []),
                           extra_bufs=stB + stF + [stL])

        def A_(fn, r, w):
            S.op(act, fn, r, w)

        def V_(fn, r, w):
            S.op(dve, fn, r, w)

        def G_(fn, r, w):
            S.op(dve, fn, r, w)

        def rstd_act(dst_fn, src_fn, r, w):
            S.op(act, lambda h: h.activation(out=dst_fn(), in_=src_fn(), func=AF.Ln), r, w)
            S.op(act, lambda h: h.activation(out=dst_fn(), in_=dst_fn(), func=AF.Exp, scale=-0.5), w, w)

        def T_(fn, r, w, sig=True):
            S.op(pe, fn, r, w, signal=sig)

        def qg_of_tile(t):
            if t == QT0:
                return 0, 0
            return 1 + (t - 16) // 4, ((t - 16) % 4) * 128

        def transpose_out(src, nfeat_chunks, kc0, t, ptb, mT):
            qg, toff = qg_of_tile(t)
            for c8 in range(0, nfeat_chunks, 8):
                for k in range(8):
                    c = c8 + k
                    T_(lambda h, c=c, k=k: h.transpose(out=ptb[:, k * 128:(k + 1) * 128],
                                                       in_=src[:, c * 128:(c + 1) * 128], identity=ident_b[:]),
                       [src, ident_b], [ptb], sig=(k == 7))
                evac(lambda c8=c8: (mT[:, c8:c8 + 8, :], ptb[:].rearrange("p (k t) -> p k t", k=8)), ptb, [mT])
            S.dma(act, lambda h: h.dma_start(out=mixT[qg][:, kc0:kc0 + nfeat_chunks, toff:toff + 128],
                                              in_=mT[:, 0:nfeat_chunks, :]), None, mT)

        if want("proj32"):
            with ExitStack() as st:
                p32_x = P.sb(st, "p32_x", [128, KC, 128], F32)
                p32_w = [P.sb(st, "p32_w%d" % i, [128, KC, 256], F32) for i in range(2)]
                p32_o = P.sb(st, "p32_o", [128, 16, 128], F32)
                p32_ps = [P.ps(st, "p32_ps%d" % i, [128, 512], F32) for i in range(2)]
                S.dma(sp, lambda h: h.dma_start(out=p32_x[:], in_=hnT32), p32_x, None)
                for blk in range(8):
                    wbb = p32_w[blk % 2]
                    src = w_in[:, blk * 256:(blk + 1) * 256].rearrange("(kc p) n -> p kc n", p=128)
                    S.dma(sp, lambda h, wbb=wbb, src=src: h.dma_start(out=wbb[:], in_=src), wbb, None)
                    for j in range(2):
                        pb = p32_ps[j]
                        for kc in range(KC):
                            T_(lambda h, wbb=wbb, pb=pb, kc=kc, j=j: h.matmul(
                                pb[:, 0:128], lhsT=wbb[:, kc, j * 128:(j + 1) * 128], rhs=p32_x[:, kc, :],
                                start=(kc == 0), stop=(kc == KC - 1)), [wbb, p32_x], [pb], sig=(kc == KC - 1))
                        evac(lambda blk=blk, j=j, pb=pb: (p32_o[:, blk * 2 + j, :], pb[:, 0:128]), pb, [p32_o])
                S.dma(act, lambda h: h.dma_start(out=qk32T, in_=p32_o[:]), None, p32_o)
                S.barrier(release=[p32_x, p32_o] + p32_w)

        if want("cast"):
            cast_w(CB["gu"], wg_bf, w_ffn_gate, D, 512)
            cast_w(CB["gu"], wu_bf, w_ffn_up, D, 512)
            cast_w(CB["wd"], wd_bf, w_ffn_down, DFF, 1376)

        if want("gla"):
            with ExitStack() as st:
                W2 = P.sb(st, "gl_W2", [16, 1024], F32)
                b2 = P.sb(st, "gl_b2", [1, 1024], F32)
                ones1 = P.sb(st, "gl_ones", [1, 128], F32)
                gm = P.sb(st, "gl_gm", [128, 3, 128], F32)
                amask = P.sb(st, "gl_amask", [128, 128], F32)
                gng = P.sb(st, "gl_gng", [128, GDV], F32)
                mhalf = P.sb(st, "gl_mh", [128, 1], F32)
                S.dma(sp, lambda h: h.dma_start(out=W2[:], in_=gla_w_gate2), W2, None)
                S.dma(sp, lambda h: h.dma_start(out=b2[:], in_=gla_b_gate), b2, None)
                S.dma(sp, lambda h: h.dma_start(out=gm[:], in_=c_gmats), gm, None)
                S.dma(sp, lambda h: h.dma_start(out=amask[:], in_=c_attmask), amask, None)
                S.dma(sp, lambda h: h.dma_start(out=gng[:], in_=bcast_row(gla_norm_g, 128)), gng, None)
                G_(lambda h: h.memset(ones1[:], 1.0), [], [ones1])
                G_(lambda h: h.memset(mhalf[:], -0.5), [], [mhalf])
                kTg = [P.sb(st, "gl_kT%d" % i, [128, 8, 512], BF16) for i in range(2)]
                qTg = [P.sb(st, "gl_qT%d" % i, [128, 8, 512], BF16) for i in range(2)]
                vt = [P.sb(st, "gl_v%d" % i, [128, 2048], BF16) for i in range(2)]
                grt = [P.sb(st, "gl_gr%d" % i, [128, 2048], F32) for i in range(2)]
                glr = [P.sb(st, "gl_glr%d" % i, [16, 128], F32) for i in range(2)]
                e_sb = P.sb(st, "gl_e", [128, 1024], F32)
                sp_ = P.sb(st, "gl_sp", [128, 1024], F32)
                kes = P.sb(st, "gl_kes", [128, 1024], F32)
                eb = P.sb(st, "gl_eb", [128, 8, 128], F32)
                e3 = P.sb(st, "gl_e3", [128, 8, 128], F32)
                e3n = P.sb(st, "gl_e3n", [128, 8, 128], F32)
                qsT = P.sb(st, "gl_qsT", [128, 8, 128], BF16)
                qdT = P.sb(st, "gl_qdT", [128, 8, 128], BF16)
                kdT = P.sb(st, "gl_kdT", [128, 8, 128], BF16)
                kend = P.sb(st, "gl_kend", [128, 1024], BF16)
                att_sb = P.sb(st, "gl_att", [128, 128], BF16)
                qk32 = P.sb(st, "gl_qk32", [128, 16, 128], F32)
                qd32 = P.sb(st, "gl_qd32", [128, 8, 128], F32)
                kd32 = P.sb(st, "gl_kd32", [128, 8, 128], F32)
                Sst = [P.sb(st, "gl_S%d" % i, [128, 512], F32) for i in range(8)]
                Sbf = [P.sb(st, "gl_Sb%d" % i, [128, 512], BF16) for i in range(8)]
                ysb = P.sb(st, "gl_y", [128, 512], F32)
                junk = P.sb(st, "gl_junk", [128, 512], BF16)
                nst = P.sb(st, "gl_nst", [128, 4], F32)
                sg = P.sb(st, "gl_sg", [128, 2048], F32)
                og = P.sb(st, "gl_og", [128, 2048], BF16)
                mT = P.sb(st, "gl_mT", [128, 16, 128], BF16)
                zb = P.ps(st, "gl_zb", [128, 1024], F32)
                ktp = P.ps(st, "gl_ktp", [128, 1024], BF16)
                attp = P.ps(st, "gl_attp", [128, 512], F32)
                op_ = P.ps(st, "gl_op", [128, 512], F32)
                kvps = [P.ps(st, "gl_kvp%d" % i, [128, 512], F32) for i in range(3)]
                kvc = [0]
                for i in range(8):
                    G_(lambda h, i=i: h.memset(Sst[i][:], 0.0), [], [Sst[i]])
                    G_(lambda h, i=i: h.memset(Sbf[i][:], 0.0), [], [Sbf[i]])
                gkT_v = gkT.rearrange("(c p) s -> p c s", p=128)
                gqT_v = gqT.rearrange("(c p) s -> p c s", p=128)
                gla_tiles = P.opts.get("gla_tiles", list(range(LT)))

                def gla_tile(t):
                    g4, ti = t // 4, t % 4
                    isq = t >= QT0
                    kg, qg_ = kTg[g4 % 2], qTg[g4 % 2]
                    vb, grb, lrb = vt[t % 2], grt[t % 2], glr[t % 2]
                    tsl = slice(ti * 128, (ti + 1) * 128)
                    if ti == 0 or t == gla_tiles[0]:
                        S.dma(sp, lambda h: h.dma_start(out=kg[:], in_=gkT_v[:, :, g4 * 512:(g4 + 1) * 512]), kg, None)
                        if g4 >= 3:
                            S.dma(sp, lambda h: h.dma_start(out=qg_[:], in_=gqT_v[:, :, g4 * 512:(g4 + 1) * 512]), qg_, None)
                    S.dma(sp, lambda h: h.dma_start(out=vb[:], in_=gv[t * 128:(t + 1) * 128, :]), vb, None)
                    S.dma(sp, lambda h: h.dma_start(out=lrb[:], in_=glrT[:, t * 128:(t + 1) * 128]), lrb, None)
                    if isq:
                        S.dma(sp, lambda h: h.dma_start(out=grb[:], in_=gr[t * 128:(t + 1) * 128, :]), grb, None)
                    for hf in range(2):
                        cs = slice(hf * 512, (hf + 1) * 512)
                        T_(lambda h, cs=cs: h.matmul(zb[:, cs], lhsT=lrb[:, :], rhs=W2[:, cs], start=True, stop=False),
                           [lrb, W2], [zb], sig=False)
                        T_(lambda h, cs=cs: h.matmul(zb[:, cs], lhsT=ones1[:, :], rhs=b2[:, cs], start=False, stop=True),
                           [ones1, b2], [zb])
                    A_(lambda h: h.activation(out=e_sb[:], in_=zb[:], func=AF.Exp, scale=-1.0), [zb], [e_sb])
                    A_(lambda h: h.activation(out=sp_[:], in_=e_sb[:], func=AF.Ln, bias=1.0, scale=1.0), [e_sb], [sp_])
                    for hf in range(2):
                        cs = slice(hf * 512, (hf + 1) * 512)
                        T_(lambda h, cs=cs: h.matmul(zb[:, cs], lhsT=gm[:, 0, :], rhs=sp_[:, cs], start=True, stop=True),
                           [gm, sp_], [zb])
                    A_(lambda h: h.activation(out=kes[:], in_=zb[:], func=AF.Exp, scale=-1.0 / 16), [zb], [kes])
                    for half in range(2):
                        for k in range(4):
                            dc = half * 4 + k
                            T_(lambda h, dc=dc, k=k: h.matmul(zb[:, k * 256:(k + 1) * 256], lhsT=sp_[:, dc * 128:(dc + 1) * 128],
                                                              rhs=gm[:, 1:3, :], start=True, stop=True),
                               [sp_, gm], [zb], sig=(k == 3))
                        ds = slice(half * 4, half * 4 + 4)
                        pEv = zb[:].rearrange("p (k c) -> p k c", k=4)
                        A_(lambda h, ds=ds, pEv=pEv: h.activation(out=eb[:, ds, :], in_=pEv[:, :, 0:128], func=AF.Exp,
                                                          scale=-1.0 / 16), [zb], [eb])
                        if isq:
                            A_(lambda h, ds=ds, pEv=pEv: h.activation(out=e3[:, ds, :], in_=pEv[:, :, 128:256], func=AF.Exp,
                                                              scale=-1.0 / 16), [zb], [e3])
                            A_(lambda h, ds=ds, pEv=pEv: h.activation(out=e3n[:, ds, :], in_=pEv[:, :, 128:256], func=AF.Exp,
                                                              scale=1.0 / 16), [zb], [e3n])
                    if isq:
                        V_(lambda h: h.scalar_tensor_tensor(out=qsT[:], in0=qg_[:, :, tsl], scalar=1.0 / 16, in1=eb[:],
                                                            op0=ALU.mult, op1=ALU.mult), [qg_, eb], [qsT])
                        V_(lambda h: h.scalar_tensor_tensor(out=qdT[:], in0=qg_[:, :, tsl], scalar=1.0 / 16, in1=e3[:],
                                                            op0=ALU.mult, op1=ALU.mult), [qg_, e3], [qdT])
                        V_(lambda h: h.tensor_tensor(out=kdT[:], in0=kg[:, :, tsl], in1=e3n[:], op=ALU.mult),
                           [kg, e3n], [kdT])
                    hp = (t == 16) and want("proj32")
                    if hp:
                        S.dma(sp, lambda h: h.dma_start(out=qk32[:], in_=qk32T), qk32, None)
                        V_(lambda h: h.scalar_tensor_tensor(out=qd32[:], in0=qk32[:, 0:8, :], scalar=1.0 / 16, in1=e3[:],
                                                            op0=ALU.mult, op1=ALU.mult), [qk32, e3], [qd32])
                        V_(lambda h: h.tensor_tensor(out=kd32[:], in0=qk32[:, 8:16, :], in1=e3n[:], op=ALU.mult),
                           [qk32, e3n], [kd32])
                    for dc in range(8):
                        T_(lambda h, dc=dc: h.transpose(out=ktp[:, dc * 128:(dc + 1) * 128], in_=kg[:, dc, tsl],
                                                        identity=ident_b[:]), [kg, ident_b], [ktp], sig=(dc == 7))
                    V_(lambda h: h.tensor_tensor(out=kend[:], in0=ktp[:], in1=kes[:], op=ALU.mult), [ktp, kes], [kend])

                    def head(hh):
                        es_ = slice(hh * 512, (hh + 1) * 512)
                        if isq:
                            for i, dc in enumerate((2 * hh, 2 * hh + 1)):
                                if hp:
                                    T_(lambda h, dc=dc, i=i: h.matmul(attp[:, 0:128], lhsT=kd32[:, dc, :], rhs=qd32[:, dc, :],
                                                                      start=(i == 0), stop=(i == 1)),
                                       [kd32, qd32], [attp], sig=(i == 1))
                                else:
                                    T_(lambda h, dc=dc, i=i: h.matmul(attp[:, 0:128], lhsT=kdT[:, dc, :], rhs=qdT[:, dc, :],
                                                                      start=(i == 0), stop=(i == 1)),
                                       [kdT, qdT], [attp], sig=(i == 1))
                            V_(lambda h: h.tensor_tensor(out=att_sb[:], in0=attp[:, 0:128], in1=amask[:], op=ALU.mult),
                               [attp, amask], [att_sb])
                            T_(lambda h: h.matmul(op_[:, :], lhsT=att_sb[:, :], rhs=vb[:, es_], start=True, stop=False),
                               [att_sb, vb], [op_], sig=False)
                        for ch in range(2):
                            ps_ = slice(ch * 64, (ch + 1) * 64)
                            if isq:
                                for i, dc in enumerate((2 * hh, 2 * hh + 1)):
                                    last = (ch == 1 and i == 1)
                                    T_(lambda h, dc=dc, last=last, ps_=ps_: h.matmul(
                                        op_[ps_, :], lhsT=qsT[:, dc, ps_], rhs=Sbf[dc][:, :], start=False, stop=last),
                                       [qsT, Sbf[dc]], [op_], sig=last)
                            if t == gla_tiles[-1] and ch == 1:
                                continue
                            for dc in (2 * hh, 2 * hh + 1):
                                kvp = kvps[kvc[0] % 3]
                                kvc[0] += 1
                                T_(lambda h, dc=dc, ps_=ps_, kvp=kvp: h.matmul(kvp[:, :], lhsT=kend[ps_, dc * 128:(dc + 1) * 128],
                                                                      rhs=vb[ps_, es_], start=True, stop=True),
                                   [kend, vb], [kvp])
                                V_(lambda h, dc=dc, ch=ch, kvp=kvp: h.scalar_tensor_tensor(
                                    out=Sst[dc][:], in0=Sst[dc][:], scalar=eb[:, dc, ch * 64 + 63:ch * 64 + 64],
                                    in1=kvp[:], op0=ALU.mult, op1=ALU.add), [Sst[dc], eb, kvp], [Sst[dc]])
                                A_(lambda h, dc=dc: h.copy(out=Sbf[dc][:], in_=Sst[dc][:]), [Sst[dc]], [Sbf[dc]])
                        if isq:
                            A_(lambda h: h.activation(out=junk[:], in_=op_[:], func=AF.Square, accum_out=nst[:, 0:1]),
                               [op_], [junk, nst])
                            V_(lambda h: h.tensor_scalar(out=nst[:, 1:2], in0=nst[:, 0:1], scalar1=1.0 / GDV, scalar2=EPS,
                                                         op0=ALU.mult, op1=ALU.add), [nst], [nst])
                            A_(lambda h: h.activation(out=nst[:, 2:3], in_=nst[:, 1:2], func=AF.Ln), [nst], [nst])
                            A_(lambda h: h.activation(out=nst[:, 2:3], in_=nst[:, 2:3], func=AF.Exp, scale=-0.5), [nst], [nst])
                            V_(lambda h: h.scalar_tensor_tensor(out=ysb[:], in0=op_[:], scalar=nst[:, 2:3], in1=gng[:],
                                                                op0=ALU.mult, op1=ALU.mult), [op_, nst, gng], [ysb])
                            V_(lambda h: h.tensor_tensor(out=og[:, es_], in0=ysb[:], in1=sg[:, es_], op=ALU.mult),
                               [ysb, sg], [og])
                    if isq:
                        A_(lambda h: h.activation(out=sg[:], in_=grb[:], func=AF.Silu), [grb], [sg])
                    for hh in range(GH):
                        head(hh)
                    if isq:
                        transpose_out(og, 16, 0, t, ktp, mT)

                for t in gla_tiles:
                    gla_tile(t)
                S.barrier(release=[W2, b2, gm, amask, gng, qk32] + kTg + qTg + vt + grt + glr + [mT])

        DCFG = (1, 4, 16)
        if want("dil"):
            with ExitStack() as st:
                relb = P.sb(st, "dl_relb", [32, DH], F32)
                oneh = P.sb(st, "dl_oneh", [32, 3, 384], F32)
                bneg = P.sb(st, "dl_bneg", [DH, 3, 384], F32)
                antiI = P.sb(st, "dl_antiI", [128, 2, 256], F32)
                flags = P.sb(st, "dl_flags", [128, 2], F32)
                negc = P.sb(st, "dl_negc", [128, 1], F32)
                S.dma(sp, lambda h: h.dma_start(out=relb[:], in_=rel_bias), relb, None)
                S.dma(sp, lambda h: h.dma_start(out=oneh[:], in_=c_onehot), oneh, None)
                S.dma(sp, lambda h: h.dma_start(out=bneg[:], in_=c_bandneg), bneg, None)
                S.dma(sp, lambda h: h.dma_start(out=antiI[:], in_=c_antiI), antiI, None)
                S.dma(sp, lambda h: h.dma_start(out=flags[:], in_=c_flags), flags, None)
                G_(lambda h: h.memset(negc[:], NEG), [], [negc])
                fext_sb = P.sb(st, "dl_fext", [DH, 384], F32)
                Hk = P.sb(st, "dl_Hk", [128, 2, DH, 128], F32)
                tbl = P.sb(st, "dl_tbl", [128, DH, 256], F32)
                Qb = [P.sb(st, "dl_Q%d" % i, [128, 2048], BF16) for i in range(2)]
                Kb = [P.sb(st, "dl_K%d" % i, [128, 2048], BF16) for i in range(2)]
                Vb = [P.sb(st, "dl_V%d" % i, [128, 2048], BF16) for i in range(3)]
                QT = [P.sb(st, "dl_QT%d" % i, [128, DH, 128], BF16) for i in range(2)]
                KT = [P.sb(st, "dl_KT%d" % i, [128, DH, 128], BF16) for i in range(3)]
                Sp = [P.sb(st, "dl_Sp%d" % i, [128, 2, 256], F32) for i in range(2)]
                Pb = [P.sb(st, "dl_P%d" % i, [128, 256], BF16) for i in range(3)]
                PT = [P.sb(st, "dl_PT%d" % i, [128, 4, 256], BF16) for i in range(2)]
                Oall = [P.sb(st, "dl_O%d" % i, [128, 2048], F32) for i in range(2)]
                msb = [P.sb(st, "dl_ms%d" % i, [128, 2, DH], F32) for i in range(2)]
                nmx = [P.sb(st, "dl_nm%d" % i, [128, 2], F32) for i in range(2)]
                trp = [P.ps(st, "dl_trp%d" % i, [128, 1024], BF16) for i in range(2)]
                Sps = [P.ps(st, "dl_Sps%d" % i, [128, 512], F32) for i in range(2)]
                ptp = P.ps(st, "dl_ptp", [128, 1024], BF16)
                ops = [P.ps(st, "dl_ops%d" % i, [128, 512], F32) for i in range(2)]
                fextb = Buf("fext_d")
                ctr = {"u": 0, "v": 0, "k": 0}
                SCALE = DE ** -0.5
                dil_cfgs = P.opts.get("dil_cfgs", [0, 1, 2])

                def build_tables(ci):
                    T_(lambda h: h.matmul(Sps[0][0:DH, 0:384], lhsT=relb[:, :], rhs=oneh[:, ci, :], start=True, stop=True),
                       [relb, oneh], [Sps[0]])
                    V_(lambda h: h.tensor_tensor(out=fext_sb[:], in0=Sps[0][0:DH, 0:384], in1=bneg[:, ci, :], op=ALU.add),
                       [Sps[0], bneg], [fext_sb])
                    S.dma(act, lambda h: h.dma_start(out=fext_d[ci], in_=fext_sb[:]), fextb, fext_sb)
                    for c in range(2):
                        src = bass.AP(fext_d.tensor, ci * DH * 384 + c * 128, [[1, 128], [384, DH], [1, 128]])
                        S.dma(sp, lambda h, c=c, src=src: h.dma_start(out=Hk[:, c, :, :], in_=src), Hk, fextb)
                    for hh in range(DH):
                        pb = Sps[hh % 2]
                        for c in range(2):
                            T_(lambda h, hh=hh, c=c, pb=pb: h.matmul(pb[:, 0:256], lhsT=Hk[:, c, hh, :], rhs=antiI[:, c, :],
                                                                    start=(c == 0), stop=(c == 1)),
                               [Hk, antiI], [pb], sig=(c == 1))
                        evac(lambda hh=hh, pb=pb: (tbl[:, hh, :], pb[:, 0:256]), pb, [tbl])

                def load_rows(buf, src, d, r, b):
                    start = d * 128 * b + r
                    S.dma(sp, lambda h: h.dma_start(out=buf[:], in_=src[start:start + d * 127 + 1:d, :]), buf, None)

                def transpose_all(srcb, dstT):
                    for half in range(2):
                        pb = trp[half]
                        for k in range(8):
                            hh = half * 8 + k
                            T_(lambda h, hh=hh, k=k, pb=pb: h.transpose(out=pb[:, k * 128:(k + 1) * 128],
                                                                        in_=srcb[:, hh * 128:(hh + 1) * 128],
                                                                        identity=ident_b[:]),
                               [srcb, ident_b], [pb], sig=(k == 7))
                        evac(lambda half=half, pb=pb: (dstT[:, half * 8:(half + 1) * 8, :],
                                                       pb[:].rearrange("p (k t) -> p k t", k=8)), pb, [dstT])

                def unit(ci, d, r, b, kt_prev, v_prev, kt_cur, v_cur, cvar):
                    u = ctr["u"]
                    ctr["u"] += 1
                    qb, qt = Qb[u % 2], QT[u % 2]
                    ob, mb = Oall[u % 2], msb[u % 2]
                    load_rows(qb, dq, d, r, b)
                    transpose_all(qb, qt)
                    def pairA(hp):
                        sps = Sps[hp % 2]
                        for i in range(2):
                            hh = hp * 2 + i
                            T_(lambda h, hh=hh, i=i: h.matmul(sps[:, i * 256:i * 256 + 128], lhsT=qt[:, hh, :],
                                                              rhs=kt_prev[:, hh, :], start=True, stop=True),
                               [qt, kt_prev], [sps], sig=False)
                            T_(lambda h, hh=hh, i=i: h.matmul(sps[:, i * 256 + 128:i * 256 + 256], lhsT=qt[:, hh, :],
                                                              rhs=kt_cur[:, hh, :], start=True, stop=True),
                               [qt, kt_cur], [sps], sig=(i == 1))

                    def pairB(hp):
                        sps, spb, nm = Sps[hp % 2], Sp[hp % 2], nmx[hp % 2]
                        V_(lambda h, hp=hp: h.scalar_tensor_tensor(
                            out=spb[:], in0=sps[:].rearrange("p (i k) -> p i k", i=2), scalar=SCALE,
                            in1=tbl[:, 2 * hp:2 * hp + 2, :], op0=ALU.mult, op1=ALU.add), [sps, tbl], [spb])
                        if cvar is not None:
                            V_(lambda h: h.tensor_scalar(out=spb[:, :, 0:128], in0=spb[:, :, 0:128], scalar1=cvar,
                                                         scalar2=None, op0=ALU.add), [spb, flags, negc], [spb])
                        V_(lambda h: h.tensor_reduce(out=nm[:], in_=spb[:], axis=AX.X, op=ALU.max, negate=True),
                           [spb], [nm])
                        V_(lambda h, hp=hp: h.tensor_copy(out=mb[:, 0, 2 * hp:2 * hp + 2], in_=nm[:]), [nm], [mb])
                        for i in range(2):
                            hh = hp * 2 + i
                            pbuf = Pb[hh % 3]
                            A_(lambda h, hh=hh, i=i, pbuf=pbuf: h.activation(
                                out=pbuf[:], in_=spb[:, i, :], func=AF.Exp, bias=nm[:, i:i + 1], scale=1.0,
                                accum_out=mb[:, 1, hh:hh + 1]), [spb, nm], [pbuf, mb])
                            q4 = hh % 4
                            for c in range(2):
                                T_(lambda h, pbuf=pbuf, q4=q4, c=c: h.transpose(
                                    out=ptp[:, q4 * 256 + c * 128:q4 * 256 + (c + 1) * 128],
                                    in_=pbuf[:, c * 128:(c + 1) * 128], identity=ident_b[:]),
                                   [pbuf, ident_b], [ptp], sig=(c == 1))
                        if hp % 2 == 1:
                            g4 = hp // 2
                            ptb, opb = PT[g4 % 2], ops[g4 % 2]
                            evac(lambda ptb=ptb: (ptb[:], ptp[:].rearrange("p (a k) -> p a k", a=4)), ptp, [ptb])
                            for q4 in range(4):
                                hh = g4 * 4 + q4
                                T_(lambda h, hh=hh, q4=q4: h.matmul(opb[:, q4 * 128:(q4 + 1) * 128], lhsT=ptb[:, q4, 0:128],
                                                                    rhs=v_prev[:, hh * 128:(hh + 1) * 128],
                                                                    start=True, stop=False),
                                   [ptb, v_prev], [opb], sig=False)
                                T_(lambda h, hh=hh, q4=q4: h.matmul(opb[:, q4 * 128:(q4 + 1) * 128], lhsT=ptb[:, q4, 128:256],
                                                                    rhs=v_cur[:, hh * 128:(hh + 1) * 128],
                                                                    start=False, stop=True),
                                   [ptb, v_cur], [opb], sig=(q4 == 3))
                            evac(lambda g4=g4, opb=opb: (ob[:, g4 * 512:(g4 + 1) * 512], opb[:]), opb, [ob])
                    pairA(0)
                    for hp_ in range(DH // 2):
                        if hp_ + 1 < DH // 2:
                            pairA(hp_ + 1)
                        pairB(hp_)
                    start = d * 128 * b + r
                    rows = slice(start, start + d * 127 + 1, d)
                    S.dma(act, lambda h: h.dma_start(out=oc[ci][rows, :], in_=ob[:]), None, ob)
                    S.dma(act, lambda h: h.dma_start(out=msc[ci][rows, :, :], in_=mb[:]), None, mb)

                def load_kv(d, r, b):
                    kb = Kb[ctr["k"] % 2]
                    ktb = KT[ctr["k"] % 3]
                    vb = Vb[ctr["k"] % 3]
                    ctr["k"] += 1
                    load_rows(kb, dk, d, r, b)
                    load_rows(vb, dv, d, r, b)
                    transpose_all(kb, ktb)
                    return ktb, vb

                for ci in dil_cfgs:
                    d = DCFG[ci]
                    build_tables(ci)
                    nfirst = 16 // d
                    for r in range(d):
                        if d == 1:
                            qbs = list(range(15, 32))
                        elif d == 4:
                            qbs = list(range(3, 8)) if r >= 2 else list(range(4, 8))
                        else:
                            qbs = [0, 1] if r >= 14 else [1]
                        prev = None
                        for b in qbs:
                            if prev is None and b > 0:
                                prev = load_kv(d, r, b - 1)
                            cur = load_kv(d, r, b)
                            if b == 0:
                                cvar = negc[:, 0:1]
                                pk, pv = cur
                            else:
                                pk, pv = prev
                                cvar = flags[:, 0:1] if (b - 1) < nfirst else None
                            unit(ci, d, r, b, pk, pv, cur[0], cur[1], cvar)
                            prev = cur
                S.barrier(release=[relb, oneh, bneg, antiI, flags, Hk, fextb] + Qb + Kb + Vb + Oall + msb + [fext_sb])

        if want("dilc"):
            with ExitStack() as st:
                O3 = [[P.sb(st, "dc_O%d_%d" % (i, j), [128, 2048], F32) for j in range(3)] for i in range(2)]
                ms3 = [P.sb(st, "dc_ms%d" % i, [128, 3, 2, DH], F32) for i in range(2)]
                mneg = P.sb(st, "dc_mneg", [128, DH], F32)
                w3 = P.sb(st, "dc_w3", [128, 3, DH], F32)
                ws3 = P.sb(st, "dc_ws3", [128, 3, DH], F32)
                den = P.sb(st, "dc_den", [128, DH], F32)
                acc = P.sb(st, "dc_acc", [128, 2048], F32)
                tmp = P.sb(st, "dc_tmp", [128, 2048], F32)
                od = P.sb(st, "dc_od", [128, 2048], BF16)
                mT2 = P.sb(st, "dc_mT", [128, 16, 128], BF16)
                ptb2 = P.ps(st, "dc_ptb", [128, 1024], BF16)
                for it, t in enumerate(P.opts.get("dilc_tiles", list(range(QT0, LT)))):
                    Ob, mb = O3[it % 2], ms3[it % 2]
                    rows = slice(t * 128, (t + 1) * 128)
                    for ci in range(3):
                        S.dma(sp, lambda h, ci=ci, Ob=Ob, rows=rows: h.dma_start(out=Ob[ci][:], in_=oc[ci][rows, :]), Ob[ci], None)
                        S.dma(sp, lambda h, ci=ci, mb=mb, rows=rows: h.dma_start(out=mb[:, ci, :, :], in_=msc[ci][rows, :, :]), mb, None)
                    V_(lambda h, mb=mb: h.tensor_tensor(out=mneg[:], in0=mb[:, 0, 0, :], in1=mb[:, 1, 0, :], op=ALU.min),
                       [mb], [mneg])
                    V_(lambda h, mb=mb: h.tensor_tensor(out=mneg[:], in0=mneg[:], in1=mb[:, 2, 0, :], op=ALU.min),
                       [mb, mneg], [mneg])
                    for ci in range(3):
                        V_(lambda h, ci=ci, mb=mb: h.tensor_tensor(out=w3[:, ci, :], in0=mneg[:], in1=mb[:, ci, 0, :],
                                                                  op=ALU.subtract), [mneg, mb], [w3])
                    A_(lambda h: h.activation(out=w3[:], in_=w3[:], func=AF.Exp), [w3], [w3])
                    V_(lambda h, mb=mb: h.tensor_tensor(out=ws3[:], in0=w3[:], in1=mb[:, :, 1, :], op=ALU.mult),
                       [w3, mb], [ws3])
                    V_(lambda h: h.tensor_tensor(out=den[:], in0=ws3[:, 0, :], in1=ws3[:, 1, :], op=ALU.add), [ws3], [den])
                    V_(lambda h: h.tensor_tensor(out=den[:], in0=den[:], in1=ws3[:, 2, :], op=ALU.add), [ws3, den], [den])
                    V_(lambda h: h.reciprocal(out=den[:], in_=den[:]), [den], [den])
                    for ci in range(3):
                        V_(lambda h, ci=ci: h.tensor_tensor(out=w3[:, ci, :], in0=w3[:, ci, :], in1=den[:], op=ALU.mult),
                           [w3, den], [w3])

                    def bc(ci):
                        return w3[:, ci, :].unsqueeze(2).broadcast_to([128, DH, 128])

                    def v3(b_):
                        return b_[:].rearrange("p (a e) -> p a e", a=DH)
                    V_(lambda h, Ob=Ob: h.tensor_tensor(out=v3(acc), in0=v3(Ob[0]), in1=bc(0), op=ALU.mult),
                       [Ob[0], w3], [acc])
                    G_(lambda h, Ob=Ob: h.tensor_tensor(out=v3(tmp), in0=v3(Ob[1]), in1=bc(1), op=ALU.mult),
                       [Ob[1], w3], [tmp])
                    V_(lambda h: h.tensor_tensor(out=acc[:], in0=acc[:], in1=tmp[:], op=ALU.add), [acc, tmp], [acc])
                    G_(lambda h, Ob=Ob: h.tensor_tensor(out=v3(tmp), in0=v3(Ob[2]), in1=bc(2), op=ALU.mult),
                       [Ob[2], w3], [tmp])
                    V_(lambda h: h.tensor_tensor(out=od[:], in0=acc[:], in1=tmp[:], op=ALU.add), [acc, tmp], [od])
                    transpose_out(od, 16, 16, t, ptb2, mT2)
                S.barrier(release=[mT2] + O3[0] + O3[1] + ms3)

        def qg_rows(qg):
            return (1920, 128) if qg == 0 else (2048 + (qg - 1) * 512, 512)
        QGS = P.opts.get("qgs", [0, 1, 2, 3, 4])
        NORM_Q_GROUPS = [[15]] + [[16 + 4 * i + j for j in range(4)] for i in range(4)]

        def resid_gemm(tag, xT_scr, kcx, w_scr, castbuf, res_src, dst):
            with ExitStack() as st:
                rs = [P.sb(st, tag + "_rs%d" % i, [128, 4, 512], F32) for i in range(2)]
                so = [P.sb(st, tag + "_so%d" % i, [128, 4, 512], F32) for i in range(2)]
                rr = [0]

                def mk(c0):
                    state = {}

                    def epi(pb, m, g, blk):
                        r0, T = qg_rows(g)
                        if m == 0:
                            state["rs"], state["so"] = rs[rr[0] % 2], so[rr[0] % 2]
                            rr[0] += 1
                            rsb = state["rs"]
                            srcv = res_src[r0:r0 + T, c0:c0 + 512].rearrange("(m p) c -> p m c", p=128)
                            S.dma(sp, lambda h: h.dma_start(out=rsb[:, 0:T // 128, :], in_=srcv), rsb, None)
                        rsb, sob = state["rs"], state["so"]
                        V_(lambda h: h.tensor_tensor(out=sob[:, m, :], in0=pb[:, :], in1=rsb[:, m, :], op=ALU.add),
                           [pb, rsb], [sob])

                    def fin(g, blk):
                        r0, T = qg_rows(g)
                        sob = state["so"]
                        dv_ = dst[r0:r0 + T, c0:c0 + 512].rearrange("(m p) c -> p m c", p=128)
                        S.dma(act, lambda h: h.dma_start(out=dv_, in_=sob[:, 0:T // 128, :]), None, sob)
                    return (w_scr, c0, 512, "A", epi, fin)
                blocks = [mk(c0) for c0 in range(0, D, 512)]
                gemm_stage(tag, [(g, qg_rows(g)[1]) for g in QGS], lambda g: xT_scr[g][:, :, 0:qg_rows(g)[1]], kcx, 512,
                           castbuf, lambda g: blocks, extra_bufs=rs + so)

        if want("wout"):
            resid_gemm("wo", mixT, KC, wout_bf, CB["mid"], x_loc, h1)

        if want("norm2"):
            norm_stage("n2", lambda t: h1[t * 128:(t + 1) * 128, :], NORM_Q_GROUPS, norm_xattn_g,
                       dstT=lambda g: hn2T[qg_of_tile(g[0])[0]][:, :, 0:len(g) * 128])
            norm_stage("nm", lambda t: mem[t * 128:(t + 1) * 128, :], [[0, 1]], mem_norm_g,
                       dstT=lambda g: memT[0][:, :, 0:256])

        if want("xattn"):
            with ExitStack() as st:
                xstB = [P.sb(st, "xa_stB%d" % i, [128, 4, 512], BF16) for i in range(2)]
                xrr = [0]

                def mkx(w_scr, mode, dst_fn):
                    state = {}

                    def epi(pb, i, g, blk):
                        T = 256 if g == "mem" else qg_rows(g)[1]
                        if i == 0:
                            state["stg"] = xstB[xrr[0] % 2]
                            xrr[0] += 1
                        stg = state["stg"]
                        if mode == "B":
                            evac(lambda: (stg[:, i, 0:T], pb[:, 0:T]), pb, [stg])
                        else:
                            evac(lambda: (stg[:, i, :], pb[:, 0:512]), pb, [stg])

                    def fin(g, blk):
                        stg = state["stg"]
                        T = 256 if g == "mem" else qg_rows(g)[1]
                        if mode == "B":
                            S.dma(act, lambda h: h.dma_start(out=dst_fn(g), in_=stg[:, :, 0:T]), None, stg)
                        else:
                            S.dma(act, lambda h: h.dma_start(out=dst_fn(g), in_=stg[:, 0:T // 128, :]), None, stg)
                    return (w_scr, 0, 512, mode, epi, fin)
                gemm_stage("xkv", [("mem", 256)], lambda g: memT[0][:, :, 0:256], KC, 512, CB["mid"],
                           lambda g: [mkx(wxk_bf, "B", lambda g: kxT[:, :, :]),
                                      mkx(wxv_bf, "A", lambda g: vx[:, :, :])], extra_bufs=[])
                gemm_stage("xq", [(g, qg_rows(g)[1]) for g in QGS], lambda g: hn2T[g][:, :, 0:qg_rows(g)[1]], KC, 512,
                           CB["mid"], lambda g: [mkx(wxq_bf, "B", lambda g: qxT[g][:, :, 0:qg_rows(g)[1]])],
                           extra_bufs=xstB)
            with ExitStack() as st:
                kx = P.sb(st, "xa_kx", [128, 4, 256], BF16)
                vxs = P.sb(st, "xa_vx", [128, 2, 512], BF16)
                S.dma(sp, lambda h: h.dma_start(out=kx[:], in_=kxT[:, :, :]), kx, None)
                S.dma(sp, lambda h: h.dma_start(out=vxs[:], in_=vx[:, :, :]), vxs, None)
                qx = [P.sb(st, "xa_qx%d" % i, [128, 4, 512], BF16) for i in range(2)]
                xPb = [P.sb(st, "xa_P%d" % i, [128, 256], BF16) for i in range(2)]
                xPTb = [P.sb(st, "xa_PT%d" % i, [128, 256], BF16) for i in range(2)]
                st4 = [P.sb(st, "xa_st%d" % i, [128, 4], F32) for i in range(2)]
                oxb = [P.sb(st, "xa_ox%d" % i, [128, 512], BF16) for i in range(2)]
                oxT_sb = [P.sb(st, "xa_oxT%d" % i, [128, 4, 512], BF16) for i in range(2)]
                xSps = [P.ps(st, "xa_Sps%d" % i, [128, 512], F32) for i in range(2)]
                xptp = [P.ps(st, "xa_ptp%d" % i, [128, 1024], BF16) for i in range(2)]
                xops = [P.ps(st, "xa_ops%d" % i, [128, 512], F32) for i in range(2)]
                xSC = DE ** -0.5
                xcnt = [0]

                def xtile(gi, g, m):
                    qb, oT = qx[gi % 2], oxT_sb[gi % 2]
                    ob = oxb[xcnt[0] % 2]
                    tsl = slice(m * 128, (m + 1) * 128)
                    for hh in range(4):
                        u = xcnt[0] * 4 + hh
                        sps, pb_, ptb_, stt, ptp_, ops_ = xSps[u % 2], xPb[u % 2], xPTb[u % 2], st4[u % 2], xptp[u % 2], xops[u % 2]

                        def one(hh=hh, sps=sps, pb_=pb_, ptb_=ptb_, stt=stt, ptp_=ptp_, ops_=ops_):
                            T_(lambda h: h.matmul(sps[:, 0:256], lhsT=qb[:, hh, tsl], rhs=kx[:, hh, :], start=True, stop=True),
                               [qb, kx], [sps])
                            V_(lambda h: h.tensor_reduce(out=stt[:, 0:1], in_=sps[:, 0:256], axis=AX.X, op=ALU.max,
                                                         negate=True), [sps], [stt])
                            V_(lambda h: h.tensor_scalar(out=stt[:, 1:2], in0=stt[:, 0:1], scalar1=xSC, scalar2=None,
                                                         op0=ALU.mult), [stt], [stt])
                            A_(lambda h: h.activation(out=pb_[:], in_=sps[:, 0:256], func=AF.Exp, bias=stt[:, 1:2], scale=xSC,
                                                      accum_out=stt[:, 2:3]), [sps, stt], [pb_, stt])
                            V_(lambda h: h.reciprocal(out=stt[:, 3:4], in_=stt[:, 2:3]), [stt], [stt])
                            for c in range(2):
                                T_(lambda h, c=c: h.transpose(out=ptp_[:, c * 128:(c + 1) * 128],
                                                              in_=pb_[:, c * 128:(c + 1) * 128], identity=ident_b[:]),
                                   [pb_, ident_b], [ptp_], sig=(c == 1))
                            evac(lambda: (ptb_[:], ptp_[:, 0:256]), ptp_, [ptb_])
                            for c in range(2):
                                T_(lambda h, c=c: h.matmul(ops_[:, 0:128], lhsT=ptb_[:, c * 128:(c + 1) * 128],
                                                           rhs=vxs[:, c, hh * 128:(hh + 1) * 128], start=(c == 0), stop=(c == 1)),
                                   [ptb_, vxs], [ops_], sig=(c == 1))
                            V_(lambda h: h.tensor_scalar(out=ob[:, hh * 128:(hh + 1) * 128], in0=ops_[:, 0:128],
                                                         scalar1=stt[:, 3:4], scalar2=None, op0=ALU.mult), [ops_, stt], [ob])
                        one()
                    pt2 = xptp[xcnt[0] % 2]
                    for k in range(4):
                        T_(lambda h, k=k: h.transpose(out=pt2[:, k * 128:(k + 1) * 128], in_=ob[:, k * 128:(k + 1) * 128],
                                                      identity=ident_b[:]), [ob, ident_b], [pt2], sig=(k == 3))
                    evac(lambda: (oT[:, :, tsl], pt2[:, 0:512].rearrange("p (k t) -> p k t", k=4)), pt2, [oT])
                    xcnt[0] += 1

                for gi, g in enumerate(QGS):
                    r0, T = qg_rows(g)
                    qb, oT = qx[gi % 2], oxT_sb[gi % 2]
                    S.dma(sp, lambda h, qb=qb, g=g, T=T: h.dma_start(out=qb[:, :, 0:T], in_=qxT[g][:, :, 0:T]), qb, None)
                    for m in range(T // 128):
                        xtile(gi, g, m)
                    S.dma(act, lambda h, oT=oT, g=g, T=T: h.dma_start(out=oxT[g][:, :, 0:T], in_=oT[:, :, 0:T]), None, oT)
                S.barrier(release=[kx, vxs] + qx + oxT_sb)
            resid_gemm("xo", oxT, 4, wxo_bf, CB["mid"], h1, h2)

        if want("norm3"):
            norm_stage("n3", lambda t: h2[t * 128:(t + 1) * 128, :], NORM_Q_GROUPS, norm_ffn_g,
                       dstT=lambda g: hn3T[qg_of_tile(g[0])[0]][:, :, 0:len(g) * 128])

        if want("ffn"):
            with ExitStack() as st:
                f_cwrow = P.sb(st, "f_cwrow", [FC, 4, 128], F32)
                f_cw = P.sb(st, "f_cw", [128, 4, FC], F32)
                f_flags = P.sb(st, "f_flags", [128, 2], F32)
                f_gprev = P.sb(st, "f_gprev", [128, FC, 2], F32)
                f_xT = P.sb(st, "f_xT", [128, KC, 512], BF16)
                f_xh = P.sb(st, "f_xh", [128, KC, 128], BF16)
                f_act = P.sb(st, "f_act", [128, 43, 512], BF16)
                f_wb = [P.sb(st, "f_wb%d" % i, [128, KC, 256], BF16) for i in range(3)]
                f_wd = [P.sb(st, "f_wd%d" % i, [128, 8, 512], BF16) for i in range(3)]
                f_gx = [P.sb(st, "f_gx%d" % i, [128, 514], F32) for i in range(2)]
                f_acc = [P.sb(st, "f_acc%d" % i, [128, 512], F32) for i in range(2)]
                f_sg = [P.sb(st, "f_sg%d" % i, [128, 2, 512], F32) for i in range(2)]
                f_res = [P.sb(st, "f_res%d" % i, [128, 512], F32) for i in range(4)]
                f_so = [P.sb(st, "f_so%d" % i, [128, 512], F32) for i in range(4)]
                f_ps = [P.ps(st, "f_ps%d" % i, [128, 512], F32) for i in range(3)]
                f_psh = P.ps(st, "f_psh", [128, 512], F32)
                f_psd = [P.ps(st, "f_psd%d" % i, [128, 512], F32) for i in range(4)]
                S.dma(sp, lambda h: h.dma_start(out=f_cwrow[:, 0:3, :], in_=ffn_conv_w.rearrange("i (j p) -> j i p", p=128)),
                      f_cwrow, None)
                S.dma(sp, lambda h: h.dma_start(out=f_cwrow[:, 3:4, :], in_=ffn_conv_b.rearrange("o (j p) -> j o p", p=128)),
                      f_cwrow, None)
                S.dma(sp, lambda h: h.dma_start(out=f_flags[:], in_=c_flags), f_flags, None)
                for i in range(4):
                    T_(lambda h, i=i: h.transpose(out=f_ps[0][:, i * 128:i * 128 + FC], in_=f_cwrow[:, i, :],
                                                  identity=ident_f[0:FC, 0:FC]), [f_cwrow, ident_f], [f_ps[0]], sig=(i == 3))
                V_(lambda h: h.tensor_copy(out=f_cw[:], in_=f_ps[0][:].rearrange("p (i j) -> p i j", i=4)[:, :, 0:FC]),
                   [f_ps[0]], [f_cw])
                G_(lambda h: h.memset(f_gprev[:], 0.0), [], [f_gprev])
                fc = {"w": 0, "p": 0, "c": 0, "d": 0, "e": 0}
                ffn_tgs = P.opts.get("ffn_tgs", [1, 2, 3, 4])
                h3b = {tg: Buf("h3b%d" % tg) for tg in ffn_tgs}

                def load_wblk(w_scr, c0, ncols):
                    wbb = f_wb[fc["w"] % 3]
                    fc["w"] += 1
                    src = w_scr[:, c0:c0 + ncols].rearrange("(kc p) n -> p kc n", p=128)
                    S.dma(sp, lambda h: h.dma_start(out=wbb[:, :, 0:ncols], in_=src), wbb, CB["gu"])
                    return wbb

                def mm_chunk(wbb, c):
                    pb = f_ps[fc["p"] % 3]
                    fc["p"] += 1
                    for kc in range(KC):
                        T_(lambda h, kc=kc: h.matmul(pb[:, :], lhsT=wbb[:, kc, c * 128:(c + 1) * 128], rhs=f_xT[:, kc, :],
                                                     start=(kc == 0), stop=(kc == KC - 1)), [wbb, f_xT], [pb], sig=(kc == KC - 1))
                    return pb

                def gate_chunk(tg, wbb, c, j, sgb):
                    pb = mm_chunk(wbb, c)
                    if tg == 1:
                        for kc in range(KC):
                            T_(lambda h, kc=kc: h.matmul(f_psh[:, 0:2], lhsT=wbb[:, kc, c * 128:(c + 1) * 128], rhs=f_xh[:, kc, 126:128],
                                                         start=(kc == 0), stop=(kc == KC - 1)), [wbb, f_xh], [f_psh],
                               sig=(kc == KC - 1))
                        V_(lambda h: h.tensor_scalar(out=f_gprev[:, j, :], in0=f_psh[:, 0:2], scalar1=f_flags[:, 1:2],
                                                     scalar2=None, op0=ALU.mult), [f_psh, f_flags], [f_gprev])
                    gx, acc = f_gx[fc["c"] % 2], f_acc[fc["c"] % 2]
                    fc["c"] += 1
                    A_(lambda h: h.copy(out=gx[:, 2:514], in_=pb[:, :]), [pb], [gx])
                    G_(lambda h: h.tensor_copy(out=gx[:, 0:2], in_=f_gprev[:, j, :]), [f_gprev], [gx])
                    G_(lambda h: h.tensor_copy(out=f_gprev[:, j, :], in_=gx[:, 512:514]), [gx], [f_gprev])
                    V_(lambda h: h.tensor_scalar(out=acc[:], in0=gx[:, 0:512], scalar1=f_cw[:, 0, j:j + 1],
                                                 scalar2=f_cw[:, 3, j:j + 1], op0=ALU.mult, op1=ALU.add), [gx, f_cw], [acc])
                    V_(lambda h: h.scalar_tensor_tensor(out=acc[:], in0=gx[:, 1:513], scalar=f_cw[:, 1, j:j + 1], in1=acc[:],
                                                        op0=ALU.mult, op1=ALU.add), [gx, f_cw, acc], [acc])
                    V_(lambda h: h.scalar_tensor_tensor(out=acc[:], in0=gx[:, 2:514], scalar=f_cw[:, 2, j:j + 1], in1=acc[:],
                                                        op0=ALU.mult, op1=ALU.add), [gx, f_cw, acc], [acc])
                    A_(lambda h: h.activation(out=sgb[:, c, :], in_=acc[:], func=AF.Silu), [acc], [sgb])

                def up_chunk(wbb, c, jl, sgb):
                    pb = mm_chunk(wbb, c)
                    V_(lambda h: h.tensor_tensor(out=f_act[:, jl, :], in0=pb[:, :], in1=sgb[:, c, :], op=ALU.mult),
                       [pb, sgb], [f_act])

                def down_phase(tg, half):
                    r0, T = qg_rows(tg)
                    j0 = half * 43
                    src_h = h2 if half == 0 else h3
                    for nb in range(8):
                        cs = slice(nb * 512, (nb + 1) * 512)
                        psd = f_psd if nb % 2 == 0 else [f_ps[0], f_ps[1], f_ps[2], f_psh]
                        for m in range(4):
                            rr_ = slice(r0 + m * 128, r0 + (m + 1) * 128)
                            S.dma(sp, lambda h, m=m, rr_=rr_, cs=cs: h.dma_start(out=f_res[m][:], in_=src_h[rr_, cs]), f_res[m],
                                  h3b[tg] if half == 1 else None)
                        for jg in range(0, 43, 8):
                            nj = min(8, 43 - jg)
                            wdp = f_wd[fc["d"] % 3]
                            fc["d"] += 1
                            rows = slice((j0 + jg) * 128, (j0 + jg + nj) * 128)
                            srcw = wd_bf[rows, cs].rearrange("(jj p) c -> p jj c", p=128)
                            S.dma(sp, lambda h, wdp=wdp, srcw=srcw, nj=nj: h.dma_start(out=wdp[:, 0:nj, :], in_=srcw),
                                  wdp, CB["wd"])
                            for jj in range(nj):
                                jl = jg + jj
                                for m in range(4):
                                    T_(lambda h, wdp=wdp, jj=jj, jl=jl, m=m, nj=nj, psd=psd: h.matmul(
                                        psd[m][:, :], lhsT=f_act[:, jl, m * 128:(m + 1) * 128], rhs=wdp[:, jj, :],
                                        start=(jl == 0), stop=(jl == 42)), [f_act, wdp], [psd[m]], sig=(jl == 42 or (jj == nj - 1 and m == 3)))
                        for m in range(4):
                            rb, sob = f_res[m], f_so[m]
                            rr_ = slice(r0 + m * 128, r0 + (m + 1) * 128)
                            V_(lambda h, rb=rb, sob=sob, m=m, psd=psd: h.tensor_tensor(out=sob[:], in0=psd[m][:, :], in1=rb[:], op=ALU.add),
                               [psd[m], rb], [sob])
                            S.dma(act, lambda h, sob=sob, rr_=rr_, cs=cs: h.dma_start(out=h3[rr_, cs], in_=sob[:]), h3b[tg], sob)

                for tg in ffn_tgs:
                    S.dma(sp, lambda h, tg=tg: h.dma_start(out=f_xT[:], in_=hn3T[tg]), f_xT, None)
                    if tg == 1:
                        S.dma(sp, lambda h: h.dma_start(out=f_xh[:], in_=hn3T[0][:, :, 0:128]), f_xh, None)
                    for half in range(2):
                        j0 = half * 43
                        for b0 in range(0, 43, 2):
                            nch = min(2, 43 - b0)
                            c0 = (j0 + b0) * 128
                            sgb = f_sg[(b0 // 2) % 2]
                            wbb = load_wblk(wg_bf, c0, nch * 128)
                            for c in range(nch):
                                gate_chunk(tg, wbb, c, j0 + b0 + c, sgb)
                            wbb = load_wblk(wu_bf, c0, nch * 128)
                            for c in range(nch):
                                up_chunk(wbb, c, b0 + c, sgb)
                        down_phase(tg, half)
                S.barrier(release=[f_cwrow, f_flags, f_xT, f_xh] + f_wb + f_wd + f_res + f_so + list(h3b.values()))

        if want("final"):
            norm_stage("nf", lambda t: h3[t * 128:(t + 1) * 128, :], [[t] for t in P.opts.get("final_tiles", list(range(16, 32)))],
                       final_norm_g, dst_tok=lambda t: out[(t - 16) * 128:(t - 15) * 128, :])

        S.barrier()
        with nc.Block() as block:
            S.emit(block)
    return nc, P, S


def host_consts():
    c = {}
    c["c_ident"] = np.eye(128, dtype=np.float32)
    j = np.arange(128)[:, None]
    cc = np.arange(128)[None, :]
    same = (j // 64) == (cc // 64)
    tri = (same & (j <= cc)).astype(np.float32)
    m2 = (same & (j > cc)).astype(np.float32)
    mid = (same & ((j % 64) <= 32)).astype(np.float32)
    m3 = tri - mid
    c["c_gmats"] = np.ascontiguousarray(np.stack([m2, tri, m3], axis=1)).astype(np.float32)
    c["c_attmask"] = tri.copy()
    kk = np.arange(256)[:, None]
    kj = np.arange(256)[None, :]
    J = (kk == 255 - kj).astype(np.float32)
    c["c_antiI"] = np.ascontiguousarray(J.reshape(2, 128, 256).transpose(1, 0, 2))
    onehot = np.zeros((32, 3, 384), np.float32)
    bandneg = np.zeros((DH, 3, 384), np.float32)
    for ci, d in enumerate((1, 4, 16)):
        for i in range(384):
            rel = i - 127
            if 0 <= rel <= 128:
                dist = rel * d
                if dist < 16:
                    b = dist
                else:
                    b = 16 + int(np.float32(np.log(np.float32(max(dist, 1)) / np.float32(16)) /
                                            np.float32(np.log(2048 / 16)) * np.float32(16)))
                    b = min(b, 31)
                onehot[b, ci, i] = 1.0
            else:
                bandneg[:, ci, i] = NEG
    c["c_onehot"] = onehot
    c["c_bandneg"] = bandneg
    return c


def make_in_maps(inputs):
    consts = host_consts()
    x = np.asarray(inputs["x"], dtype=np.float32)
    maps = []
    for c in range(8):
        b, half = c // 2, c % 2
        m = {}
        if half == 0:
            xl = np.zeros((SEQ, D), np.float32)
            xl[2048:] = x[b, :2048]
        else:
            xl = np.ascontiguousarray(x[b])
        m["x_loc"] = xl
        m["mem"] = np.ascontiguousarray(inputs["mem"][b], dtype=np.float32)
        m["rel_bias"] = np.ascontiguousarray(inputs["rel_bias"], dtype=np.float32)
        for k in ("norm_mix_g", "gla_b_gate", "gla_norm_g", "norm_xattn_g", "mem_norm_g", "norm_ffn_g", "ffn_conv_b"):
            m[k] = np.ascontiguousarray(np.asarray(inputs[k], dtype=np.float32).reshape(1, -1))
        m["final_norm_g"] = np.ascontiguousarray(np.asarray(inputs["final_norm_g"], dtype=np.float32).reshape(1, -1))
        for k in ("w_in", "gla_w_gate2", "w_out", "w_xq", "w_xk", "w_xv", "w_xo", "w_ffn_gate", "w_ffn_up",
                  "ffn_conv_w", "w_ffn_down"):
            m[k] = np.ascontiguousarray(np.asarray(inputs[k], dtype=np.float32)[0])
        m.update(consts)
        fl = np.zeros((128, 2), np.float32)
        fl[:, 0] = NEG if half == 0 else 0.0
        fl[:, 1] = 0.0 if half == 0 else 1.0
        m["c_flags"] = fl
        maps.append(m)
    return maps


def kernel(**inputs):
    nc, P, S = build()
    maps = make_in_maps(inputs)
    res = run_bass_kernel_spmd(nc, maps, core_ids=list(range(8)))
    outp = np.empty((4, SEQ, D), np.float32)
    for c in range(8):
        b, half = c // 2, c % 2
        outp[b, half * 2048:(half + 1) * 2048] = res.results[c]["out"]
    return outp
```

```python
import numpy as np
from contextlib import ExitStack
import concourse.bass as bass
import concourse.mybir as mybir
from concourse.bass_utils import run_bass_kernel_spmd

F32 = mybir.dt.float32
BF16 = mybir.dt.bfloat16
AF = mybir.ActivationFunctionType
ALU = mybir.AluOpType
AX = mybir.AxisListType

D = 4096
SEQ = 4096
LT = 32
QT0 = 15
NQT = LT - QT0
KC = D // 128
GH, GDK, GDV = 4, 256, 512
DH, DE = 16, 128
DFF = 11008
FC = DFF // 128
XW = 512
MEM = 256
NCOL = 12304
C_GQ, C_GK, C_GV, C_GLR, C_GR, C_DQ, C_DK, C_DV = 0, 1024, 2048, 4096, 4112, 6160, 8208, 10256
EPS = 1e-6
NEG = -30000.0


class Buf:
    def __init__(self, name, t=None):
        self.name = name
        self.t = t
        self.lw = None
        self.rd = []
        self.sem = None
        self.cnt = 0

    def __getitem__(self, k):
        return self.t[k]


class Eng:
    def __init__(self, name, same_raw):
        self.name = name
        self.ops = []
        self.cnt = 0
        self.waited = {}
        self.sem = None
        self.same_raw = same_raw


class Sched:
    def __init__(self, nc, es):
        self.nc = nc
        self.es = es
        self.pe = Eng("pe", False)
        self.act = Eng("act", True)
        self.dve = Eng("dve", True)
        self.pool = Eng("pool", True)
        self.sp = Eng("sp", False)
        self.engs = [self.pe, self.act, self.dve, self.pool, self.sp]
        for e in self.engs[:4]:
            e.sem = es.enter_context(nc.semaphore("sem_" + e.name))
        self.dsems = []
        self.free_dsems = []
        self.nsem = 0

    def dma_sem(self, barrier=True):
        if barrier and self.free_dsems:
            return self.free_dsems.pop()
        s = self.es.enter_context(self.nc.semaphore("dsem%d" % self.nsem))
        self.nsem += 1
        rec = [s, 0]
        if barrier:
            self.dsems.append(rec)
        return rec

    def _waits(self, eng, deps):
        ws = []
        for d in deps:
            if d is None:
                continue
            sem, val = d
            if sem is eng.sem and not eng.same_raw:
                continue
            k = id(sem)
            if eng.waited.get(k, 0) < val:
                eng.waited[k] = val
                ws.append((sem, val))
        return ws

    def op(self, eng, fn, reads=(), writes=(), signal=True):
        deps = []
        for b in reads:
            deps.append(b.lw)
        for b in writes:
            if b.lw is not None and b.lw[0] is not eng.sem:
                deps.append(b.lw)
            for r in b.rd:
                if r[0] is not eng.sem:
                    deps.append(r)
        ws = self._waits(eng, deps)
        val = eng.cnt + 1
        if signal:
            eng.cnt = val
            eng.ops.append((ws, fn, (eng.sem, 1)))
        else:
            eng.ops.append((ws, fn, None))
        for b in reads:
            b.rd.append((eng.sem, val))
        for b in writes:
            b.lw = (eng.sem, val)
            b.rd = []

    def dma(self, eng, fn, out, in_):
        deps = []
        if in_ is not None:
            deps.append(in_.lw)
        if out is not None:
            deps.append(out.lw)
            deps.extend(out.rd)
        ws = self._waits(eng, deps)
        tgt = out if out is not None else in_
        if tgt.sem is None:
            tgt.sem = self.dma_sem()
        tgt.sem[1] += 16
        key = (tgt.sem[0], tgt.sem[1])
        eng.ops.append((ws, fn, (tgt.sem[0], 16)))
        if out is not None:
            out.lw = key
            out.rd = []
        if in_ is not None:
            in_.rd.append(key)

    def barrier(self, release=()):
        for e in self.engs:
            deps = []
            for o in self.engs[:4]:
                if o is not e and o.cnt > 0:
                    deps.append((o.sem, o.cnt))
            for rec in self.dsems:
                if rec[1] > 0:
                    deps.append((rec[0], rec[1]))
            ws = self._waits(e, deps)
            if ws:
                e.ops.append((ws, None, None))
        for b in release:
            if b.sem is not None:
                self.free_dsems.append(b.sem)
                b.sem = None

    def emit(self, block):
        def run(eng, h):
            for ws, fn, inc in eng.ops:
                for sem, val in ws:
                    h.wait_ge(sem, val)
                if fn is not None:
                    ins = fn(h)
                    if inc is not None:
                        ins.then_inc(inc[0], inc[1])

        @block.tensor
        def _(h):
            run(self.pe, h)

        @block.scalar
        def _(h):
            run(self.act, h)

        @block.vector
        def _(h):
            run(self.dve, h)

        @block.gpsimd
        def _(h):
            run(self.pool, h)

        @block.sync
        def _(h):
            run(self.sp, h)


class Prog:
    def __init__(self, debug_outs=(), stages=None):
        self.nc = bass.Bass("TRN2", target_bir_lowering=False)
        self.es = ExitStack()
        self.debug_outs = set(debug_outs)
        self.stages = stages
        self.ins = {}
        self.scr = {}
        self.npsum = 0

    def inp(self, name, shape, dt=F32):
        t = self.nc.dram_tensor(name, list(shape), dt, kind="ExternalInput")
        self.ins[name] = t
        return t.ap()

    def scratch(self, name, shape, dt):
        kind = "ExternalOutput" if name in self.debug_outs else "Internal"
        t = self.nc.dram_tensor(name, list(shape), dt, kind=kind)
        self.scr[name] = t
        return t.ap()

    def sb(self, st, name, shape, dt):
        t = st.enter_context(self.nc.sbuf_tensor(name, list(shape), dt))
        return Buf(name, t)

    def ps(self, st, name, shape, dt):
        t = st.enter_context(self.nc.psum_tensor(name, list(shape), dt))
        return Buf(name, t)


def bcast_row(ap_row, nparts):
    return bass.AP(ap_row.tensor, ap_row.offset, [[0, nparts]] + [list(x) for x in ap_row.ap[1:]])


def build(debug_outs=(), stages=None, opts=None):
    P = Prog(debug_outs, stages)
    P.opts = opts or {}
    nc = P.nc
    es = P.es
    S = Sched(nc, es)
    pe, act, dve, pool, sp = S.pe, S.act, S.dve, S.pool, S.sp

    def want(name):
        return stages is None or name in stages

    x_loc = P.inp("x_loc", [SEQ, D])
    mem = P.inp("mem", [MEM, D])
    rel_bias = P.inp("rel_bias", [32, DH])
    norm_mix_g = P.inp("norm_mix_g", [1, D])
    w_in = P.inp("w_in", [D, NCOL])
    gla_w_gate2 = P.inp("gla_w_gate2", [16, 1024])
    gla_b_gate = P.inp("gla_b_gate", [1, 1024])
    gla_norm_g = P.inp("gla_norm_g", [1, GDV])
    w_out = P.inp("w_out", [D, D])
    norm_xattn_g = P.inp("norm_xattn_g", [1, D])
    mem_norm_g = P.inp("mem_norm_g", [1, D])
    w_xq = P.inp("w_xq", [D, XW])
    w_xk = P.inp("w_xk", [D, XW])
    w_xv = P.inp("w_xv", [D, XW])
    w_xo = P.inp("w_xo", [XW, D])
    norm_ffn_g = P.inp("norm_ffn_g", [1, D])
    w_ffn_gate = P.inp("w_ffn_gate", [D, DFF])
    w_ffn_up = P.inp("w_ffn_up", [D, DFF])
    ffn_conv_w = P.inp("ffn_conv_w", [3, DFF])
    ffn_conv_b = P.inp("ffn_conv_b", [1, DFF])
    w_ffn_down = P.inp("w_ffn_down", [DFF, D])
    final_norm_g = P.inp("final_norm_g", [1, D])
    c_ident = P.inp("c_ident", [128, 128])
    c_gmats = P.inp("c_gmats", [128, 3, 128])
    c_attmask = P.inp("c_attmask", [128, 128])
    c_antiI = P.inp("c_antiI", [128, 2, 256])
    c_onehot = P.inp("c_onehot", [32, 3, 384])
    c_bandneg = P.inp("c_bandneg", [DH, 3, 384])
    c_flags = P.inp("c_flags", [128, 2])
    out = nc.dram_tensor("out", [2048, D], F32, kind="ExternalOutput").ap()

    win_bf = P.scratch("win_bf", [D, NCOL], BF16)
    wout_bf = P.scratch("wout_bf", [D, D], BF16)
    wxq_bf = P.scratch("wxq_bf", [D, XW], BF16)
    wxk_bf = P.scratch("wxk_bf", [D, XW], BF16)
    wxv_bf = P.scratch("wxv_bf", [D, XW], BF16)
    wxo_bf = P.scratch("wxo_bf", [XW, D], BF16)
    wg_bf = P.scratch("wg_bf", [D, DFF], BF16)
    wu_bf = P.scratch("wu_bf", [D, DFF], BF16)
    wd_bf = P.scratch("wd_bf", [DFF, D], BF16)
    hnT = P.scratch("hnT", [8, 128, KC, 512], BF16)
    gqT = P.scratch("gqT", [1024, SEQ], BF16)
    gkT = P.scratch("gkT", [1024, SEQ], BF16)
    gv = P.scratch("gv", [SEQ, 2048], BF16)
    glrT = P.scratch("glrT", [16, SEQ], F32)
    gr = P.scratch("gr", [SEQ, 2048], F32)
    dq = P.scratch("dq", [SEQ, 2048], BF16)
    dk = P.scratch("dk", [SEQ, 2048], BF16)
    dv = P.scratch("dv", [SEQ, 2048], BF16)
    fext_d = P.scratch("fext_d", [3, DH, 384], F32)
    oc = [P.scratch("oc%d" % i, [SEQ, 2048], F32) for i in range(3)]
    msc = [P.scratch("msc%d" % i, [SEQ, 2, DH], F32) for i in range(3)]
    h1 = P.scratch("h1", [SEQ, D], F32)
    h2 = P.scratch("h2", [SEQ, D], F32)
    h3 = P.scratch("h3", [SEQ, D], F32)
    hn2T = P.scratch("hn2T", [5, 128, KC, 512], BF16)
    hn3T = P.scratch("hn3T", [5, 128, KC, 512], BF16)
    memT = P.scratch("memT", [1, 128, KC, 512], BF16)
    kxT = P.scratch("kxT", [128, 4, 256], BF16)
    vx = P.scratch("vx", [128, 2, 512], BF16)
    qxT = P.scratch("qxT", [5, 128, 4, 512], BF16)
    oxT = P.scratch("oxT", [5, 128, 4, 512], BF16)
    hnT32 = P.scratch("hnT32", [128, KC, 128], F32)
    qk32T = P.scratch("qk32T", [128, 16, 128], F32)
    mixT = P.scratch("mixT", [5, 128, KC, 512], BF16)

    with es:
        ident_f = P.sb(es, "ident_f", [128, 128], F32)
        ident_b = P.sb(es, "ident_b", [128, 128], BF16)
        S.dma(sp, lambda h: h.dma_start(out=ident_f[:], in_=c_ident), ident_f, None)
        S.op(dve, lambda h: h.tensor_copy(out=ident_b[:], in_=ident_f[:]), [ident_f], [ident_b])

        def cast_w(cb, dst, src, rows, rstep):
            for r0 in range(0, rows, rstep):
                r1 = min(rows, r0 + rstep)
                S.dma(pool, lambda h, r0=r0, r1=r1: h.dma_start(out=dst[r0:r1, :], in_=src[r0:r1, :],
                                                                max_dma_last_dim=8192), cb, None)
        CB = {}
        for nm in ("win", "mid", "gu", "wd"):
            CB[nm] = Buf("cast_" + nm)
            CB[nm].sem = S.dma_sem(barrier=False)
        if want("cast"):
            cast_w(CB["win"], win_bf, w_in, D, 512)
            cast_w(CB["mid"], wout_bf, w_out, D, 1024)
            cast_w(CB["mid"], wxq_bf, w_xq, D, 4096)
            cast_w(CB["mid"], wxk_bf, w_xk, D, 4096)
            cast_w(CB["mid"], wxv_bf, w_xv, D, 4096)
            cast_w(CB["mid"], wxo_bf, w_xo, XW, 512)

        def norm_stage(tag, src_rows, tiles, gain_row, dstT=None, dst_tok=None, dst_tok_dt=F32, f32T=None):
            with ExitStack() as st:
                gain = P.sb(st, tag + "_gain", [128, D], F32)
                S.dma(sp, lambda h: h.dma_start(out=gain[:], in_=bcast_row(gain_row, 128)), gain, None)
                xt = [P.sb(st, tag + "_x%d" % i, [128, D], F32) for i in range(2)]
                junk = P.sb(st, tag + "_junk", [128, D], BF16)
                hn = [P.sb(st, tag + "_hn%d" % i, [128, D], BF16 if dstT is not None else dst_tok_dt) for i in range(2)]
                ss = [P.sb(st, tag + "_ss%d" % i, [128, 4], F32) for i in range(2)]
                mhalf = P.sb(st, tag + "_mh", [128, 1], F32)
                S.op(dve, lambda h: h.memset(mhalf[:], -0.5), [], [mhalf])
                ng = 4
                grp = [P.sb(st, tag + "_grp%d" % i, [128, KC, 512], BF16) for i in range(2)] if dstT is not None else None
                pst = [P.ps(st, tag + "_pt%d" % i, [128, 1024], BF16) for i in range(2)] if dstT is not None else None
                allb = [gain, junk, mhalf] + xt + hn + ss + (grp or [])
                if f32T is not None:
                    hn32 = P.sb(st, tag + "_hn32", [128, D], F32)
                    hT32 = P.sb(st, tag + "_hT32", [128, KC, 128], F32)
                    ps32 = P.ps(st, tag + "_ps32", [128, 512], F32)
                    allb += [hn32, hT32]
                groups = tiles if (len(tiles) > 0 and isinstance(tiles[0], list)) else \
                    [tiles[i:i + ng] for i in range(0, len(tiles), ng)]
                it = 0
                for gi, g in enumerate(groups):
                    gb = grp[gi % 2] if grp else None
                    for ti, t in enumerate(g):
                        xb, hb, sb_ = xt[it % 2], hn[it % 2], ss[it % 2]
                        S.dma(sp, lambda h, xb=xb, t=t: h.dma_start(out=xb[:], in_=src_rows(t)), xb, None)
                        S.op(dve, lambda h, xb=xb, sb_=sb_: h.scalar_tensor_tensor(
                            out=junk[:], in0=xb[:], scalar=1.0, in1=xb[:], op0=ALU.mult, op1=ALU.mult,
                            accum_out=sb_[:, 0:1]), [xb], [junk, sb_])
                        S.op(dve, lambda h, sb_=sb_: h.tensor_scalar(out=sb_[:, 1:2], in0=sb_[:, 0:1], scalar1=1.0 / D,
                                                                     scalar2=EPS, op0=ALU.mult, op1=ALU.add), [sb_], [sb_])
                        S.op(act, lambda h, sb_=sb_: h.activation(out=sb_[:, 2:3], in_=sb_[:, 1:2], func=AF.Ln), [sb_], [sb_])
                        S.op(act, lambda h, sb_=sb_: h.activation(out=sb_[:, 2:3], in_=sb_[:, 2:3], func=AF.Exp, scale=-0.5),
                             [sb_], [sb_])
                        S.op(dve, lambda h, xb=xb, hb=hb, sb_=sb_: h.scalar_tensor_tensor(
                            out=hb[:], in0=xb[:], scalar=sb_[:, 2:3], in1=gain[:], op0=ALU.mult, op1=ALU.mult),
                            [xb, sb_, gain], [hb])
                        if f32T is not None and t == f32T[0]:
                            S.op(dve, lambda h, xb=xb, sb_=sb_: h.scalar_tensor_tensor(
                                out=hn32[:], in0=xb[:], scalar=sb_[:, 2:3], in1=gain[:], op0=ALU.mult, op1=ALU.mult),
                                [xb, sb_, gain], [hn32])
                            for kc4 in range(0, KC, 4):
                                for k in range(4):
                                    kc = kc4 + k
                                    S.op(pe, lambda h, kc=kc, k=k: h.transpose(
                                        out=ps32[:, k * 128:(k + 1) * 128], in_=hn32[:, kc * 128:(kc + 1) * 128],
                                        identity=ident_f[:]), [hn32, ident_f], [ps32], signal=(k == 3))
                                S.op(act, lambda h, kc4=kc4: h.copy(out=hT32[:, kc4:kc4 + 4, :],
                                                                   in_=ps32[:].rearrange("p (k t) -> p k t", k=4)),
                                     [ps32], [hT32])
                            S.dma(act, lambda h: h.dma_start(out=f32T[1], in_=hT32[:]), None, hT32)
                        if dst_tok is not None:
                            S.dma(act, lambda h, hb=hb, t=t: h.dma_start(out=dst_tok(t), in_=hb[:]), None, hb)
                        if dstT is not None:
                            for kc8 in range(0, KC, 8):
                                pb = pst[(kc8 // 8) % 2]
                                for k in range(8):
                                    kc = kc8 + k
                                    S.op(pe, lambda h, pb=pb, hb=hb, kc=kc, k=k: h.transpose(
                                        out=pb[:, k * 128:(k + 1) * 128], in_=hb[:, kc * 128:(kc + 1) * 128],
                                        identity=ident_b[:]), [hb, ident_b], [pb], signal=(k == 7))
                                eng = act if (kc8 // 8) % 2 == 0 else dve
                                if eng is act:
                                    S.op(act, lambda h, pb=pb, gb=gb, kc8=kc8, ti=ti: h.copy(
                                        out=gb[:, kc8:kc8 + 8, ti * 128:(ti + 1) * 128],
                                        in_=pb[:].rearrange("p (k t) -> p k t", k=8)), [pb], [gb])
                                else:
                                    S.op(dve, lambda h, pb=pb, gb=gb, kc8=kc8, ti=ti: h.tensor_copy(
                                        out=gb[:, kc8:kc8 + 8, ti * 128:(ti + 1) * 128],
                                        in_=pb[:].rearrange("p (k t) -> p k t", k=8)), [pb], [gb])
                        it += 1
                    if dstT is not None:
                        nt = len(g) * 128
                        S.dma(act, lambda h, gb=gb, g=g, nt=nt: h.dma_start(out=dstT(g), in_=gb[:, :, 0:nt]), None, gb)
                S.barrier(release=allb)

        if want("norm1"):
            norm_stage("n1", lambda t: x_loc[t * 128:(t + 1) * 128, :], list(range(LT)), norm_mix_g,
                       dstT=lambda g: hnT[g[0] // 4], f32T=(16, hnT32))

        def gemm_stage(tag, groups, xT_of, kcx, T, castbuf, blocks_of, extra_bufs=()):
            with ExitStack() as st:
                xT = [P.sb(st, tag + "_xT%d" % i, [128, kcx, T], BF16) for i in range(2)]
                wb = [P.sb(st, tag + "_wb%d" % i, [128, kcx, 512], BF16) for i in range(2)]
                psb = [P.ps(st, tag + "_ps%d" % i, [128, 512], F32) for i in range(6)]
                pi = 0
                wi = 0
                Tmax = T
                for gi, g in enumerate(groups):
                    if isinstance(g, tuple):
                        g, T = g
                    else:
                        T = Tmax
                    xb = xT[gi % 2]
                    S.dma(sp, lambda h, xb=xb, g=g, T=T: h.dma_start(out=xb[:, :, 0:T], in_=xT_of(g)), xb, None)
                    for blk in blocks_of(g):
                        w_dram, c0, ncols, mode, epi, fin = blk
                        wbb = wb[wi % 2]
                        wi += 1
                        src = w_dram[:, c0:c0 + ncols].rearrange("(kc p) n -> p kc n", p=128)
                        S.dma(sp, lambda h, wbb=wbb, src=src, ncols=ncols: h.dma_start(
                            out=wbb[:, :, 0:ncols], in_=src), wbb, castbuf)
                        if mode == "B":
                            nj = (ncols + 127) // 128
                            for j in range(nj):
                                mcols = min(128, ncols - j * 128)
                                pb = psb[pi % 6]
                                pi += 1
                                for kc in range(kcx):
                                    S.op(pe, lambda h, pb=pb, wbb=wbb, xb=xb, kc=kc, j=j, mcols=mcols, T=T: h.matmul(
                                        pb[0:mcols, 0:T], lhsT=wbb[:, kc, j * 128:j * 128 + mcols], rhs=xb[:, kc, 0:T],
                                        start=(kc == 0), stop=(kc == kcx - 1)), [wbb, xb], [pb], signal=(kc == kcx - 1))
                                epi(pb, j, g, blk)
                        else:
                            for m in range(T // 128):
                                pb = psb[pi % 6]
                                pi += 1
                                for kc in range(kcx):
                                    S.op(pe, lambda h, pb=pb, wbb=wbb, xb=xb, kc=kc, m=m, ncols=ncols: h.matmul(
                                        pb[:, 0:ncols], lhsT=xb[:, kc, m * 128:(m + 1) * 128], rhs=wbb[:, kc, 0:ncols],
                                        start=(kc == 0), stop=(kc == kcx - 1)), [wbb, xb], [pb], signal=(kc == kcx - 1))
                                epi(pb, m, g, blk)
                        if fin is not None:
                            fin(g, blk)
                S.barrier(release=xT + wb + list(extra_bufs))

        evac_rr = [0]

        def evac(out_ap_fn, pb, writes):
            evac_rr[0] += 1
            if evac_rr[0] % 2 == 0:
                S.op(act, lambda h: h.copy(out=out_ap_fn()[0], in_=out_ap_fn()[1]), [pb], writes)
            else:
                S.op(dve, lambda h: h.tensor_copy(out=out_ap_fn()[0], in_=out_ap_fn()[1]), [pb], writes)

        if want("proj"):
            with ExitStack() as st:
                stB = [P.sb(st, "pj_stB%d" % i, [128, 4, 512], BF16) for i in range(2)]
                stF = [P.sb(st, "pj_stF%d" % i, [128, 4, 512], F32) for i in range(2)]
                stL = P.sb(st, "pj_stL", [16, 512], F32)
                rr = {"B": 0, "F": 0}

                def mk_block(c0, ncols, mode, dst, dcol0, dt):
                    state = {}

                    def epi(pb, i, g, blk):
                        if i == 0:
                            key = "F" if dt is F32 else "B"
                            pool_ = stF if dt is F32 else stB
                            state["stg"] = pool_[rr[key] % 2]
                            rr[key] += 1
                        stg = state["stg"]
                        if mode == "B":
                            evac(lambda: (stg[:, i, :], pb[:, 0:512]), pb, [stg])
                        else:
                            evac(lambda: (stg[:, i, 0:ncols], pb[:, 0:ncols]), pb, [stg])

                    def fin(g, blk):
                        stg = state["stg"]
                        if mode == "B":
                            d = dst[dcol0:dcol0 + ncols, g * 512:(g + 1) * 512].rearrange("(j p) t -> p j t", p=128)
                            S.dma(act, lambda h: h.dma_start(out=d, in_=stg[:, 0:ncols // 128, :]), None, stg)
                        else:
                            d = dst[g * 512:(g + 1) * 512, dcol0:dcol0 + ncols].rearrange("(m p) c -> p m c", p=128)
                            S.dma(act, lambda h: h.dma_start(out=d, in_=stg[:, :, 0:ncols]), None, stg)
                    return (win_bf, c0, ncols, mode, epi, fin)

                def glr_block():
                    def epi(pb, i, g, blk):
                        S.op(act, lambda h: h.copy(out=stL[:, :], in_=pb[0:16, 0:512]), [pb], [stL])

                    def fin(g, blk):
                        S.dma(act, lambda h: h.dma_start(out=glrT[:, g * 512:(g + 1) * 512], in_=stL[:, :]), None, stL)
                    return (win_bf, C_GLR, 16, "B", epi, fin)

                kv_blocks = []
                q_blocks = []
                for b in range(2):
                    kv_blocks.append(mk_block(C_GK + b * 512, 512, "B", gkT, b * 512, BF16))
                    q_blocks.append(mk_block(C_GQ + b * 512, 512, "B", gqT, b * 512, BF16))
                kv_blocks.append(glr_block())
                for b in range(4):
                    kv_blocks.append(mk_block(C_GV + b * 512, 512, "A", gv, b * 512, BF16))
                    kv_blocks.append(mk_block(C_DK + b * 512, 512, "A", dk, b * 512, BF16))
                    kv_blocks.append(mk_block(C_DV + b * 512, 512, "A", dv, b * 512, BF16))
                    q_blocks.append(mk_block(C_GR + b * 512, 512, "A", gr, b * 512, F32))
                    q_blocks.append(mk_block(C_DQ + b * 512, 512, "A", dq, b * 512, BF16))

                pj_groups = P.opts.get("pj_groups", list(range(8)))
                gemm_stage("pj", pj_groups, lambda g: hnT[g], KC, 512, CB["win"],
                           lambda g: kv_blocks + (q_blocks if g >= 3 else []),
                           extra_bufs=stB + stF + [stL])

        def A_(fn, r, w):
            S.op(act, fn, r, w)

        def V_(fn, r, w):
            S.op(dve, fn, r, w)

        def G_(fn, r, w):
            S.op(dve, fn, r, w)

        def rstd_act(dst_fn, src_fn, r, w):
            S.op(act, lambda h: h.activation(out=dst_fn(), in_=src_fn(), func=AF.Ln), r, w)
            S.op(act, lambda h: h.activation(out=dst_fn(), in_=dst_fn(), func=AF.Exp, scale=-0.5), w, w)

        def T_(fn, r, w, sig=True):
            S.op(pe, fn, r, w, signal=sig)

        def qg_of_tile(t):
            if t == QT0:
                return 0, 0
            return 1 + (t - 16) // 4, ((t - 16) % 4) * 128

        def transpose_out(src, nfeat_chunks, kc0, t, ptb, mT):
            qg, toff = qg_of_tile(t)
            for c8 in range(0, nfeat_chunks, 8):
                for k in range(8):
                    c = c8 + k
                    T_(lambda h, c=c, k=k: h.transpose(out=ptb[:, k * 128:(k + 1) * 128],
                                                       in_=src[:, c * 128:(c + 1) * 128], identity=ident_b[:]),
                       [src, ident_b], [ptb], sig=(k == 7))
                evac(lambda c8=c8: (mT[:, c8:c8 + 8, :], ptb[:].rearrange("p (k t) -> p k t", k=8)), ptb, [mT])
            S.dma(act, lambda h: h.dma_start(out=mixT[qg][:, kc0:kc0 + nfeat_chunks, toff:toff + 128],
                                              in_=mT[:, 0:nfeat_chunks, :]), None, mT)

        if want("proj32"):
            with ExitStack() as st:
                p32_x = P.sb(st, "p32_x", [128, KC, 128], F32)
                p32_w = [P.sb(st, "p32_w%d" % i, [128, KC, 256], F32) for i in range(2)]
                p32_o = P.sb(st, "p32_o", [128, 16, 128], F32)
                p32_ps = [P.ps(st, "p32_ps%d" % i, [128, 512], F32) for i in range(2)]
                S.dma(sp, lambda h: h.dma_start(out=p32_x[:], in_=hnT32), p32_x, None)
                for blk in range(8):
                    wbb = p32_w[blk % 2]
                    src = w_in[:, blk * 256:(blk + 1) * 256].rearrange("(kc p) n -> p kc n", p=128)
                    S.dma(sp, lambda h, wbb=wbb, src=src: h.dma_start(out=wbb[:], in_=src), wbb, None)
                    for j in range(2):
                        pb = p32_ps[j]
                        for kc in range(KC):
                            T_(lambda h, wbb=wbb, pb=pb, kc=kc, j=j: h.matmul(
                                pb[:, 0:128], lhsT=wbb[:, kc, j * 128:(j + 1) * 128], rhs=p32_x[:, kc, :],
                                start=(kc == 0), stop=(kc == KC - 1)), [wbb, p32_x], [pb], sig=(kc == KC - 1))
                        evac(lambda blk=blk, j=j, pb=pb: (p32_o[:, blk * 2 + j, :], pb[:, 0:128]), pb, [p32_o])
                S.dma(act, lambda h: h.dma_start(out=qk32T, in_=p32_o[:]), None, p32_o)
                S.barrier(release=[p32_x, p32_o] + p32_w)

        if want("cast"):
            cast_w(CB["gu"], wg_bf, w_ffn_gate, D, 512)
            cast_w(CB["gu"], wu_bf, w_ffn_up, D, 512)
            cast_w(CB["wd"], wd_bf, w_ffn_down, DFF, 1376)

        if want("gla"):
            with ExitStack() as st:
                W2 = P.sb(st, "gl_W2", [16, 1024], F32)
                b2 = P.sb(st, "gl_b2", [1, 1024], F32)
                ones1 = P.sb(st, "gl_ones", [1, 128], F32)
                gm = P.sb(st, "gl_gm", [128, 3, 128], F32)
                amask = P.sb(st, "gl_amask", [128, 128], F32)
                gng = P.sb(st, "gl_gng", [128, GDV], F32)
                mhalf = P.sb(st, "gl_mh", [128, 1], F32)
                S.dma(sp, lambda h: h.dma_start(out=W2[:], in_=gla_w_gate2), W2, None)
                S.dma(sp, lambda h: h.dma_start(out=b2[:], in_=gla_b_gate), b2, None)
                S.dma(sp, lambda h: h.dma_start(out=gm[:], in_=c_gmats), gm, None)
                S.dma(sp, lambda h: h.dma_start(out=amask[:], in_=c_attmask), amask, None)
                S.dma(sp, lambda h: h.dma_start(out=gng[:], in_=bcast_row(gla_norm_g, 128)), gng, None)
                G_(lambda h: h.memset(ones1[:], 1.0), [], [ones1])
                G_(lambda h: h.memset(mhalf[:], -0.5), [], [mhalf])
                kTg = [P.sb(st, "gl_kT%d" % i, [128, 8, 512], BF16) for i in range(2)]
                qTg = [P.sb(st, "gl_qT%d" % i, [128, 8, 512], BF16) for i in range(2)]
                vt = [P.sb(st, "gl_v%d" % i, [128, 2048], BF16) for i in range(2)]
                grt = [P.sb(st, "gl_gr%d" % i, [128, 2048], F32) for i in range(2)]
                glr = [P.sb(st, "gl_glr%d" % i, [16, 128], F32) for i in range(2)]
                e_sb_2 = [P.sb(st, "gl_e" + "%d" % i, [128, 1024], F32) for i in range(2)]
                spx_2 = [P.sb(st, "gl_sp" + "%d" % i, [128, 1024], F32) for i in range(2)]
                kes_2 = [P.sb(st, "gl_kes" + "%d" % i, [128, 1024], F32) for i in range(2)]
                eb_2 = [P.sb(st, "gl_eb" + "%d" % i, [128, 8, 128], F32) for i in range(2)]
                e3_2 = [P.sb(st, "gl_e3" + "%d" % i, [128, 8, 128], F32) for i in range(2)]
                e3n_2 = [P.sb(st, "gl_e3n" + "%d" % i, [128, 8, 128], F32) for i in range(2)]
                qsT_2 = [P.sb(st, "gl_qsT" + "%d" % i, [128, 8, 128], BF16) for i in range(2)]
                qdT_2 = [P.sb(st, "gl_qdT" + "%d" % i, [128, 8, 128], BF16) for i in range(2)]
                kdT_2 = [P.sb(st, "gl_kdT" + "%d" % i, [128, 8, 128], BF16) for i in range(2)]
                kend_2 = [P.sb(st, "gl_kend" + "%d" % i, [128, 1024], BF16) for i in range(2)]
                att_sb = P.sb(st, "gl_att", [128, 128], BF16)
                qk32 = P.sb(st, "gl_qk32", [128, 16, 128], F32)
                qd32 = P.sb(st, "gl_qd32", [128, 8, 128], F32)
                kd32 = P.sb(st, "gl_kd32", [128, 8, 128], F32)
                Sst = [P.sb(st, "gl_S%d" % i, [128, 512], F32) for i in range(8)]
                Sbf = [P.sb(st, "gl_Sb%d" % i, [128, 512], BF16) for i in range(8)]
                ysb = P.sb(st, "gl_y", [128, 512], F32)
                junk = P.sb(st, "gl_junk", [128, 512], BF16)
                nst = P.sb(st, "gl_nst", [128, 4], F32)
                sg = P.sb(st, "gl_sg", [128, 2048], F32)
                og = P.sb(st, "gl_og", [128, 2048], BF16)
                mT = P.sb(st, "gl_mT", [128, 16, 128], BF16)
                zb = P.ps(st, "gl_zb", [128, 1024], F32)
                ktp = P.ps(st, "gl_ktp", [128, 1024], BF16)
                attp = P.ps(st, "gl_attp", [128, 512], F32)
                op_ = P.ps(st, "gl_op", [128, 512], F32)
                kvps = [P.ps(st, "gl_kvp%d" % i, [128, 512], F32) for i in range(3)]
                kvc = [0]
                for i in range(8):
                    G_(lambda h, i=i: h.memset(Sst[i][:], 0.0), [], [Sst[i]])
                    G_(lambda h, i=i: h.memset(Sbf[i][:], 0.0), [], [Sbf[i]])
                gkT_v = gkT.rearrange("(c p) s -> p c s", p=128)
                gqT_v = gqT.rearrange("(c p) s -> p c s", p=128)
                gla_tiles = P.opts.get("gla_tiles", list(range(LT)))

                def gla_G(t):
                    g4, ti = t // 4, t % 4
                    isq = t >= QT0
                    kg, qg_ = kTg[g4 % 2], qTg[g4 % 2]
                    vb, grb, lrb = vt[t % 2], grt[t % 2], glr[t % 2]
                    tsl = slice(ti * 128, (ti + 1) * 128)
                    pp = gidx[t] % 2
                    e_sb, sp_, kes, eb, e3, e3n = e_sb_2[pp], spx_2[pp], kes_2[pp], eb_2[pp], e3_2[pp], e3n_2[pp]
                    qsT, qdT, kdT, kend = qsT_2[pp], qdT_2[pp], kdT_2[pp], kend_2[pp]
                    if ti == 0 or t == gla_tiles[0]:
                        S.dma(sp, lambda h: h.dma_start(out=kg[:], in_=gkT_v[:, :, g4 * 512:(g4 + 1) * 512]), kg, None)
                        if g4 >= 3:
                            S.dma(sp, lambda h: h.dma_start(out=qg_[:], in_=gqT_v[:, :, g4 * 512:(g4 + 1) * 512]), qg_, None)
                    S.dma(sp, lambda h: h.dma_start(out=vb[:], in_=gv[t * 128:(t + 1) * 128, :]), vb, None)
                    S.dma(sp, lambda h: h.dma_start(out=lrb[:], in_=glrT[:, t * 128:(t + 1) * 128]), lrb, None)
                    if isq:
                        S.dma(sp, lambda h: h.dma_start(out=grb[:], in_=gr[t * 128:(t + 1) * 128, :]), grb, None)
                    for hf in range(2):
                        cs = slice(hf * 512, (hf + 1) * 512)
                        T_(lambda h, cs=cs: h.matmul(zb[:, cs], lhsT=lrb[:, :], rhs=W2[:, cs], start=True, stop=False),
                           [lrb, W2], [zb], sig=False)
                        T_(lambda h, cs=cs: h.matmul(zb[:, cs], lhsT=ones1[:, :], rhs=b2[:, cs], start=False, stop=True),
                           [ones1, b2], [zb])
                    A_(lambda h: h.activation(out=e_sb[:], in_=zb[:], func=AF.Exp, scale=-1.0), [zb], [e_sb])
                    A_(lambda h: h.activation(out=sp_[:], in_=e_sb[:], func=AF.Ln, bias=1.0, scale=1.0), [e_sb], [sp_])
                    for hf in range(2):
                        cs = slice(hf * 512, (hf + 1) * 512)
                        T_(lambda h, cs=cs: h.matmul(zb[:, cs], lhsT=gm[:, 0, :], rhs=sp_[:, cs], start=True, stop=True),
                           [gm, sp_], [zb])
                    A_(lambda h: h.activation(out=kes[:], in_=zb[:], func=AF.Exp, scale=-1.0 / 16), [zb], [kes])
                    for half in range(2):
                        for k in range(4):
                            dc = half * 4 + k
                            T_(lambda h, dc=dc, k=k: h.matmul(zb[:, k * 256:(k + 1) * 256], lhsT=sp_[:, dc * 128:(dc + 1) * 128],
                                                              rhs=gm[:, 1:3, :], start=True, stop=True),
                               [sp_, gm], [zb], sig=(k == 3))
                        ds = slice(half * 4, half * 4 + 4)
                        pEv = zb[:].rearrange("p (k c) -> p k c", k=4)
                        A_(lambda h, ds=ds, pEv=pEv: h.activation(out=eb[:, ds, :], in_=pEv[:, :, 0:128], func=AF.Exp,
                                                          scale=-1.0 / 16), [zb], [eb])
                        if isq:
                            A_(lambda h, ds=ds, pEv=pEv: h.activation(out=e3[:, ds, :], in_=pEv[:, :, 128:256], func=AF.Exp,
                                                              scale=-1.0 / 16), [zb], [e3])
                            A_(lambda h, ds=ds, pEv=pEv: h.activation(out=e3n[:, ds, :], in_=pEv[:, :, 128:256], func=AF.Exp,
                                                              scale=1.0 / 16), [zb], [e3n])
                    if isq:
                        V_(lambda h: h.scalar_tensor_tensor(out=qsT[:], in0=qg_[:, :, tsl], scalar=1.0 / 16, in1=eb[:],
                                                            op0=ALU.mult, op1=ALU.mult), [qg_, eb], [qsT])
                        V_(lambda h: h.scalar_tensor_tensor(out=qdT[:], in0=qg_[:, :, tsl], scalar=1.0 / 16, in1=e3[:],
                                                            op0=ALU.mult, op1=ALU.mult), [qg_, e3], [qdT])
                        V_(lambda h: h.tensor_tensor(out=kdT[:], in0=kg[:, :, tsl], in1=e3n[:], op=ALU.mult),
                           [kg, e3n], [kdT])
                    hp = (t == 16) and want("proj32")
                    hpflag[t] = hp
                    if hp:
                        S.dma(sp, lambda h: h.dma_start(out=qk32[:], in_=qk32T), qk32, None)
                        V_(lambda h: h.scalar_tensor_tensor(out=qd32[:], in0=qk32[:, 0:8, :], scalar=1.0 / 16, in1=e3[:],
                                                            op0=ALU.mult, op1=ALU.mult), [qk32, e3], [qd32])
                        V_(lambda h: h.tensor_tensor(out=kd32[:], in0=qk32[:, 8:16, :], in1=e3n[:], op=ALU.mult),
                           [qk32, e3n], [kd32])
                    for dc in range(8):
                        T_(lambda h, dc=dc: h.transpose(out=ktp[:, dc * 128:(dc + 1) * 128], in_=kg[:, dc, tsl],
                                                        identity=ident_b[:]), [kg, ident_b], [ktp], sig=(dc == 7))
                    V_(lambda h: h.tensor_tensor(out=kend[:], in0=ktp[:], in1=kes[:], op=ALU.mult), [ktp, kes], [kend])


                def gla_H(t):
                    g4, ti = t // 4, t % 4
                    isq = t >= QT0
                    kg, qg_ = kTg[g4 % 2], qTg[g4 % 2]
                    vb, grb, lrb = vt[t % 2], grt[t % 2], glr[t % 2]
                    tsl = slice(ti * 128, (ti + 1) * 128)
                    pp = gidx[t] % 2
                    e_sb, sp_, kes, eb, e3, e3n = e_sb_2[pp], spx_2[pp], kes_2[pp], eb_2[pp], e3_2[pp], e3n_2[pp]
                    qsT, qdT, kdT, kend = qsT_2[pp], qdT_2[pp], kdT_2[pp], kend_2[pp]
                    hp = hpflag[t]
                    def head(hh):
                        es_ = slice(hh * 512, (hh + 1) * 512)
                        if isq:
                            for i, dc in enumerate((2 * hh, 2 * hh + 1)):
                                if hp:
                                    T_(lambda h, dc=dc, i=i: h.matmul(attp[:, 0:128], lhsT=kd32[:, dc, :], rhs=qd32[:, dc, :],
                                                                      start=(i == 0), stop=(i == 1)),
                                       [kd32, qd32], [attp], sig=(i == 1))
                                else:
                                    T_(lambda h, dc=dc, i=i: h.matmul(attp[:, 0:128], lhsT=kdT[:, dc, :], rhs=qdT[:, dc, :],
                                                                      start=(i == 0), stop=(i == 1)),
                                       [kdT, qdT], [attp], sig=(i == 1))
                            V_(lambda h: h.tensor_tensor(out=att_sb[:], in0=attp[:, 0:128], in1=amask[:], op=ALU.mult),
                               [attp, amask], [att_sb])
                            T_(lambda h: h.matmul(op_[:, :], lhsT=att_sb[:, :], rhs=vb[:, es_], start=True, stop=False),
                               [att_sb, vb], [op_], sig=False)
                        for ch in range(2):
                            ps_ = slice(ch * 64, (ch + 1) * 64)
                            if isq:
                                for i, dc in enumerate((2 * hh, 2 * hh + 1)):
                                    last = (ch == 1 and i == 1)
                                    T_(lambda h, dc=dc, last=last, ps_=ps_: h.matmul(
                                        op_[ps_, :], lhsT=qsT[:, dc, ps_], rhs=Sbf[dc][:, :], start=False, stop=last),
                                       [qsT, Sbf[dc]], [op_], sig=last)
                            if t == gla_tiles[-1] and ch == 1:
                                continue
                            for dc in (2 * hh, 2 * hh + 1):
                                kvp = kvps[kvc[0] % 3]
                                kvc[0] += 1
                                T_(lambda h, dc=dc, ps_=ps_, kvp=kvp: h.matmul(kvp[:, :], lhsT=kend[ps_, dc * 128:(dc + 1) * 128],
                                                                      rhs=vb[ps_, es_], start=True, stop=True),
                                   [kend, vb], [kvp])
                                V_(lambda h, dc=dc, ch=ch, kvp=kvp: h.scalar_tensor_tensor(
                                    out=Sst[dc][:], in0=Sst[dc][:], scalar=eb[:, dc, ch * 64 + 63:ch * 64 + 64],
                                    in1=kvp[:], op0=ALU.mult, op1=ALU.add), [Sst[dc], eb, kvp], [Sst[dc]])
                                A_(lambda h, dc=dc: h.copy(out=Sbf[dc][:], in_=Sst[dc][:]), [Sst[dc]], [Sbf[dc]])
                        if isq:
                            A_(lambda h: h.activation(out=junk[:], in_=op_[:], func=AF.Square, accum_out=nst[:, 0:1]),
                               [op_], [junk, nst])
                            V_(lambda h: h.tensor_scalar(out=nst[:, 1:2], in0=nst[:, 0:1], scalar1=1.0 / GDV, scalar2=EPS,
                                                         op0=ALU.mult, op1=ALU.add), [nst], [nst])
                            A_(lambda h: h.activation(out=nst[:, 2:3], in_=nst[:, 1:2], func=AF.Ln), [nst], [nst])
                            A_(lambda h: h.activation(out=nst[:, 2:3], in_=nst[:, 2:3], func=AF.Exp, scale=-0.5), [nst], [nst])
                            V_(lambda h: h.scalar_tensor_tensor(out=ysb[:], in0=op_[:], scalar=nst[:, 2:3], in1=gng[:],
                                                                op0=ALU.mult, op1=ALU.mult), [op_, nst, gng], [ysb])
                            V_(lambda h: h.tensor_tensor(out=og[:, es_], in0=ysb[:], in1=sg[:, es_], op=ALU.mult),
                               [ysb, sg], [og])
                    if isq:
                        A_(lambda h: h.activation(out=sg[:], in_=grb[:], func=AF.Silu), [grb], [sg])
                    for hh in range(GH):
                        head(hh)
                    if isq:
                        transpose_out(og, 16, 0, t, ktp, mT)

                gidx = {t: i for i, t in enumerate(gla_tiles)}
                hpflag = {}
                gla_G(gla_tiles[0])
                for i_, t in enumerate(gla_tiles):
                    if i_ + 1 < len(gla_tiles):
                        gla_G(gla_tiles[i_ + 1])
                    gla_H(t)
                S.barrier(release=[W2, b2, gm, amask, gng, qk32] + kTg + qTg + vt + grt + glr + [mT])

        DCFG = (1, 4, 16)
        if want("dil"):
            with ExitStack() as st:
                relb = P.sb(st, "dl_relb", [32, DH], F32)
                oneh = P.sb(st, "dl_oneh", [32, 3, 384], F32)
                bneg = P.sb(st, "dl_bneg", [DH, 3, 384], F32)
                antiI = P.sb(st, "dl_antiI", [128, 2, 256], F32)
                flags = P.sb(st, "dl_flags", [128, 2], F32)
                negc = P.sb(st, "dl_negc", [128, 1], F32)
                S.dma(sp, lambda h: h.dma_start(out=relb[:], in_=rel_bias), relb, None)
                S.dma(sp, lambda h: h.dma_start(out=oneh[:], in_=c_onehot), oneh, None)
                S.dma(sp, lambda h: h.dma_start(out=bneg[:], in_=c_bandneg), bneg, None)
                S.dma(sp, lambda h: h.dma_start(out=antiI[:], in_=c_antiI), antiI, None)
                S.dma(sp, lambda h: h.dma_start(out=flags[:], in_=c_flags), flags, None)
                G_(lambda h: h.memset(negc[:], NEG), [], [negc])
                fext_sb = P.sb(st, "dl_fext", [DH, 384], F32)
                Hk = P.sb(st, "dl_Hk", [128, 2, DH, 128], F32)
                tbl = P.sb(st, "dl_tbl", [128, DH, 256], F32)
                Qb = [P.sb(st, "dl_Q%d" % i, [128, 2048], BF16) for i in range(2)]
                Kb = [P.sb(st, "dl_K%d" % i, [128, 2048], BF16) for i in range(2)]
                Vb = [P.sb(st, "dl_V%d" % i, [128, 2048], BF16) for i in range(3)]
                QT = [P.sb(st, "dl_QT%d" % i, [128, DH, 128], BF16) for i in range(2)]
                KT = [P.sb(st, "dl_KT%d" % i, [128, DH, 128], BF16) for i in range(3)]
                Sp = [P.sb(st, "dl_Sp%d" % i, [128, 2, 256], F32) for i in range(2)]
                Pb = [P.sb(st, "dl_P%d" % i, [128, 256], BF16) for i in range(3)]
                PT = [P.sb(st, "dl_PT%d" % i, [128, 4, 256], BF16) for i in range(2)]
                Oall = [P.sb(st, "dl_O%d" % i, [128, 2048], F32) for i in range(2)]
                msb = [P.sb(st, "dl_ms%d" % i, [128, 2, DH], F32) for i in range(2)]
                nmx = [P.sb(st, "dl_nm%d" % i, [128, 2], F32) for i in range(2)]
                trp = [P.ps(st, "dl_trp%d" % i, [128, 1024], BF16) for i in range(2)]
                Sps = [P.ps(st, "dl_Sps%d" % i, [128, 512], F32) for i in range(2)]
                ptp = P.ps(st, "dl_ptp", [128, 1024], BF16)
                ops = [P.ps(st, "dl_ops%d" % i, [128, 512], F32) for i in range(2)]
                fextb = Buf("fext_d")
                ctr = {"u": 0, "v": 0, "k": 0}
                SCALE = DE ** -0.5
                dil_cfgs = P.opts.get("dil_cfgs", [0, 1, 2])

                def build_tables(ci):
                    T_(lambda h: h.matmul(Sps[0][0:DH, 0:384], lhsT=relb[:, :], rhs=oneh[:, ci, :], start=True, stop=True),
                       [relb, oneh], [Sps[0]])
                    V_(lambda h: h.tensor_tensor(out=fext_sb[:], in0=Sps[0][0:DH, 0:384], in1=bneg[:, ci, :], op=ALU.add),
                       [Sps[0], bneg], [fext_sb])
                    S.dma(act, lambda h: h.dma_start(out=fext_d[ci], in_=fext_sb[:]), fextb, fext_sb)
                    for c in range(2):
                        src = bass.AP(fext_d.tensor, ci * DH * 384 + c * 128, [[1, 128], [384, DH], [1, 128]])
                        S.dma(sp, lambda h, c=c, src=src: h.dma_start(out=Hk[:, c, :, :], in_=src), Hk, fextb)
                    for hh in range(DH):
                        pb = Sps[hh % 2]
                        for c in range(2):
                            T_(lambda h, hh=hh, c=c, pb=pb: h.matmul(pb[:, 0:256], lhsT=Hk[:, c, hh, :], rhs=antiI[:, c, :],
                                                                    start=(c == 0), stop=(c == 1)),
                               [Hk, antiI], [pb], sig=(c == 1))
                        evac(lambda hh=hh, pb=pb: (tbl[:, hh, :], pb[:, 0:256]), pb, [tbl])

                def load_rows(buf, src, d, r, b):
                    start = d * 128 * b + r
                    S.dma(sp, lambda h: h.dma_start(out=buf[:], in_=src[start:start + d * 127 + 1:d, :]), buf, None)

                def transpose_all(srcb, dstT):
                    for half in range(2):
                        pb = trp[half]
                        for k in range(8):
                            hh = half * 8 + k
                            T_(lambda h, hh=hh, k=k, pb=pb: h.transpose(out=pb[:, k * 128:(k + 1) * 128],
                                                                        in_=srcb[:, hh * 128:(hh + 1) * 128],
                                                                        identity=ident_b[:]),
                               [srcb, ident_b], [pb], sig=(k == 7))
                        evac(lambda half=half, pb=pb: (dstT[:, half * 8:(half + 1) * 8, :],
                                                       pb[:].rearrange("p (k t) -> p k t", k=8)), pb, [dstT])

                def unit(ci, d, r, b, kt_prev, v_prev, kt_cur, v_cur, cvar):
                    u = ctr["u"]
                    ctr["u"] += 1
                    qb, qt = Qb[u % 2], QT[u % 2]
                    ob, mb = Oall[u % 2], msb[u % 2]
                    load_rows(qb, dq, d, r, b)
                    transpose_all(qb, qt)
                    def pairA(hp):
                        sps = Sps[hp % 2]
                        for i in range(2):
                            hh = hp * 2 + i
                            T_(lambda h, hh=hh, i=i: h.matmul(sps[:, i * 256:i * 256 + 128], lhsT=qt[:, hh, :],
                                                              rhs=kt_prev[:, hh, :], start=True, stop=True),
                               [qt, kt_prev], [sps], sig=False)
                            T_(lambda h, hh=hh, i=i: h.matmul(sps[:, i * 256 + 128:i * 256 + 256], lhsT=qt[:, hh, :],
                                                              rhs=kt_cur[:, hh, :], start=True, stop=True),
                               [qt, kt_cur], [sps], sig=(i == 1))

                    def pairB(hp):
                        sps, spb, nm = Sps[hp % 2], Sp[hp % 2], nmx[hp % 2]
                        V_(lambda h, hp=hp: h.scalar_tensor_tensor(
                            out=spb[:], in0=sps[:].rearrange("p (i k) -> p i k", i=2), scalar=SCALE,
                            in1=tbl[:, 2 * hp:2 * hp + 2, :], op0=ALU.mult, op1=ALU.add), [sps, tbl], [spb])
                        if cvar is not None:
                            V_(lambda h: h.tensor_scalar(out=spb[:, :, 0:128], in0=spb[:, :, 0:128], scalar1=cvar,
                                                         scalar2=None, op0=ALU.add), [spb, flags, negc], [spb])
                        V_(lambda h: h.tensor_reduce(out=nm[:], in_=spb[:], axis=AX.X, op=ALU.max, negate=True),
                           [spb], [nm])
                        V_(lambda h, hp=hp: h.tensor_copy(out=mb[:, 0, 2 * hp:2 * hp + 2], in_=nm[:]), [nm], [mb])
                        for i in range(2):
                            hh = hp * 2 + i
                            pbuf = Pb[hh % 3]
                            A_(lambda h, hh=hh, i=i, pbuf=pbuf: h.activation(
                                out=pbuf[:], in_=spb[:, i, :], func=AF.Exp, bias=nm[:, i:i + 1], scale=1.0,
                                accum_out=mb[:, 1, hh:hh + 1]), [spb, nm], [pbuf, mb])
                            q4 = hh % 4
                            for c in range(2):
                                T_(lambda h, pbuf=pbuf, q4=q4, c=c: h.transpose(
                                    out=ptp[:, q4 * 256 + c * 128:q4 * 256 + (c + 1) * 128],
                                    in_=pbuf[:, c * 128:(c + 1) * 128], identity=ident_b[:]),
                                   [pbuf, ident_b], [ptp], sig=(c == 1))
                        if hp % 2 == 1:
                            g4 = hp // 2
                            ptb, opb = PT[g4 % 2], ops[g4 % 2]
                            evac(lambda ptb=ptb: (ptb[:], ptp[:].rearrange("p (a k) -> p a k", a=4)), ptp, [ptb])
                            for q4 in range(4):
                                hh = g4 * 4 + q4
                                T_(lambda h, hh=hh, q4=q4: h.matmul(opb[:, q4 * 128:(q4 + 1) * 128], lhsT=ptb[:, q4, 0:128],
                                                                    rhs=v_prev[:, hh * 128:(hh + 1) * 128],
                                                                    start=True, stop=False),
                                   [ptb, v_prev], [opb], sig=False)
                                T_(lambda h, hh=hh, q4=q4: h.matmul(opb[:, q4 * 128:(q4 + 1) * 128], lhsT=ptb[:, q4, 128:256],
                                                                    rhs=v_cur[:, hh * 128:(hh + 1) * 128],
                                                                    start=False, stop=True),
                                   [ptb, v_cur], [opb], sig=(q4 == 3))
                            evac(lambda g4=g4, opb=opb: (ob[:, g4 * 512:(g4 + 1) * 512], opb[:]), opb, [ob])
                    pairA(0)
                    for hp_ in range(DH // 2):
                        if hp_ + 1 < DH // 2:
                            pairA(hp_ + 1)
                        pairB(hp_)
                    start = d * 128 * b + r
                    rows = slice(start, start + d * 127 + 1, d)
                    S.dma(act, lambda h: h.dma_start(out=oc[ci][rows, :], in_=ob[:]), None, ob)
                    S.dma(act, lambda h: h.dma_start(out=msc[ci][rows, :, :], in_=mb[:]), None, mb)

                def load_kv(d, r, b):
                    kb = Kb[ctr["k"] % 2]
                    ktb = KT[ctr["k"] % 3]
                    vb = Vb[ctr["k"] % 3]
                    ctr["k"] += 1
                    load_rows(kb, dk, d, r, b)
                    load_rows(vb, dv, d, r, b)
                    transpose_all(kb, ktb)
                    return ktb, vb

                for ci in dil_cfgs:
                    d = DCFG[ci]
                    build_tables(ci)
                    nfirst = 16 // d
                    for r in range(d):
                        if d == 1:
                            qbs = list(range(15, 32))
                        elif d == 4:
                            qbs = list(range(3, 8)) if r >= 2 else list(range(4, 8))
                        else:
                            qbs = [0, 1] if r >= 14 else [1]
                        prev = None
                        for b in qbs:
                            if prev is None and b > 0:
                                prev = load_kv(d, r, b - 1)
                            cur = load_kv(d, r, b)
                            if b == 0:
                                cvar = negc[:, 0:1]
                                pk, pv = cur
                            else:
                                pk, pv = prev
                                cvar = flags[:, 0:1] if (b - 1) < nfirst else None
                            unit(ci, d, r, b, pk, pv, cur[0], cur[1], cvar)
                            prev = cur
                S.barrier(release=[relb, oneh, bneg, antiI, flags, Hk, fextb] + Qb + Kb + Vb + Oall + msb + [fext_sb])

        if want("dilc"):
            with ExitStack() as st:
                O3 = [[P.sb(st, "dc_O%d_%d" % (i, j), [128, 2048], F32) for j in range(3)] for i in range(2)]
                ms3 = [P.sb(st, "dc_ms%d" % i, [128, 3, 2, DH], F32) for i in range(2)]
                mneg = P.sb(st, "dc_mneg", [128, DH], F32)
                w3 = P.sb(st, "dc_w3", [128, 3, DH], F32)
                ws3 = P.sb(st, "dc_ws3", [128, 3, DH], F32)
                den = P.sb(st, "dc_den", [128, DH], F32)
                acc = P.sb(st, "dc_acc", [128, 2048], F32)
                tmp = P.sb(st, "dc_tmp", [128, 2048], F32)
                od = P.sb(st, "dc_od", [128, 2048], BF16)
                mT2 = P.sb(st, "dc_mT", [128, 16, 128], BF16)
                ptb2 = P.ps(st, "dc_ptb", [128, 1024], BF16)
                for it, t in enumerate(P.opts.get("dilc_tiles", list(range(QT0, LT)))):
                    Ob, mb = O3[it % 2], ms3[it % 2]
                    rows = slice(t * 128, (t + 1) * 128)
                    for ci in range(3):
                        S.dma(sp, lambda h, ci=ci, Ob=Ob, rows=rows: h.dma_start(out=Ob[ci][:], in_=oc[ci][rows, :]), Ob[ci], None)
                        S.dma(sp, lambda h, ci=ci, mb=mb, rows=rows: h.dma_start(out=mb[:, ci, :, :], in_=msc[ci][rows, :, :]), mb, None)
                    V_(lambda h, mb=mb: h.tensor_tensor(out=mneg[:], in0=mb[:, 0, 0, :], in1=mb[:, 1, 0, :], op=ALU.min),
                       [mb], [mneg])
                    V_(lambda h, mb=mb: h.tensor_tensor(out=mneg[:], in0=mneg[:], in1=mb[:, 2, 0, :], op=ALU.min),
                       [mb, mneg], [mneg])
                    for ci in range(3):
                        V_(lambda h, ci=ci, mb=mb: h.tensor_tensor(out=w3[:, ci, :], in0=mneg[:], in1=mb[:, ci, 0, :],
                                                                  op=ALU.subtract), [mneg, mb], [w3])
                    A_(lambda h: h.activation(out=w3[:], in_=w3[:], func=AF.Exp), [w3], [w3])
                    V_(lambda h, mb=mb: h.tensor_tensor(out=ws3[:], in0=w3[:], in1=mb[:, :, 1, :], op=ALU.mult),
                       [w3, mb], [ws3])
                    V_(lambda h: h.tensor_tensor(out=den[:], in0=ws3[:, 0, :], in1=ws3[:, 1, :], op=ALU.add), [ws3], [den])
                    V_(lambda h: h.tensor_tensor(out=den[:], in0=den[:], in1=ws3[:, 2, :], op=ALU.add), [ws3, den], [den])
                    V_(lambda h: h.reciprocal(out=den[:], in_=den[:]), [den], [den])
                    for ci in range(3):
                        V_(lambda h, ci=ci: h.tensor_tensor(out=w3[:, ci, :], in0=w3[:, ci, :], in1=den[:], op=ALU.mult),
                           [w3, den], [w3])

                    def bc(ci):
                        return w3[:, ci, :].unsqueeze(2).broadcast_to([128, DH, 128])

                    def v3(b_):
                        return b_[:].rearrange("p (a e) -> p a e", a=DH)
                    V_(lambda h, Ob=Ob: h.tensor_tensor(out=v3(acc), in0=v3(Ob[0]), in1=bc(0), op=ALU.mult),
                       [Ob[0], w3], [acc])
                    G_(lambda h, Ob=Ob: h.tensor_tensor(out=v3(tmp), in0=v3(Ob[1]), in1=bc(1), op=ALU.mult),
                       [Ob[1], w3], [tmp])
                    V_(lambda h: h.tensor_tensor(out=acc[:], in0=acc[:], in1=tmp[:], op=ALU.add), [acc, tmp], [acc])
                    G_(lambda h, Ob=Ob: h.tensor_tensor(out=v3(tmp), in0=v3(Ob[2]), in1=bc(2), op=ALU.mult),
                       [Ob[2], w3], [tmp])
                    V_(lambda h: h.tensor_tensor(out=od[:], in0=acc[:], in1=tmp[:], op=ALU.add), [acc, tmp], [od])
                    transpose_out(od, 16, 16, t, ptb2, mT2)
                S.barrier(release=[mT2] + O3[0] + O3[1] + ms3)

        def qg_rows(qg):
            return (1920, 128) if qg == 0 else (2048 + (qg - 1) * 512, 512)
        QGS = P.opts.get("qgs", [0, 1, 2, 3, 4])
        NORM_Q_GROUPS = [[15]] + [[16 + 4 * i + j for j in range(4)] for i in range(4)]

        def resid_gemm(tag, xT_scr, kcx, w_scr, castbuf, res_src, dst):
            with ExitStack() as st:
                rs = [P.sb(st, tag + "_rs%d" % i, [128, 4, 512], F32) for i in range(2)]
                so = [P.sb(st, tag + "_so%d" % i, [128, 4, 512], F32) for i in range(2)]
                rr = [0]

                def mk(c0):
                    state = {}

                    def epi(pb, m, g, blk):
                        r0, T = qg_rows(g)
                        if m == 0:
                            state["rs"], state["so"] = rs[rr[0] % 2], so[rr[0] % 2]
                            rr[0] += 1
                            rsb = state["rs"]
                            srcv = res_src[r0:r0 + T, c0:c0 + 512].rearrange("(m p) c -> p m c", p=128)
                            S.dma(sp, lambda h: h.dma_start(out=rsb[:, 0:T // 128, :], in_=srcv), rsb, None)
                        rsb, sob = state["rs"], state["so"]
                        V_(lambda h: h.tensor_tensor(out=sob[:, m, :], in0=pb[:, :], in1=rsb[:, m, :], op=ALU.add),
                           [pb, rsb], [sob])

                    def fin(g, blk):
                        r0, T = qg_rows(g)
                        sob = state["so"]
                        dv_ = dst[r0:r0 + T, c0:c0 + 512].rearrange("(m p) c -> p m c", p=128)
                        S.dma(act, lambda h: h.dma_start(out=dv_, in_=sob[:, 0:T // 128, :]), None, sob)
                    return (w_scr, c0, 512, "A", epi, fin)
                blocks = [mk(c0) for c0 in range(0, D, 512)]
                gemm_stage(tag, [(g, qg_rows(g)[1]) for g in QGS], lambda g: xT_scr[g][:, :, 0:qg_rows(g)[1]], kcx, 512,
                           castbuf, lambda g: blocks, extra_bufs=rs + so)

        if want("wout"):
            resid_gemm("wo", mixT, KC, wout_bf, CB["mid"], x_loc, h1)

        if want("norm2"):
            norm_stage("n2", lambda t: h1[t * 128:(t + 1) * 128, :], NORM_Q_GROUPS, norm_xattn_g,
                       dstT=lambda g: hn2T[qg_of_tile(g[0])[0]][:, :, 0:len(g) * 128])
            norm_stage("nm", lambda t: mem[t * 128:(t + 1) * 128, :], [[0, 1]], mem_norm_g,
                       dstT=lambda g: memT[0][:, :, 0:256])

        if want("xattn"):
            with ExitStack() as st:
                xstB = [P.sb(st, "xa_stB%d" % i, [128, 4, 512], BF16) for i in range(2)]
                xrr = [0]

                def mkx(w_scr, mode, dst_fn):
                    state = {}

                    def epi(pb, i, g, blk):
                        T = 256 if g == "mem" else qg_rows(g)[1]
                        if i == 0:
                            state["stg"] = xstB[xrr[0] % 2]
                            xrr[0] += 1
                        stg = state["stg"]
                        if mode == "B":
                            evac(lambda: (stg[:, i, 0:T], pb[:, 0:T]), pb, [stg])
                        else:
                            evac(lambda: (stg[:, i, :], pb[:, 0:512]), pb, [stg])

                    def fin(g, blk):
                        stg = state["stg"]
                        T = 256 if g == "mem" else qg_rows(g)[1]
                        if mode == "B":
                            S.dma(act, lambda h: h.dma_start(out=dst_fn(g), in_=stg[:, :, 0:T]), None, stg)
                        else:
                            S.dma(act, lambda h: h.dma_start(out=dst_fn(g), in_=stg[:, 0:T // 128, :]), None, stg)
                    return (w_scr, 0, 512, mode, epi, fin)
                gemm_stage("xkv", [("mem", 256)], lambda g: memT[0][:, :, 0:256], KC, 512, CB["mid"],
                           lambda g: [mkx(wxk_bf, "B", lambda g: kxT[:, :, :]),
                                      mkx(wxv_bf, "A", lambda g: vx[:, :, :])], extra_bufs=[])
                gemm_stage("xq", [(g, qg_rows(g)[1]) for g in QGS], lambda g: hn2T[g][:, :, 0:qg_rows(g)[1]], KC, 512,
                           CB["mid"], lambda g: [mkx(wxq_bf, "B", lambda g: qxT[g][:, :, 0:qg_rows(g)[1]])],
                           extra_bufs=xstB)
            with ExitStack() as st:
                kx = P.sb(st, "xa_kx", [128, 4, 256], BF16)
                vxs = P.sb(st, "xa_vx", [128, 2, 512], BF16)
                S.dma(sp, lambda h: h.dma_start(out=kx[:], in_=kxT[:, :, :]), kx, None)
                S.dma(sp, lambda h: h.dma_start(out=vxs[:], in_=vx[:, :, :]), vxs, None)
                qx = [P.sb(st, "xa_qx%d" % i, [128, 4, 512], BF16) for i in range(2)]
                xPb = [P.sb(st, "xa_P%d" % i, [128, 256], BF16) for i in range(2)]
                xPTb = [P.sb(st, "xa_PT%d" % i, [128, 256], BF16) for i in range(2)]
                st4 = [P.sb(st, "xa_st%d" % i, [128, 4], F32) for i in range(2)]
                oxb = [P.sb(st, "xa_ox%d" % i, [128, 512], BF16) for i in range(2)]
                oxT_sb = [P.sb(st, "xa_oxT%d" % i, [128, 4, 512], BF16) for i in range(2)]
                xSps = [P.ps(st, "xa_Sps%d" % i, [128, 512], F32) for i in range(2)]
                xptp = [P.ps(st, "xa_ptp%d" % i, [128, 1024], BF16) for i in range(2)]
                xops = [P.ps(st, "xa_ops%d" % i, [128, 512], F32) for i in range(2)]
                xSC = DE ** -0.5
                xcnt = [0]

                def xtile(gi, g, m):
                    qb, oT = qx[gi % 2], oxT_sb[gi % 2]
                    ob = oxb[xcnt[0] % 2]
                    tsl = slice(m * 128, (m + 1) * 128)
                    for hh in range(4):
                        u = xcnt[0] * 4 + hh
                        sps, pb_, ptb_, stt, ptp_, ops_ = xSps[u % 2], xPb[u % 2], xPTb[u % 2], st4[u % 2], xptp[u % 2], xops[u % 2]

                        def one(hh=hh, sps=sps, pb_=pb_, ptb_=ptb_, stt=stt, ptp_=ptp_, ops_=ops_):
                            T_(lambda h: h.matmul(sps[:, 0:256], lhsT=qb[:, hh, tsl], rhs=kx[:, hh, :], start=True, stop=True),
                               [qb, kx], [sps])
                            V_(lambda h: h.tensor_reduce(out=stt[:, 0:1], in_=sps[:, 0:256], axis=AX.X, op=ALU.max,
                                                         negate=True), [sps], [stt])
                            V_(lambda h: h.tensor_scalar(out=stt[:, 1:2], in0=stt[:, 0:1], scalar1=xSC, scalar2=None,
                                                         op0=ALU.mult), [stt], [stt])
                            A_(lambda h: h.activation(out=pb_[:], in_=sps[:, 0:256], func=AF.Exp, bias=stt[:, 1:2], scale=xSC,
                                                      accum_out=stt[:, 2:3]), [sps, stt], [pb_, stt])
                            V_(lambda h: h.reciprocal(out=stt[:, 3:4], in_=stt[:, 2:3]), [stt], [stt])
                            for c in range(2):
                                T_(lambda h, c=c: h.transpose(out=ptp_[:, c * 128:(c + 1) * 128],
                                                              in_=pb_[:, c * 128:(c + 1) * 128], identity=ident_b[:]),
                                   [pb_, ident_b], [ptp_], sig=(c == 1))
                            evac(lambda: (ptb_[:], ptp_[:, 0:256]), ptp_, [ptb_])
                            for c in range(2):
                                T_(lambda h, c=c: h.matmul(ops_[:, 0:128], lhsT=ptb_[:, c * 128:(c + 1) * 128],
                                                           rhs=vxs[:, c, hh * 128:(hh + 1) * 128], start=(c == 0), stop=(c == 1)),
                                   [ptb_, vxs], [ops_], sig=(c == 1))
                            V_(lambda h: h.tensor_scalar(out=ob[:, hh * 128:(hh + 1) * 128], in0=ops_[:, 0:128],
                                                         scalar1=stt[:, 3:4], scalar2=None, op0=ALU.mult), [ops_, stt], [ob])
                        one()
                    pt2 = xptp[xcnt[0] % 2]
                    for k in range(4):
                        T_(lambda h, k=k: h.transpose(out=pt2[:, k * 128:(k + 1) * 128], in_=ob[:, k * 128:(k + 1) * 128],
                                                      identity=ident_b[:]), [ob, ident_b], [pt2], sig=(k == 3))
                    evac(lambda: (oT[:, :, tsl], pt2[:, 0:512].rearrange("p (k t) -> p k t", k=4)), pt2, [oT])
                    xcnt[0] += 1

                for gi, g in enumerate(QGS):
                    r0, T = qg_rows(g)
                    qb, oT = qx[gi % 2], oxT_sb[gi % 2]
                    S.dma(sp, lambda h, qb=qb, g=g, T=T: h.dma_start(out=qb[:, :, 0:T], in_=qxT[g][:, :, 0:T]), qb, None)
                    for m in range(T // 128):
                        xtile(gi, g, m)
                    S.dma(act, lambda h, oT=oT, g=g, T=T: h.dma_start(out=oxT[g][:, :, 0:T], in_=oT[:, :, 0:T]), None, oT)
                S.barrier(release=[kx, vxs] + qx + oxT_sb)
            resid_gemm("xo", oxT, 4, wxo_bf, CB["mid"], h1, h2)

        if want("norm3"):
            norm_stage("n3", lambda t: h2[t * 128:(t + 1) * 128, :], NORM_Q_GROUPS, norm_ffn_g,
                       dstT=lambda g: hn3T[qg_of_tile(g[0])[0]][:, :, 0:len(g) * 128])

        if want("ffn"):
            with ExitStack() as st:
                f_cwrow = P.sb(st, "f_cwrow", [FC, 4, 128], F32)
                f_cw = P.sb(st, "f_cw", [128, 4, FC], F32)
                f_flags = P.sb(st, "f_flags", [128, 2], F32)
                f_gprev = P.sb(st, "f_gprev", [128, FC, 2], F32)
                f_xT = P.sb(st, "f_xT", [128, KC, 512], BF16)
                f_xh = P.sb(st, "f_xh", [128, KC, 128], BF16)
                f_act = P.sb(st, "f_act", [128, 43, 512], BF16)
                f_wb = [P.sb(st, "f_wb%d" % i, [128, KC, 256], BF16) for i in range(3)]
                f_wd = [P.sb(st, "f_wd%d" % i, [128, 8, 512], BF16) for i in range(4)]
                f_gx = [P.sb(st, "f_gx%d" % i, [128, 514], F32) for i in range(2)]
                f_acc = [P.sb(st, "f_acc%d" % i, [128, 512], F32) for i in range(2)]
                f_sg = [P.sb(st, "f_sg%d" % i, [128, 2, 512], F32) for i in range(2)]
                f_res = [P.sb(st, "f_res%d" % i, [128, 512], F32) for i in range(4)]
                f_so = [P.sb(st, "f_so%d" % i, [128, 512], F32) for i in range(4)]
                f_ps = [P.ps(st, "f_ps%d" % i, [128, 512], F32) for i in range(3)]
                f_psh = P.ps(st, "f_psh", [128, 512], F32)
                f_psd = [P.ps(st, "f_psd%d" % i, [128, 512], F32) for i in range(4)]
                S.dma(sp, lambda h: h.dma_start(out=f_cwrow[:, 0:3, :], in_=ffn_conv_w.rearrange("i (j p) -> j i p", p=128)),
                      f_cwrow, None)
                S.dma(sp, lambda h: h.dma_start(out=f_cwrow[:, 3:4, :], in_=ffn_conv_b.rearrange("o (j p) -> j o p", p=128)),
                      f_cwrow, None)
                S.dma(sp, lambda h: h.dma_start(out=f_flags[:], in_=c_flags), f_flags, None)
                for i in range(4):
                    T_(lambda h, i=i: h.transpose(out=f_ps[0][:, i * 128:i * 128 + FC], in_=f_cwrow[:, i, :],
                                                  identity=ident_f[0:FC, 0:FC]), [f_cwrow, ident_f], [f_ps[0]], sig=(i == 3))
                V_(lambda h: h.tensor_copy(out=f_cw[:], in_=f_ps[0][:].rearrange("p (i j) -> p i j", i=4)[:, :, 0:FC]),
                   [f_ps[0]], [f_cw])
                G_(lambda h: h.memset(f_gprev[:], 0.0), [], [f_gprev])
                fc = {"w": 0, "p": 0, "c": 0, "d": 0, "e": 0}
                ffn_tgs = P.opts.get("ffn_tgs", [1, 2, 3, 4])
                h3b = {tg: Buf("h3b%d" % tg) for tg in ffn_tgs}

                def load_wblk(w_scr, c0, ncols):
                    wbb = f_wb[fc["w"] % 3]
                    fc["w"] += 1
                    src = w_scr[:, c0:c0 + ncols].rearrange("(kc p) n -> p kc n", p=128)
                    S.dma(sp, lambda h: h.dma_start(out=wbb[:, :, 0:ncols], in_=src), wbb, CB["gu"])
                    return wbb

                def mm_chunk(wbb, c):
                    pb = f_ps[fc["p"] % 3]
                    fc["p"] += 1
                    for kc in range(KC):
                        T_(lambda h, kc=kc: h.matmul(pb[:, :], lhsT=wbb[:, kc, c * 128:(c + 1) * 128], rhs=f_xT[:, kc, :],
                                                     start=(kc == 0), stop=(kc == KC - 1)), [wbb, f_xT], [pb], sig=(kc == KC - 1))
                    return pb

                def gate_chunk(tg, wbb, c, j, sgb):
                    pb = mm_chunk(wbb, c)
                    if tg == 1:
                        for kc in range(KC):
                            T_(lambda h, kc=kc: h.matmul(f_psh[:, 0:2], lhsT=wbb[:, kc, c * 128:(c + 1) * 128], rhs=f_xh[:, kc, 126:128],
                                                         start=(kc == 0), stop=(kc == KC - 1)), [wbb, f_xh], [f_psh],
                               sig=(kc == KC - 1))
                        V_(lambda h: h.tensor_scalar(out=f_gprev[:, j, :], in0=f_psh[:, 0:2], scalar1=f_flags[:, 1:2],
                                                     scalar2=None, op0=ALU.mult), [f_psh, f_flags], [f_gprev])
                    gx, acc = f_gx[fc["c"] % 2], f_acc[fc["c"] % 2]
                    fc["c"] += 1
                    A_(lambda h: h.copy(out=gx[:, 2:514], in_=pb[:, :]), [pb], [gx])
                    G_(lambda h: h.tensor_copy(out=gx[:, 0:2], in_=f_gprev[:, j, :]), [f_gprev], [gx])
                    G_(lambda h: h.tensor_copy(out=f_gprev[:, j, :], in_=gx[:, 512:514]), [gx], [f_gprev])
                    V_(lambda h: h.tensor_scalar(out=acc[:], in0=gx[:, 0:512], scalar1=f_cw[:, 0, j:j + 1],
                                                 scalar2=f_cw[:, 3, j:j + 1], op0=ALU.mult, op1=ALU.add), [gx, f_cw], [acc])
                    V_(lambda h: h.scalar_tensor_tensor(out=acc[:], in0=gx[:, 1:513], scalar=f_cw[:, 1, j:j + 1], in1=acc[:],
                                                        op0=ALU.mult, op1=ALU.add), [gx, f_cw, acc], [acc])
                    V_(lambda h: h.scalar_tensor_tensor(out=acc[:], in0=gx[:, 2:514], scalar=f_cw[:, 2, j:j + 1], in1=acc[:],
                                                        op0=ALU.mult, op1=ALU.add), [gx, f_cw, acc], [acc])
                    A_(lambda h: h.activation(out=sgb[:, c, :], in_=acc[:], func=AF.Silu), [acc], [sgb])

                def up_chunk(wbb, c, jl, sgb):
                    pb = mm_chunk(wbb, c)
                    V_(lambda h: h.tensor_tensor(out=f_act[:, jl, :], in0=pb[:, :], in1=sgb[:, c, :], op=ALU.mult),
                       [pb, sgb], [f_act])

                def down_phase(tg, half):
                    r0, T = qg_rows(tg)
                    j0 = half * 43
                    src_h = h2 if half == 0 else h3
                    for nb in range(8):
                        cs = slice(nb * 512, (nb + 1) * 512)
                        psd = f_psd if nb % 2 == 0 else [f_ps[0], f_ps[1], f_ps[2], f_psh]
                        for m in range(4):
                            rr_ = slice(r0 + m * 128, r0 + (m + 1) * 128)
                            S.dma(sp, lambda h, m=m, rr_=rr_, cs=cs: h.dma_start(out=f_res[m][:], in_=src_h[rr_, cs]), f_res[m],
                                  h3b[tg] if half == 1 else None)
                        for jg in range(0, 43, 8):
                            nj = min(8, 43 - jg)
                            wdp = f_wd[fc["d"] % 4]
                            fc["d"] += 1
                            rows = slice((j0 + jg) * 128, (j0 + jg + nj) * 128)
                            srcw = wd_bf[rows, cs].rearrange("(jj p) c -> p jj c", p=128)
                            S.dma(sp, lambda h, wdp=wdp, srcw=srcw, nj=nj: h.dma_start(out=wdp[:, 0:nj, :], in_=srcw),
                                  wdp, CB["wd"])
                            for jj in range(nj):
                                jl = jg + jj
                                for m in range(4):
                                    T_(lambda h, wdp=wdp, jj=jj, jl=jl, m=m, nj=nj, psd=psd: h.matmul(
                                        psd[m][:, :], lhsT=f_act[:, jl, m * 128:(m + 1) * 128], rhs=wdp[:, jj, :],
                                        start=(jl == 0), stop=(jl == 42)), [f_act, wdp], [psd[m]], sig=(jl == 42 or (jj == nj - 1 and m == 3)))
                        for m in range(4):
                            rb, sob = f_res[m], f_so[m]
                            rr_ = slice(r0 + m * 128, r0 + (m + 1) * 128)
                            V_(lambda h, rb=rb, sob=sob, m=m, psd=psd: h.tensor_tensor(out=sob[:], in0=psd[m][:, :], in1=rb[:], op=ALU.add),
                               [psd[m], rb], [sob])
                            S.dma(act, lambda h, sob=sob, rr_=rr_, cs=cs: h.dma_start(out=h3[rr_, cs], in_=sob[:]), h3b[tg], sob)

                for tg in ffn_tgs:
                    S.dma(sp, lambda h, tg=tg: h.dma_start(out=f_xT[:], in_=hn3T[tg]), f_xT, None)
                    if tg == 1:
                        S.dma(sp, lambda h: h.dma_start(out=f_xh[:], in_=hn3T[0][:, :, 0:128]), f_xh, None)
                    for half in range(2):
                        j0 = half * 43
                        for b0 in range(0, 43, 2):
                            nch = min(2, 43 - b0)
                            c0 = (j0 + b0) * 128
                            sgb = f_sg[(b0 // 2) % 2]
                            wbb = load_wblk(wg_bf, c0, nch * 128)
                            for c in range(nch):
                                gate_chunk(tg, wbb, c, j0 + b0 + c, sgb)
                            wbb = load_wblk(wu_bf, c0, nch * 128)
                            for c in range(nch):
                                up_chunk(wbb, c, b0 + c, sgb)
                        down_phase(tg, half)
                S.barrier(release=[f_cwrow, f_flags, f_xT, f_xh] + f_wb + f_wd + f_res + f_so + list(h3b.values()))

        if want("final"):
            norm_stage("nf", lambda t: h3[t * 128:(t + 1) * 128, :], [[t] for t in P.opts.get("final_tiles", list(range(16, 32)))],
                       final_norm_g, dst_tok=lambda t: out[(t - 16) * 128:(t - 15) * 128, :])

        S.barrier()
        with nc.Block() as block:
            S.emit(block)
    return nc, P, S


def host_consts():
    c = {}
    c["c_ident"] = np.eye(128, dtype=np.float32)
    j = np.arange(128)[:, None]
    cc = np.arange(128)[None, :]
    same = (j // 64) == (cc // 64)
    tri = (same & (j <= cc)).astype(np.float32)
    m2 = (same & (j > cc)).astype(np.float32)
    mid = (same & ((j % 64) <= 32)).astype(np.float32)
    m3 = tri - mid
    c["c_gmats"] = np.ascontiguousarray(np.stack([m2, tri, m3], axis=1)).astype(np.float32)
    c["c_attmask"] = tri.copy()
    kk = np.arange(256)[:, None]
    kj = np.arange(256)[None, :]
    J = (kk == 255 - kj).astype(np.float32)
    c["c_antiI"] = np.ascontiguousarray(J.reshape(2, 128, 256).transpose(1, 0, 2))
    onehot = np.zeros((32, 3, 384), np.float32)
    bandneg = np.zeros((DH, 3, 384), np.float32)
    for ci, d in enumerate((1, 4, 16)):
        for i in range(384):
            rel = i - 127
            if 0 <= rel <= 128:
                dist = rel * d
                if dist < 16:
                    b = dist
                else:
                    b = 16 + int(np.float32(np.log(np.float32(max(dist, 1)) / np.float32(16)) /
                                            np.float32(np.log(2048 / 16)) * np.float32(16)))
                    b = min(b, 31)
                onehot[b, ci, i] = 1.0
            else:
                bandneg[:, ci, i] = NEG
    c["c_onehot"] = onehot
    c["c_bandneg"] = bandneg
    return c


def make_in_maps(inputs):
    consts = host_consts()
    x = np.asarray(inputs["x"], dtype=np.float32)
    maps = []
    for c in range(8):
        b, half = c // 2, c % 2
        m = {}
        if half == 0:
            xl = np.zeros((SEQ, D), np.float32)
            xl[2048:] = x[b, :2048]
        else:
            xl = np.ascontiguousarray(x[b])
        m["x_loc"] = xl
        m["mem"] = np.ascontiguousarray(inputs["mem"][b], dtype=np.float32)
        m["rel_bias"] = np.ascontiguousarray(inputs["rel_bias"], dtype=np.float32)
        for k in ("norm_mix_g", "gla_b_gate", "gla_norm_g", "norm_xattn_g", "mem_norm_g", "norm_ffn_g", "ffn_conv_b"):
            m[k] = np.ascontiguousarray(np.asarray(inputs[k], dtype=np.float32).reshape(1, -1))
        m["final_norm_g"] = np.ascontiguousarray(np.asarray(inputs["final_norm_g"], dtype=np.float32).reshape(1, -1))
        for k in ("w_in", "gla_w_gate2", "w_out", "w_xq", "w_xk", "w_xv", "w_xo", "w_ffn_gate", "w_ffn_up",
                  "ffn_conv_w", "w_ffn_down"):
            m[k] = np.ascontiguousarray(np.asarray(inputs[k], dtype=np.float32)[0])
        m.update(consts)
        fl = np.zeros((128, 2), np.float32)
        fl[:, 0] = NEG if half == 0 else 0.0
        fl[:, 1] = 0.0 if half == 0 else 1.0
        m["c_flags"] = fl
        maps.append(m)
    return maps


def kernel(**inputs):
    nc, P, S = build()
    maps = make_in_maps(inputs)
    res = run_bass_kernel_spmd(nc, maps, core_ids=list(range(8)))
    outp = np.empty((4, SEQ, D), np.float32)
    for c in range(8):
        b, half = c // 2, c % 2
        outp[b, half * 2048:(half + 1) * 2048] = res.results[c]["out"]
    return outp
```

```python
import numpy as np
from contextlib import ExitStack
import concourse.bass as bass
import concourse.mybir as mybir
from concourse.bass_utils import run_bass_kernel_spmd

F32 = mybir.dt.float32
BF16 = mybir.dt.bfloat16
AF = mybir.ActivationFunctionType
ALU = mybir.AluOpType
AX = mybir.AxisListType

D = 4096
SEQ = 4096
LT = 32
QT0 = 15
NQT = LT - QT0
KC = D // 128
GH, GDK, GDV = 4, 256, 512
DH, DE = 16, 128
DFF = 11008
FC = DFF // 128
XW = 512
MEM = 256
NCOL = 12304
C_GQ, C_GK, C_GV, C_GLR, C_GR, C_DQ, C_DK, C_DV = 0, 1024, 2048, 4096, 4112, 6160, 8208, 10256
EPS = 1e-6
NEG = -30000.0


class Buf:
    def __init__(self, name, t=None):
        self.name = name
        self.t = t
        self.lw = None
        self.rd = []
        self.sem = None
        self.cnt = 0

    def __getitem__(self, k):
        return self.t[k]


class Eng:
    def __init__(self, name, same_raw):
        self.name = name
        self.ops = []
        self.cnt = 0
        self.waited = {}
        self.sem = None
        self.same_raw = same_raw


class Sched:
    def __init__(self, nc, es):
        self.nc = nc
        self.es = es
        self.pe = Eng("pe", False)
        self.act = Eng("act", True)
        self.dve = Eng("dve", True)
        self.pool = Eng("pool", True)
        self.sp = Eng("sp", False)
        self.engs = [self.pe, self.act, self.dve, self.pool, self.sp]
        for e in self.engs[:4]:
            e.sem = es.enter_context(nc.semaphore("sem_" + e.name))
        self.dsems = []
        self.free_dsems = []
        self.nsem = 0

    def dma_sem(self, barrier=True):
        if barrier and self.free_dsems:
            return self.free_dsems.pop()
        s = self.es.enter_context(self.nc.semaphore("dsem%d" % self.nsem))
        self.nsem += 1
        rec = [s, 0]
        if barrier:
            self.dsems.append(rec)
        return rec

    def _waits(self, eng, deps):
        ws = []
        for d in deps:
            if d is None:
                continue
            sem, val = d
            if sem is eng.sem and not eng.same_raw:
                continue
            k = id(sem)
            if eng.waited.get(k, 0) < val:
                eng.waited[k] = val
                ws.append((sem, val))
        return ws

    def op(self, eng, fn, reads=(), writes=(), signal=True):
        deps = []
        for b in reads:
            deps.append(b.lw)
        for b in writes:
            if b.lw is not None and b.lw[0] is not eng.sem:
                deps.append(b.lw)
            for r in b.rd:
                if r[0] is not eng.sem:
                    deps.append(r)
        ws = self._waits(eng, deps)
        val = eng.cnt + 1
        if signal:
            eng.cnt = val
            eng.ops.append((ws, fn, (eng.sem, 1)))
        else:
            eng.ops.append((ws, fn, None))
        for b in reads:
            b.rd.append((eng.sem, val))
        for b in writes:
            b.lw = (eng.sem, val)
            b.rd = []

    def dma(self, eng, fn, out, in_):
        deps = []
        if in_ is not None:
            deps.append(in_.lw)
        if out is not None:
            deps.append(out.lw)
            deps.extend(out.rd)
        ws = self._waits(eng, deps)
        tgt = out if out is not None else in_
        if tgt.sem is None:
            tgt.sem = self.dma_sem()
        tgt.sem[1] += 16
        key = (tgt.sem[0], tgt.sem[1])
        eng.ops.append((ws, fn, (tgt.sem[0], 16)))
        if out is not None:
            out.lw = key
            out.rd = []
        if in_ is not None:
            in_.rd.append(key)

    def barrier(self, release=()):
        for e in self.engs:
            deps = []
            for o in self.engs[:4]:
                if o is not e and o.cnt > 0:
                    deps.append((o.sem, o.cnt))
            for rec in self.dsems:
                if rec[1] > 0:
                    deps.append((rec[0], rec[1]))
            ws = self._waits(e, deps)
            if ws:
                e.ops.append((ws, None, None))
        for b in release:
            if b.sem is not None:
                self.free_dsems.append(b.sem)
                b.sem = None

    def emit(self, block):
        def run(eng, h):
            for ws, fn, inc in eng.ops:
                for sem, val in ws:
                    h.wait_ge(sem, val)
                if fn is not None:
                    ins = fn(h)
                    if inc is not None:
                        ins.then_inc(inc[0], inc[1])

        @block.tensor
        def _(h):
            run(self.pe, h)

        @block.scalar
        def _(h):
            run(self.act, h)

        @block.vector
        def _(h):
            run(self.dve, h)

        @block.gpsimd
        def _(h):
            run(self.pool, h)

        @block.sync
        def _(h):
            run(self.sp, h)


class Prog:
    def __init__(self, debug_outs=(), stages=None):
        self.nc = bass.Bass("TRN2", target_bir_lowering=False)
        self.es = ExitStack()
        self.debug_outs = set(debug_outs)
        self.stages = stages
        self.ins = {}
        self.scr = {}
        self.npsum = 0

    def inp(self, name, shape, dt=F32):
        t = self.nc.dram_tensor(name, list(shape), dt, kind="ExternalInput")
        self.ins[name] = t
        return t.ap()

    def scratch(self, name, shape, dt):
        kind = "ExternalOutput" if name in self.debug_outs else "Internal"
        t = self.nc.dram_tensor(name, list(shape), dt, kind=kind)
        self.scr[name] = t
        return t.ap()

    def sb(self, st, name, shape, dt):
        t = st.enter_context(self.nc.sbuf_tensor(name, list(shape), dt))
        return Buf(name, t)

    def ps(self, st, name, shape, dt):
        t = st.enter_context(self.nc.psum_tensor(name, list(shape), dt))
        return Buf(name, t)


def bcast_row(ap_row, nparts):
    return bass.AP(ap_row.tensor, ap_row.offset, [[0, nparts]] + [list(x) for x in ap_row.ap[1:]])


def build(debug_outs=(), stages=None, opts=None):
    P = Prog(debug_outs, stages)
    P.opts = opts or {}
    nc = P.nc
    es = P.es
    S = Sched(nc, es)
    pe, act, dve, pool, sp = S.pe, S.act, S.dve, S.pool, S.sp

    def want(name):
        return stages is None or name in stages

    x_loc = P.inp("x_loc", [SEQ, D])
    mem = P.inp("mem", [MEM, D])
    rel_bias = P.inp("rel_bias", [32, DH])
    norm_mix_g = P.inp("norm_mix_g", [1, D])
    w_in = P.inp("w_in", [D, NCOL])
    gla_w_gate2 = P.inp("gla_w_gate2", [16, 1024])
    gla_b_gate = P.inp("gla_b_gate", [1, 1024])
    gla_norm_g = P.inp("gla_norm_g", [1, GDV])
    w_out = P.inp("w_out", [D, D])
    norm_xattn_g = P.inp("norm_xattn_g", [1, D])
    mem_norm_g = P.inp("mem_norm_g", [1, D])
    w_xq = P.inp("w_xq", [D, XW])
    w_xk = P.inp("w_xk", [D, XW])
    w_xv = P.inp("w_xv", [D, XW])
    w_xo = P.inp("w_xo", [XW, D])
    norm_ffn_g = P.inp("norm_ffn_g", [1, D])
    w_ffn_gate = P.inp("w_ffn_gate", [D, DFF])
    w_ffn_up = P.inp("w_ffn_up", [D, DFF])
    ffn_conv_w = P.inp("ffn_conv_w", [3, DFF])
    ffn_conv_b = P.inp("ffn_conv_b", [1, DFF])
    w_ffn_down = P.inp("w_ffn_down", [DFF, D])
    final_norm_g = P.inp("final_norm_g", [1, D])
    c_ident = P.inp("c_ident", [128, 128])
    c_gmats = P.inp("c_gmats", [128, 3, 128])
    c_attmask = P.inp("c_attmask", [128, 128])
    c_antiI = P.inp("c_antiI", [128, 2, 256])
    c_onehot = P.inp("c_onehot", [32, 3, 384])
    c_bandneg = P.inp("c_bandneg", [DH, 3, 384])
    c_flags = P.inp("c_flags", [128, 2])
    out = nc.dram_tensor("out", [2048, D], F32, kind="ExternalOutput").ap()

    win_bf = P.scratch("win_bf", [D, NCOL], BF16)
    wout_bf = P.scratch("wout_bf", [D, D], BF16)
    wxq_bf = P.scratch("wxq_bf", [D, XW], BF16)
    wxk_bf = P.scratch("wxk_bf", [D, XW], BF16)
    wxv_bf = P.scratch("wxv_bf", [D, XW], BF16)
    wxo_bf = P.scratch("wxo_bf", [XW, D], BF16)
    wg_bf = P.scratch("wg_bf", [D, DFF], BF16)
    wu_bf = P.scratch("wu_bf", [D, DFF], BF16)
    wd_bf = P.scratch("wd_bf", [DFF, D], BF16)
    hnT = P.scratch("hnT", [8, 128, KC, 512], BF16)
    gqT = P.scratch("gqT", [1024, SEQ], BF16)
    gkT = P.scratch("gkT", [1024, SEQ], BF16)
    gv = P.scratch("gv", [SEQ, 2048], BF16)
    glrT = P.scratch("glrT", [16, SEQ], F32)
    gr = P.scratch("gr", [SEQ, 2048], F32)
    dq = P.scratch("dq", [SEQ, 2048], BF16)
    dk = P.scratch("dk", [SEQ, 2048], BF16)
    dv = P.scratch("dv", [SEQ, 2048], BF16)
    fext_d = P.scratch("fext_d", [3, DH, 384], F32)
    oc = [P.scratch("oc%d" % i, [SEQ, 2048], F32) for i in range(3)]
    msc = [P.scratch("msc%d" % i, [SEQ, 2, DH], F32) for i in range(3)]
    h1 = P.scratch("h1", [SEQ, D], F32)
    h2 = P.scratch("h2", [SEQ, D], F32)
    h3 = P.scratch("h3", [SEQ, D], F32)
    hn2T = P.scratch("hn2T", [5, 128, KC, 512], BF16)
    hn3T = P.scratch("hn3T", [5, 128, KC, 512], BF16)
    memT = P.scratch("memT", [1, 128, KC, 512], BF16)
    kxT = P.scratch("kxT", [128, 4, 256], BF16)
    vx = P.scratch("vx", [128, 2, 512], BF16)
    qxT = P.scratch("qxT", [5, 128, 4, 512], BF16)
    oxT = P.scratch("oxT", [5, 128, 4, 512], BF16)
    hnT32 = P.scratch("hnT32", [128, KC, 128], F32)
    qk32T = P.scratch("qk32T", [128, 16, 128], F32)
    mixT = P.scratch("mixT", [5, 128, KC, 512], BF16)

    with es:
        ident_f = P.sb(es, "ident_f", [128, 128], F32)
        ident_b = P.sb(es, "ident_b", [128, 128], BF16)
        S.dma(sp, lambda h: h.dma_start(out=ident_f[:], in_=c_ident), ident_f, None)
        S.op(dve, lambda h: h.tensor_copy(out=ident_b[:], in_=ident_f[:]), [ident_f], [ident_b])

        def cast_w(cb, dst, src, rows, rstep):
            for r0 in range(0, rows, rstep):
                r1 = min(rows, r0 + rstep)
                S.dma(pool, lambda h, r0=r0, r1=r1: h.dma_start(out=dst[r0:r1, :], in_=src[r0:r1, :],
                                                                max_dma_last_dim=8192), cb, None)
        CB = {}
        for nm in ("win", "mid", "gu", "wd"):
            CB[nm] = Buf("cast_" + nm)
            CB[nm].sem = S.dma_sem(barrier=False)
        if want("cast"):
            cast_w(CB["win"], win_bf, w_in, D, 512)
            cast_w(CB["mid"], wout_bf, w_out, D, 1024)
            cast_w(CB["mid"], wxq_bf, w_xq, D, 4096)
            cast_w(CB["mid"], wxk_bf, w_xk, D, 4096)
            cast_w(CB["mid"], wxv_bf, w_xv, D, 4096)
            cast_w(CB["mid"], wxo_bf, w_xo, XW, 512)

        def norm_stage(tag, src_rows, tiles, gain_row, dstT=None, dst_tok=None, dst_tok_dt=F32, f32T=None):
            with ExitStack() as st:
                gain = P.sb(st, tag + "_gain", [128, D], F32)
                S.dma(sp, lambda h: h.dma_start(out=gain[:], in_=bcast_row(gain_row, 128)), gain, None)
                xt = [P.sb(st, tag + "_x%d" % i, [128, D], F32) for i in range(2)]
                junk = P.sb(st, tag + "_junk", [128, D], BF16)
                hn = [P.sb(st, tag + "_hn%d" % i, [128, D], BF16 if dstT is not None else dst_tok_dt) for i in range(2)]
                ss = [P.sb(st, tag + "_ss%d" % i, [128, 4], F32) for i in range(2)]
                mhalf = P.sb(st, tag + "_mh", [128, 1], F32)
                S.op(dve, lambda h: h.memset(mhalf[:], -0.5), [], [mhalf])
                ng = 4
                grp = [P.sb(st, tag + "_grp%d" % i, [128, KC, 512], BF16) for i in range(2)] if dstT is not None else None
                pst = [P.ps(st, tag + "_pt%d" % i, [128, 1024], BF16) for i in range(2)] if dstT is not None else None
                allb = [gain, junk, mhalf] + xt + hn + ss + (grp or [])
                if f32T is not None:
                    hn32 = P.sb(st, tag + "_hn32", [128, D], F32)
                    hT32 = P.sb(st, tag + "_hT32", [128, KC, 128], F32)
                    ps32 = P.ps(st, tag + "_ps32", [128, 512], F32)
                    allb += [hn32, hT32]
                groups = tiles if (len(tiles) > 0 and isinstance(tiles[0], list)) else \
                    [tiles[i:i + ng] for i in range(0, len(tiles), ng)]
                it = 0
                for gi, g in enumerate(groups):
                    gb = grp[gi % 2] if grp else None
                    for ti, t in enumerate(g):
                        xb, hb, sb_ = xt[it % 2], hn[it % 2], ss[it % 2]
                        S.dma(sp, lambda h, xb=xb, t=t: h.dma_start(out=xb[:], in_=src_rows(t)), xb, None)
                        S.op(dve, lambda h, xb=xb, sb_=sb_: h.scalar_tensor_tensor(
                            out=junk[:], in0=xb[:], scalar=1.0, in1=xb[:], op0=ALU.mult, op1=ALU.mult,
                            accum_out=sb_[:, 0:1]), [xb], [junk, sb_])
                        S.op(dve, lambda h, sb_=sb_: h.tensor_scalar(out=sb_[:, 1:2], in0=sb_[:, 0:1], scalar1=1.0 / D,
                                                                     scalar2=EPS, op0=ALU.mult, op1=ALU.add), [sb_], [sb_])
                        S.op(act, lambda h, sb_=sb_: h.activation(out=sb_[:, 2:3], in_=sb_[:, 1:2], func=AF.Ln), [sb_], [sb_])
                        S.op(act, lambda h, sb_=sb_: h.activation(out=sb_[:, 2:3], in_=sb_[:, 2:3], func=AF.Exp, scale=-0.5),
                             [sb_], [sb_])
                        S.op(dve, lambda h, xb=xb, hb=hb, sb_=sb_: h.scalar_tensor_tensor(
                            out=hb[:], in0=xb[:], scalar=sb_[:, 2:3], in1=gain[:], op0=ALU.mult, op1=ALU.mult),
                            [xb, sb_, gain], [hb])
                        if f32T is not None and t == f32T[0]:
                            S.op(dve, lambda h, xb=xb, sb_=sb_: h.scalar_tensor_tensor(
                                out=hn32[:], in0=xb[:], scalar=sb_[:, 2:3], in1=gain[:], op0=ALU.mult, op1=ALU.mult),
                                [xb, sb_, gain], [hn32])
                            for kc4 in range(0, KC, 4):
                                for k in range(4):
                                    kc = kc4 + k
                                    S.op(pe, lambda h, kc=kc, k=k: h.transpose(
                                        out=ps32[:, k * 128:(k + 1) * 128], in_=hn32[:, kc * 128:(kc + 1) * 128],
                                        identity=ident_f[:]), [hn32, ident_f], [ps32], signal=(k == 3))
                                S.op(act, lambda h, kc4=kc4: h.copy(out=hT32[:, kc4:kc4 + 4, :],
                                                                   in_=ps32[:].rearrange("p (k t) -> p k t", k=4)),
                                     [ps32], [hT32])
                            S.dma(act, lambda h: h.dma_start(out=f32T[1], in_=hT32[:]), None, hT32)
                        if dst_tok is not None:
                            S.dma(act, lambda h, hb=hb, t=t: h.dma_start(out=dst_tok(t), in_=hb[:]), None, hb)
                        if dstT is not None:
                            for kc8 in range(0, KC, 8):
                                pb = pst[(kc8 // 8) % 2]
                                for k in range(8):
                                    kc = kc8 + k
                                    S.op(pe, lambda h, pb=pb, hb=hb, kc=kc, k=k: h.transpose(
                                        out=pb[:, k * 128:(k + 1) * 128], in_=hb[:, kc * 128:(kc + 1) * 128],
                                        identity=ident_b[:]), [hb, ident_b], [pb], signal=(k == 7))
                                eng = act if (kc8 // 8) % 2 == 0 else dve
                                if eng is act:
                                    S.op(act, lambda h, pb=pb, gb=gb, kc8=kc8, ti=ti: h.copy(
                                        out=gb[:, kc8:kc8 + 8, ti * 128:(ti + 1) * 128],
                                        in_=pb[:].rearrange("p (k t) -> p k t", k=8)), [pb], [gb])
                                else:
                                    S.op(dve, lambda h, pb=pb, gb=gb, kc8=kc8, ti=ti: h.tensor_copy(
                                        out=gb[:, kc8:kc8 + 8, ti * 128:(ti + 1) * 128],
                                        in_=pb[:].rearrange("p (k t) -> p k t", k=8)), [pb], [gb])
                        it += 1
                    if dstT is not None:
                        nt = len(g) * 128
                        S.dma(act, lambda h, gb=gb, g=g, nt=nt: h.dma_start(out=dstT(g), in_=gb[:, :, 0:nt]), None, gb)
                S.barrier(release=allb)

        if want("norm1"):
            norm_stage("n1", lambda t: x_loc[t * 128:(t + 1) * 128, :], list(range(LT)), norm_mix_g,
                       dstT=lambda g: hnT[g[0] // 4], f32T=(16, hnT32))

        def gemm_stage(tag, groups, xT_of, kcx, T, castbuf, blocks_of, extra_bufs=()):
            with ExitStack() as st:
                xT = [P.sb(st, tag + "_xT%d" % i, [128, kcx, T], BF16) for i in range(2)]
                wb = [P.sb(st, tag + "_wb%d" % i, [128, kcx, 512], BF16) for i in range(2)]
                psb = [P.ps(st, tag + "_ps%d" % i, [128, 512], F32) for i in range(6)]
                pi = 0
                wi = 0
                Tmax = T
                for gi, g in enumerate(groups):
                    if isinstance(g, tuple):
                        g, T = g
                    else:
                        T = Tmax
                    xb = xT[gi % 2]
                    S.dma(sp, lambda h, xb=xb, g=g, T=T: h.dma_start(out=xb[:, :, 0:T], in_=xT_of(g)), xb, None)
                    for blk in blocks_of(g):
                        w_dram, c0, ncols, mode, epi, fin = blk
                        wbb = wb[wi % 2]
                        wi += 1
                        src = w_dram[:, c0:c0 + ncols].rearrange("(kc p) n -> p kc n", p=128)
                        S.dma(sp, lambda h, wbb=wbb, src=src, ncols=ncols: h.dma_start(
                            out=wbb[:, :, 0:ncols], in_=src), wbb, castbuf)
                        if mode == "B":
                            nj = (ncols + 127) // 128
                            for j in range(nj):
                                mcols = min(128, ncols - j * 128)
                                pb = psb[pi % 6]
                                pi += 1
                                for kc in range(kcx):
                                    S.op(pe, lambda h, pb=pb, wbb=wbb, xb=xb, kc=kc, j=j, mcols=mcols, T=T: h.matmul(
                                        pb[0:mcols, 0:T], lhsT=wbb[:, kc, j * 128:j * 128 + mcols], rhs=xb[:, kc, 0:T],
                                        start=(kc == 0), stop=(kc == kcx - 1)), [wbb, xb], [pb], signal=(kc == kcx - 1))
                                epi(pb, j, g, blk)
                        else:
                            for m in range(T // 128):
                                pb = psb[pi % 6]
                                pi += 1
                                for kc in range(kcx):
                                    S.op(pe, lambda h, pb=pb, wbb=wbb, xb=xb, kc=kc, m=m, ncols=ncols: h.matmul(
                                        pb[:, 0:ncols], lhsT=xb[:, kc, m * 128:(m + 1) * 128], rhs=wbb[:, kc, 0:ncols],
                                        start=(kc == 0), stop=(kc == kcx - 1)), [wbb, xb], [pb], signal=(kc == kcx - 1))
                                epi(pb, m, g, blk)
                        if fin is not None:
                            fin(g, blk)
                S.barrier(release=xT + wb + list(extra_bufs))

        evac_rr = [0]

        def evac(out_ap_fn, pb, writes):
            evac_rr[0] += 1
            if evac_rr[0] % 2 == 0:
                S.op(act, lambda h: h.copy(out=out_ap_fn()[0], in_=out_ap_fn()[1]), [pb], writes)
            else:
                S.op(dve, lambda h: h.tensor_copy(out=out_ap_fn()[0], in_=out_ap_fn()[1]), [pb], writes)

        if want("proj"):
            with ExitStack() as st:
                stB = [P.sb(st, "pj_stB%d" % i, [128, 4, 512], BF16) for i in range(2)]
                stF = [P.sb(st, "pj_stF%d" % i, [128, 4, 512], F32) for i in range(2)]
                stL = P.sb(st, "pj_stL", [16, 512], F32)
                rr = {"B": 0, "F": 0}

                def mk_block(c0, ncols, mode, dst, dcol0, dt):
                    state = {}

                    def epi(pb, i, g, blk):
                        if i == 0:
                            key = "F" if dt is F32 else "B"
                            pool_ = stF if dt is F32 else stB
                            state["stg"] = pool_[rr[key] % 2]
                            rr[key] += 1
                        stg = state["stg"]
                        if mode == "B":
                            evac(lambda: (stg[:, i, :], pb[:, 0:512]), pb, [stg])
                        else:
                            evac(lambda: (stg[:, i, 0:ncols], pb[:, 0:ncols]), pb, [stg])

                    def fin(g, blk):
                        stg = state["stg"]
                        if mode == "B":
                            d = dst[dcol0:dcol0 + ncols, g * 512:(g + 1) * 512].rearrange("(j p) t -> p j t", p=128)
                            S.dma(act, lambda h: h.dma_start(out=d, in_=stg[:, 0:ncols // 128, :]), None, stg)
                        else:
                            d = dst[g * 512:(g + 1) * 512, dcol0:dcol0 + ncols].rearrange("(m p) c -> p m c", p=128)
                            S.dma(act, lambda h: h.dma_start(out=d, in_=stg[:, :, 0:ncols]), None, stg)
                    return (win_bf, c0, ncols, mode, epi, fin)

                def glr_block():
                    def epi(pb, i, g, blk):
                        S.op(act, lambda h: h.copy(out=stL[:, :], in_=pb[0:16, 0:512]), [pb], [stL])

                    def fin(g, blk):
                        S.dma(act, lambda h: h.dma_start(out=glrT[:, g * 512:(g + 1) * 512], in_=stL[:, :]), None, stL)
                    return (win_bf, C_GLR, 16, "B", epi, fin)

                kv_blocks = []
                q_blocks = []
                for b in range(2):
                    kv_blocks.append(mk_block(C_GK + b * 512, 512, "B", gkT, b * 512, BF16))
                    q_blocks.append(mk_block(C_GQ + b * 512, 512, "B", gqT, b * 512, BF16))
                kv_blocks.append(glr_block())
                for b in range(4):
                    kv_blocks.append(mk_block(C_GV + b * 512, 512, "A", gv, b * 512, BF16))
                    kv_blocks.append(mk_block(C_DK + b * 512, 512, "A", dk, b * 512, BF16))
                    kv_blocks.append(mk_block(C_DV + b * 512, 512, "A", dv, b * 512, BF16))
                    q_blocks.append(mk_block(C_GR + b * 512, 512, "A", gr, b * 512, F32))
                    q_blocks.append(mk_block(C_DQ + b * 512, 512, "A", dq, b * 512, BF16))

                pj_groups = P.opts.get("pj_groups", list(range(8)))
                gemm_stage("pj", pj_groups, lambda g: hnT[g], KC, 512, CB["win"],
                           lambda g: kv_blocks + (q_blocks if g >= 3 else []),
                           extra_bufs=stB + stF + [stL])

        def A_(fn, r, w):
            S.op(act, fn, r, w)

        def V_(fn, r, w):
            S.op(dve, fn, r, w)

        def G_(fn, r, w):
            S.op(dve, fn, r, w)

        def rstd_act(dst_fn, src_fn, r, w):
            S.op(act, lambda h: h.activation(out=dst_fn(), in_=src_fn(), func=AF.Ln), r, w)
            S.op(act, lambda h: h.activation(out=dst_fn(), in_=dst_fn(), func=AF.Exp, scale=-0.5), w, w)

        def T_(fn, r, w, sig=True):
            S.op(pe, fn, r, w, signal=sig)

        def qg_of_tile(t):
            if t == QT0:
                return 0, 0
            return 1 + (t - 16) // 4, ((t - 16) % 4) * 128

        def transpose_out(src, nfeat_chunks, kc0, t, ptb, mT):
            qg, toff = qg_of_tile(t)
            for c8 in range(0, nfeat_chunks, 8):
                for k in range(8):
                    c = c8 + k
                    T_(lambda h, c=c, k=k: h.transpose(out=ptb[:, k * 128:(k + 1) * 128],
                                                       in_=src[:, c * 128:(c + 1) * 128], identity=ident_b[:]),
                       [src, ident_b], [ptb], sig=(k == 7))
                evac(lambda c8=c8: (mT[:, c8:c8 + 8, :], ptb[:].rearrange("p (k t) -> p k t", k=8)), ptb, [mT])
            S.dma(act, lambda h: h.dma_start(out=mixT[qg][:, kc0:kc0 + nfeat_chunks, toff:toff + 128],
                                              in_=mT[:, 0:nfeat_chunks, :]), None, mT)

        if want("proj32"):
            with ExitStack() as st:
                p32_x = P.sb(st, "p32_x", [128, KC, 128], F32)
                p32_w = [P.sb(st, "p32_w%d" % i, [128, KC, 256], F32) for i in range(2)]
                p32_o = P.sb(st, "p32_o", [128, 16, 128], F32)
                p32_ps = [P.ps(st, "p32_ps%d" % i, [128, 512], F32) for i in range(2)]
                S.dma(sp, lambda h: h.dma_start(out=p32_x[:], in_=hnT32), p32_x, None)
                for blk in range(8):
                    wbb = p32_w[blk % 2]
                    src = w_in[:, blk * 256:(blk + 1) * 256].rearrange("(kc p) n -> p kc n", p=128)
                    S.dma(sp, lambda h, wbb=wbb, src=src: h.dma_start(out=wbb[:], in_=src), wbb, None)
                    for j in range(2):
                        pb = p32_ps[j]
                        for kc in range(KC):
                            T_(lambda h, wbb=wbb, pb=pb, kc=kc, j=j: h.matmul(
                                pb[:, 0:128], lhsT=wbb[:, kc, j * 128:(j + 1) * 128], rhs=p32_x[:, kc, :],
                                start=(kc == 0), stop=(kc == KC - 1)), [wbb, p32_x], [pb], sig=(kc == KC - 1))
                        evac(lambda blk=blk, j=j, pb=pb: (p32_o[:, blk * 2 + j, :], pb[:, 0:128]), pb, [p32_o])
                S.dma(act, lambda h: h.dma_start(out=qk32T, in_=p32_o[:]), None, p32_o)
                S.barrier(release=[p32_x, p32_o] + p32_w)

        if want("cast"):
            cast_w(CB["gu"], wg_bf, w_ffn_gate, D, 512)
            cast_w(CB["gu"], wu_bf, w_ffn_up, D, 512)
            cast_w(CB["wd"], wd_bf, w_ffn_down, DFF, 1376)

        if want("gla"):
            with ExitStack() as st:
                W2 = P.sb(st, "gl_W2", [16, 1024], F32)
                b2 = P.sb(st, "gl_b2", [1, 1024], F32)
                ones1 = P.sb(st, "gl_ones", [1, 128], F32)
                gm = P.sb(st, "gl_gm", [128, 3, 128], F32)
                amask = P.sb(st, "gl_amask", [128, 128], F32)
                gng = P.sb(st, "gl_gng", [128, GDV], F32)
                mhalf = P.sb(st, "gl_mh", [128, 1], F32)
                S.dma(sp, lambda h: h.dma_start(out=W2[:], in_=gla_w_gate2), W2, None)
                S.dma(sp, lambda h: h.dma_start(out=b2[:], in_=gla_b_gate), b2, None)
                S.dma(sp, lambda h: h.dma_start(out=gm[:], in_=c_gmats), gm, None)
                S.dma(sp, lambda h: h.dma_start(out=amask[:], in_=c_attmask), amask, None)
                S.dma(sp, lambda h: h.dma_start(out=gng[:], in_=bcast_row(gla_norm_g, 128)), gng, None)
                G_(lambda h: h.memset(ones1[:], 1.0), [], [ones1])
                G_(lambda h: h.memset(mhalf[:], -0.5), [], [mhalf])
                kTg = [P.sb(st, "gl_kT%d" % i, [128, 8, 512], BF16) for i in range(2)]
                qTg = [P.sb(st, "gl_qT%d" % i, [128, 8, 512], BF16) for i in range(2)]
                vt = [P.sb(st, "gl_v%d" % i, [128, 2048], BF16) for i in range(2)]
                grt = [P.sb(st, "gl_gr%d" % i, [128, 2048], F32) for i in range(2)]
                glr = [P.sb(st, "gl_glr%d" % i, [16, 128], F32) for i in range(2)]
                e_sb_2 = [P.sb(st, "gl_e" + "%d" % i, [128, 1024], F32) for i in range(2)]
                spx_2 = [P.sb(st, "gl_sp" + "%d" % i, [128, 1024], F32) for i in range(2)]
                kes_2 = [P.sb(st, "gl_kes" + "%d" % i, [128, 1024], F32) for i in range(2)]
                eb_2 = [P.sb(st, "gl_eb" + "%d" % i, [128, 8, 128], F32) for i in range(2)]
                e3_2 = [P.sb(st, "gl_e3" + "%d" % i, [128, 8, 128], F32) for i in range(2)]
                e3n_2 = [P.sb(st, "gl_e3n" + "%d" % i, [128, 8, 128], F32) for i in range(2)]
                qsT_2 = [P.sb(st, "gl_qsT" + "%d" % i, [128, 8, 128], BF16) for i in range(2)]
                qdT_2 = [P.sb(st, "gl_qdT" + "%d" % i, [128, 8, 128], BF16) for i in range(2)]
                kdT_2 = [P.sb(st, "gl_kdT" + "%d" % i, [128, 8, 128], BF16) for i in range(2)]
                kend_2 = [P.sb(st, "gl_kend" + "%d" % i, [128, 1024], BF16) for i in range(2)]
                att_sb = P.sb(st, "gl_att", [128, 128], BF16)
                qk32 = P.sb(st, "gl_qk32", [128, 16, 128], F32)
                qd32 = P.sb(st, "gl_qd32", [128, 8, 128], F32)
                kd32 = P.sb(st, "gl_kd32", [128, 8, 128], F32)
                Sst = [P.sb(st, "gl_S%d" % i, [128, 512], F32) for i in range(8)]
                Sbf = [P.sb(st, "gl_Sb%d" % i, [128, 512], BF16) for i in range(8)]
                ysb = P.sb(st, "gl_y", [128, 512], F32)
                junk = P.sb(st, "gl_junk", [128, 512], BF16)
                nst = P.sb(st, "gl_nst", [128, 4], F32)
                sg = P.sb(st, "gl_sg", [128, 2048], F32)
                og = P.sb(st, "gl_og", [128, 2048], BF16)
                mT = P.sb(st, "gl_mT", [128, 16, 128], BF16)
                zb = P.ps(st, "gl_zb", [128, 1024], F32)
                ktp = P.ps(st, "gl_ktp", [128, 1024], BF16)
                attp = P.ps(st, "gl_attp", [128, 512], F32)
                op_ = P.ps(st, "gl_op", [128, 512], F32)
                kvps = [P.ps(st, "gl_kvp%d" % i, [128, 512], F32) for i in range(3)]
                kvc = [0]
                for i in range(8):
                    G_(lambda h, i=i: h.memset(Sst[i][:], 0.0), [], [Sst[i]])
                    G_(lambda h, i=i: h.memset(Sbf[i][:], 0.0), [], [Sbf[i]])
                gkT_v = gkT.rearrange("(c p) s -> p c s", p=128)
                gqT_v = gqT.rearrange("(c p) s -> p c s", p=128)
                gla_tiles = P.opts.get("gla_tiles", list(range(LT)))

                def gla_G(t):
                    g4, ti = t // 4, t % 4
                    isq = t >= QT0
                    kg, qg_ = kTg[g4 % 2], qTg[g4 % 2]
                    vb, grb, lrb = vt[t % 2], grt[t % 2], glr[t % 2]
                    tsl = slice(ti * 128, (ti + 1) * 128)
                    pp = gidx[t] % 2
                    e_sb, sp_, kes, eb, e3, e3n = e_sb_2[pp], spx_2[pp], kes_2[pp], eb_2[pp], e3_2[pp], e3n_2[pp]
                    qsT, qdT, kdT, kend = qsT_2[pp], qdT_2[pp], kdT_2[pp], kend_2[pp]
                    if ti == 0 or t == gla_tiles[0]:
                        S.dma(sp, lambda h: h.dma_start(out=kg[:], in_=gkT_v[:, :, g4 * 512:(g4 + 1) * 512]), kg, None)
                        if g4 >= 3:
                            S.dma(sp, lambda h: h.dma_start(out=qg_[:], in_=gqT_v[:, :, g4 * 512:(g4 + 1) * 512]), qg_, None)
                    S.dma(sp, lambda h: h.dma_start(out=vb[:], in_=gv[t * 128:(t + 1) * 128, :]), vb, None)
                    S.dma(sp, lambda h: h.dma_start(out=lrb[:], in_=glrT[:, t * 128:(t + 1) * 128]), lrb, None)
                    if isq:
                        S.dma(sp, lambda h: h.dma_start(out=grb[:], in_=gr[t * 128:(t + 1) * 128, :]), grb, None)
                    for hf in range(2):
                        cs = slice(hf * 512, (hf + 1) * 512)
                        T_(lambda h, cs=cs: h.matmul(zb[:, cs], lhsT=lrb[:, :], rhs=W2[:, cs], start=True, stop=False),
                           [lrb, W2], [zb], sig=False)
                        T_(lambda h, cs=cs: h.matmul(zb[:, cs], lhsT=ones1[:, :], rhs=b2[:, cs], start=False, stop=True),
                           [ones1, b2], [zb])
                    A_(lambda h: h.activation(out=e_sb[:], in_=zb[:], func=AF.Exp, scale=-1.0), [zb], [e_sb])
                    A_(lambda h: h.activation(out=sp_[:], in_=e_sb[:], func=AF.Ln, bias=1.0, scale=1.0), [e_sb], [sp_])
                    for hf in range(2):
                        cs = slice(hf * 512, (hf + 1) * 512)
                        T_(lambda h, cs=cs: h.matmul(zb[:, cs], lhsT=gm[:, 0, :], rhs=sp_[:, cs], start=True, stop=True),
                           [gm, sp_], [zb])
                    A_(lambda h: h.activation(out=kes[:], in_=zb[:], func=AF.Exp, scale=-1.0 / 16), [zb], [kes])
                    for half in range(2):
                        for k in range(4):
                            dc = half * 4 + k
                            T_(lambda h, dc=dc, k=k: h.matmul(zb[:, k * 256:(k + 1) * 256], lhsT=sp_[:, dc * 128:(dc + 1) * 128],
                                                              rhs=gm[:, 1:3, :], start=True, stop=True),
                               [sp_, gm], [zb], sig=(k == 3))
                        ds = slice(half * 4, half * 4 + 4)
                        pEv = zb[:].rearrange("p (k c) -> p k c", k=4)
                        A_(lambda h, ds=ds, pEv=pEv: h.activation(out=eb[:, ds, :], in_=pEv[:, :, 0:128], func=AF.Exp,
                                                          scale=-1.0 / 16), [zb], [eb])
                        if isq:
                            A_(lambda h, ds=ds, pEv=pEv: h.activation(out=e3[:, ds, :], in_=pEv[:, :, 128:256], func=AF.Exp,
                                                              scale=-1.0 / 16), [zb], [e3])
                            A_(lambda h, ds=ds, pEv=pEv: h.activation(out=e3n[:, ds, :], in_=pEv[:, :, 128:256], func=AF.Exp,
                                                              scale=1.0 / 16), [zb], [e3n])
                    if isq:
                        V_(lambda h: h.scalar_tensor_tensor(out=qsT[:], in0=qg_[:, :, tsl], scalar=1.0 / 16, in1=eb[:],
                                                            op0=ALU.mult, op1=ALU.mult), [qg_, eb], [qsT])
                        V_(lambda h: h.scalar_tensor_tensor(out=qdT[:], in0=qg_[:, :, tsl], scalar=1.0 / 16, in1=e3[:],
                                                            op0=ALU.mult, op1=ALU.mult), [qg_, e3], [qdT])
                        V_(lambda h: h.tensor_tensor(out=kdT[:], in0=kg[:, :, tsl], in1=e3n[:], op=ALU.mult),
                           [kg, e3n], [kdT])
                    hp = (t == 16) and want("proj32")
                    hpflag[t] = hp
                    if hp:
                        S.dma(sp, lambda h: h.dma_start(out=qk32[:], in_=qk32T), qk32, None)
                        V_(lambda h: h.scalar_tensor_tensor(out=qd32[:], in0=qk32[:, 0:8, :], scalar=1.0 / 16, in1=e3[:],
                                                            op0=ALU.mult, op1=ALU.mult), [qk32, e3], [qd32])
                        V_(lambda h: h.tensor_tensor(out=kd32[:], in0=qk32[:, 8:16, :], in1=e3n[:], op=ALU.mult),
                           [qk32, e3n], [kd32])
                    for dc in range(8):
                        T_(lambda h, dc=dc: h.transpose(out=ktp[:, dc * 128:(dc + 1) * 128], in_=kg[:, dc, tsl],
                                                        identity=ident_b[:]), [kg, ident_b], [ktp], sig=(dc == 7))
                    V_(lambda h: h.tensor_tensor(out=kend[:], in0=ktp[:], in1=kes[:], op=ALU.mult), [ktp, kes], [kend])


                def gla_H(t):
                    g4, ti = t // 4, t % 4
                    isq = t >= QT0
                    kg, qg_ = kTg[g4 % 2], qTg[g4 % 2]
                    vb, grb, lrb = vt[t % 2], grt[t % 2], glr[t % 2]
                    tsl = slice(ti * 128, (ti + 1) * 128)
                    pp = gidx[t] % 2
                    e_sb, sp_, kes, eb, e3, e3n = e_sb_2[pp], spx_2[pp], kes_2[pp], eb_2[pp], e3_2[pp], e3n_2[pp]
                    qsT, qdT, kdT, kend = qsT_2[pp], qdT_2[pp], kdT_2[pp], kend_2[pp]
                    hp = hpflag[t]
                    def head(hh):
                        es_ = slice(hh * 512, (hh + 1) * 512)
                        if isq:
                            for i, dc in enumerate((2 * hh, 2 * hh + 1)):
                                if hp:
                                    T_(lambda h, dc=dc, i=i: h.matmul(attp[:, 0:128], lhsT=kd32[:, dc, :], rhs=qd32[:, dc, :],
                                                                      start=(i == 0), stop=(i == 1)),
                                       [kd32, qd32], [attp], sig=(i == 1))
                                else:
                                    T_(lambda h, dc=dc, i=i: h.matmul(attp[:, 0:128], lhsT=kdT[:, dc, :], rhs=qdT[:, dc, :],
                                                                      start=(i == 0), stop=(i == 1)),
                                       [kdT, qdT], [attp], sig=(i == 1))
                            V_(lambda h: h.tensor_tensor(out=att_sb[:], in0=attp[:, 0:128], in1=amask[:], op=ALU.mult),
                               [attp, amask], [att_sb])
                            T_(lambda h: h.matmul(op_[:, :], lhsT=att_sb[:, :], rhs=vb[:, es_], start=True, stop=False),
                               [att_sb, vb], [op_], sig=False)
                        for ch in range(2):
                            ps_ = slice(ch * 64, (ch + 1) * 64)
                            if isq:
                                for i, dc in enumerate((2 * hh, 2 * hh + 1)):
                                    last = (ch == 1 and i == 1)
                                    T_(lambda h, dc=dc, last=last, ps_=ps_: h.matmul(
                                        op_[ps_, :], lhsT=qsT[:, dc, ps_], rhs=Sbf[dc][:, :], start=False, stop=last),
                                       [qsT, Sbf[dc]], [op_], sig=last)
                            if t == gla_tiles[-1] and ch == 1:
                                continue
                            for dc in (2 * hh, 2 * hh + 1):
                                kvp = kvps[kvc[0] % 3]
                                kvc[0] += 1
                                T_(lambda h, dc=dc, ps_=ps_, kvp=kvp: h.matmul(kvp[:, :], lhsT=kend[ps_, dc * 128:(dc + 1) * 128],
                                                                      rhs=vb[ps_, es_], start=True, stop=True),
                                   [kend, vb], [kvp])
                                V_(lambda h, dc=dc, ch=ch, kvp=kvp: h.scalar_tensor_tensor(
                                    out=Sst[dc][:], in0=Sst[dc][:], scalar=eb[:, dc, ch * 64 + 63:ch * 64 + 64],
                                    in1=kvp[:], op0=ALU.mult, op1=ALU.add), [Sst[dc], eb, kvp], [Sst[dc]])
                                A_(lambda h, dc=dc: h.copy(out=Sbf[dc][:], in_=Sst[dc][:]), [Sst[dc]], [Sbf[dc]])
                        if isq:
                            A_(lambda h: h.activation(out=junk[:], in_=op_[:], func=AF.Square, accum_out=nst[:, 0:1]),
                               [op_], [junk, nst])
                            V_(lambda h: h.tensor_scalar(out=nst[:, 1:2], in0=nst[:, 0:1], scalar1=1.0 / GDV, scalar2=EPS,
                                                         op0=ALU.mult, op1=ALU.add), [nst], [nst])
                            A_(lambda h: h.activation(out=nst[:, 2:3], in_=nst[:, 1:2], func=AF.Ln), [nst], [nst])
                            A_(lambda h: h.activation(out=nst[:, 2:3], in_=nst[:, 2:3], func=AF.Exp, scale=-0.5), [nst], [nst])
                            V_(lambda h: h.scalar_tensor_tensor(out=ysb[:], in0=op_[:], scalar=nst[:, 2:3], in1=gng[:],
                                                                op0=ALU.mult, op1=ALU.mult), [op_, nst, gng], [ysb])
                            V_(lambda h: h.tensor_tensor(out=og[:, es_], in0=ysb[:], in1=sg[:, es_], op=ALU.mult),
                               [ysb, sg], [og])
                    if isq:
                        A_(lambda h: h.activation(out=sg[:], in_=grb[:], func=AF.Silu), [grb], [sg])
                    for hh in range(GH):
                        head(hh)
                    if isq:
                        transpose_out(og, 16, 0, t, ktp, mT)

                gidx = {t: i for i, t in enumerate(gla_tiles)}
                hpflag = {}
                gla_G(gla_tiles[0])
                for i_, t in enumerate(gla_tiles):
                    if i_ + 1 < len(gla_tiles):
                        gla_G(gla_tiles[i_ + 1])
                    gla_H(t)
                S.barrier(release=[W2, b2, gm, amask, gng, qk32] + kTg + qTg + vt + grt + glr + [mT])

        DCFG = (1, 4, 16)
        if want("dil"):
            with ExitStack() as st:
                relb = P.sb(st, "dl_relb", [32, DH], F32)
                oneh = P.sb(st, "dl_oneh", [32, 3, 384], F32)
                bneg = P.sb(st, "dl_bneg", [DH, 3, 384], F32)
                antiI = P.sb(st, "dl_antiI", [128, 2, 256], F32)
                flags = P.sb(st, "dl_flags", [128, 2], F32)
                negc = P.sb(st, "dl_negc", [128, 1], F32)
                S.dma(sp, lambda h: h.dma_start(out=relb[:], in_=rel_bias), relb, None)
                S.dma(sp, lambda h: h.dma_start(out=oneh[:], in_=c_onehot), oneh, None)
                S.dma(sp, lambda h: h.dma_start(out=bneg[:], in_=c_bandneg), bneg, None)
                S.dma(sp, lambda h: h.dma_start(out=antiI[:], in_=c_antiI), antiI, None)
                S.dma(sp, lambda h: h.dma_start(out=flags[:], in_=c_flags), flags, None)
                G_(lambda h: h.memset(negc[:], NEG), [], [negc])
                fext_sb = P.sb(st, "dl_fext", [DH, 384], F32)
                Hk = P.sb(st, "dl_Hk", [128, 2, DH, 128], F32)
                tbl = P.sb(st, "dl_tbl", [128, DH, 256], F32)
                Qb = [P.sb(st, "dl_Q%d" % i, [128, 2048], BF16) for i in range(2)]
                Kb = [P.sb(st, "dl_K%d" % i, [128, 2048], BF16) for i in range(2)]
                Vb = [P.sb(st, "dl_V%d" % i, [128, 2048], BF16) for i in range(3)]
                QT = [P.sb(st, "dl_QT%d" % i, [128, DH, 128], BF16) for i in range(2)]
                KT = [P.sb(st, "dl_KT%d" % i, [128, DH, 128], BF16) for i in range(3)]
                Sp = [P.sb(st, "dl_Sp%d" % i, [128, 2, 256], F32) for i in range(2)]
                Pb = [P.sb(st, "dl_P%d" % i, [128, 256], BF16) for i in range(3)]
                PT = [P.sb(st, "dl_PT%d" % i, [128, 4, 256], BF16) for i in range(2)]
                Oall = [P.sb(st, "dl_O%d" % i, [128, 2048], F32) for i in range(2)]
                msb = [P.sb(st, "dl_ms%d" % i, [128, 2, DH], F32) for i in range(2)]
                nmx = [P.sb(st, "dl_nm%d" % i, [128, 2], F32) for i in range(2)]
                trp = [P.ps(st, "dl_trp%d" % i, [128, 1024], BF16) for i in range(2)]
                Sps = [P.ps(st, "dl_Sps%d" % i, [128, 512], F32) for i in range(2)]
                ptp = P.ps(st, "dl_ptp", [128, 1024], BF16)
                ops = [P.ps(st, "dl_ops%d" % i, [128, 512], F32) for i in range(2)]
                fextb = Buf("fext_d")
                ctr = {"u": 0, "v": 0, "k": 0}
                SCALE = DE ** -0.5
                dil_cfgs = P.opts.get("dil_cfgs", [0, 1, 2])

                def build_tables(ci):
                    T_(lambda h: h.matmul(Sps[0][0:DH, 0:384], lhsT=relb[:, :], rhs=oneh[:, ci, :], start=True, stop=True),
                       [relb, oneh], [Sps[0]])
                    V_(lambda h: h.tensor_tensor(out=fext_sb[:], in0=Sps[0][0:DH, 0:384], in1=bneg[:, ci, :], op=ALU.add),
                       [Sps[0], bneg], [fext_sb])
                    S.dma(act, lambda h: h.dma_start(out=fext_d[ci], in_=fext_sb[:]), fextb, fext_sb)
                    for c in range(2):
                        src = bass.AP(fext_d.tensor, ci * DH * 384 + c * 128, [[1, 128], [384, DH], [1, 128]])
                        S.dma(sp, lambda h, c=c, src=src: h.dma_start(out=Hk[:, c, :, :], in_=src), Hk, fextb)
                    for hh in range(DH):
                        pb = Sps[hh % 2]
                        for c in range(2):
                            T_(lambda h, hh=hh, c=c, pb=pb: h.matmul(pb[:, 0:256], lhsT=Hk[:, c, hh, :], rhs=antiI[:, c, :],
                                                                    start=(c == 0), stop=(c == 1)),
                               [Hk, antiI], [pb], sig=(c == 1))
                        evac(lambda hh=hh, pb=pb: (tbl[:, hh, :], pb[:, 0:256]), pb, [tbl])

                def load_rows(buf, src, d, r, b):
                    start = d * 128 * b + r
                    S.dma(sp, lambda h: h.dma_start(out=buf[:], in_=src[start:start + d * 127 + 1:d, :]), buf, None)

                def transpose_all(srcb, dstT):
                    for half in range(2):
                        pb = trp[half]
                        for k in range(8):
                            hh = half * 8 + k
                            T_(lambda h, hh=hh, k=k, pb=pb: h.transpose(out=pb[:, k * 128:(k + 1) * 128],
                                                                        in_=srcb[:, hh * 128:(hh + 1) * 128],
                                                                        identity=ident_b[:]),
                               [srcb, ident_b], [pb], sig=(k == 7))
                        evac(lambda half=half, pb=pb: (dstT[:, half * 8:(half + 1) * 8, :],
                                                       pb[:].rearrange("p (k t) -> p k t", k=8)), pb, [dstT])

                def unit(ci, d, r, b, kt_prev, v_prev, kt_cur, v_cur, cvar):
                    u = ctr["u"]
                    ctr["u"] += 1
                    qb, qt = Qb[u % 2], QT[u % 2]
                    ob, mb = Oall[u % 2], msb[u % 2]
                    load_rows(qb, dq, d, r, b)
                    transpose_all(qb, qt)
                    def pairA(hp):
                        sps = Sps[hp % 2]
                        for i in range(2):
                            hh = hp * 2 + i
                            T_(lambda h, hh=hh, i=i: h.matmul(sps[:, i * 256:i * 256 + 128], lhsT=qt[:, hh, :],
                                                              rhs=kt_prev[:, hh, :], start=True, stop=True),
                               [qt, kt_prev], [sps], sig=False)
                            T_(lambda h, hh=hh, i=i: h.matmul(sps[:, i * 256 + 128:i * 256 + 256], lhsT=qt[:, hh, :],
                                                              rhs=kt_cur[:, hh, :], start=True, stop=True),
                               [qt, kt_cur], [sps], sig=(i == 1))

                    def pairB(hp):
                        sps, spb, nm = Sps[hp % 2], Sp[hp % 2], nmx[hp % 2]
                        V_(lambda h, hp=hp: h.scalar_tensor_tensor(
                            out=spb[:], in0=sps[:].rearrange("p (i k) -> p i k", i=2), scalar=SCALE,
                            in1=tbl[:, 2 * hp:2 * hp + 2, :], op0=ALU.mult, op1=ALU.add), [sps, tbl], [spb])
                        if cvar is not None:
                            V_(lambda h: h.tensor_scalar(out=spb[:, :, 0:128], in0=spb[:, :, 0:128], scalar1=cvar,
                                                         scalar2=None, op0=ALU.add), [spb, flags, negc], [spb])
                        V_(lambda h: h.tensor_reduce(out=nm[:], in_=spb[:], axis=AX.X, op=ALU.max, negate=True),
                           [spb], [nm])
                        V_(lambda h, hp=hp: h.tensor_copy(out=mb[:, 0, 2 * hp:2 * hp + 2], in_=nm[:]), [nm], [mb])
                        for i in range(2):
                            hh = hp * 2 + i
                            pbuf = Pb[hh % 3]
                            A_(lambda h, hh=hh, i=i, pbuf=pbuf: h.activation(
                                out=pbuf[:], in_=spb[:, i, :], func=AF.Exp, bias=nm[:, i:i + 1], scale=1.0,
                                accum_out=mb[:, 1, hh:hh + 1]), [spb, nm], [pbuf, mb])
                            q4 = hh % 4
                            for c in range(2):
                                T_(lambda h, pbuf=pbuf, q4=q4, c=c: h.transpose(
                                    out=ptp[:, q4 * 256 + c * 128:q4 * 256 + (c + 1) * 128],
                                    in_=pbuf[:, c * 128:(c + 1) * 128], identity=ident_b[:]),
                                   [pbuf, ident_b], [ptp], sig=(c == 1))
                        if hp % 2 == 1:
                            g4 = hp // 2
                            ptb, opb = PT[g4 % 2], ops[g4 % 2]
                            evac(lambda ptb=ptb: (ptb[:], ptp[:].rearrange("p (a k) -> p a k", a=4)), ptp, [ptb])
                            for q4 in range(4):
                                hh = g4 * 4 + q4
                                T_(lambda h, hh=hh, q4=q4: h.matmul(opb[:, q4 * 128:(q4 + 1) * 128], lhsT=ptb[:, q4, 0:128],
                                                                    rhs=v_prev[:, hh * 128:(hh + 1) * 128],
                                                                    start=True, stop=False),
                                   [ptb, v_prev], [opb], sig=False)
                                T_(lambda h, hh=hh, q4=q4: h.matmul(opb[:, q4 * 128:(q4 + 1) * 128], lhsT=ptb[:, q4, 128:256],
                                                                    rhs=v_cur[:, hh * 128:(hh + 1) * 128],
                                                                    start=False, stop=True),
                                   [ptb, v_cur], [opb], sig=(q4 == 3))
                            evac(lambda g4=g4, opb=opb: (ob[:, g4 * 512:(g4 + 1) * 512], opb[:]), opb, [ob])
                    pairA(0)
                    for hp_ in range(DH // 2):
                        if hp_ + 1 < DH // 2:
                            pairA(hp_ + 1)
                        pairB(hp_)
                    start = d * 128 * b + r
                    rows = slice(start, start + d * 127 + 1, d)
                    S.dma(act, lambda h: h.dma_start(out=oc[ci][rows, :], in_=ob[:]), None, ob)
                    S.dma(act, lambda h: h.dma_start(out=msc[ci][rows, :, :], in_=mb[:]), None, mb)

                def load_kv(d, r, b):
                    kb = Kb[ctr["k"] % 2]
                    ktb = KT[ctr["k"] % 3]
                    vb = Vb[ctr["k"] % 3]
                    ctr["k"] += 1
                    load_rows(kb, dk, d, r, b)
                    load_rows(vb, dv, d, r, b)
                    transpose_all(kb, ktb)
                    return ktb, vb

                for ci in dil_cfgs:
                    d = DCFG[ci]
                    build_tables(ci)
                    nfirst = 16 // d
                    for r in range(d):
                        if d == 1:
                            qbs = list(range(15, 32))
                        elif d == 4:
                            qbs = list(range(3, 8)) if r >= 2 else list(range(4, 8))
                        else:
                            qbs = [0, 1] if r >= 14 else [1]
                        prev = None
                        for b in qbs:
                            if prev is None and b > 0:
                                prev = load_kv(d, r, b - 1)
                            cur = load_kv(d, r, b)
                            if b == 0:
                                cvar = negc[:, 0:1]
                                pk, pv = cur
                            else:
                                pk, pv = prev
                                cvar = flags[:, 0:1] if (b - 1) < nfirst else None
                            unit(ci, d, r, b, pk, pv, cur[0], cur[1], cvar)
                            prev = cur
                S.barrier(release=[relb, oneh, bneg, antiI, flags, Hk, fextb] + Qb + Kb + Vb + Oall + msb + [fext_sb])

        if want("dilc"):
            with ExitStack() as st:
                O3 = [[P.sb(st, "dc_O%d_%d" % (i, j), [128, 2048], F32) for j in range(3)] for i in range(2)]
                ms3 = [P.sb(st, "dc_ms%d" % i, [128, 3, 2, DH], F32) for i in range(2)]
                mneg = P.sb(st, "dc_mneg", [128, DH], F32)
                w3 = P.sb(st, "dc_w3", [128, 3, DH], F32)
                ws3 = P.sb(st, "dc_ws3", [128, 3, DH], F32)
                den = P.sb(st, "dc_den", [128, DH], F32)
                acc = P.sb(st, "dc_acc", [128, 2048], F32)
                tmp = P.sb(st, "dc_tmp", [128, 2048], F32)
                od = P.sb(st, "dc_od", [128, 2048], BF16)
                mT2 = P.sb(st, "dc_mT", [128, 16, 128], BF16)
                ptb2 = P.ps(st, "dc_ptb", [128, 1024], BF16)
                for it, t in enumerate(P.opts.get("dilc_tiles", list(range(QT0, LT)))):
                    Ob, mb = O3[it % 2], ms3[it % 2]
                    rows = slice(t * 128, (t + 1) * 128)
                    for ci in range(3):
                        S.dma(sp, lambda h, ci=ci, Ob=Ob, rows=rows: h.dma_start(out=Ob[ci][:], in_=oc[ci][rows, :]), Ob[ci], None)
                        S.dma(sp, lambda h, ci=ci, mb=mb, rows=rows: h.dma_start(out=mb[:, ci, :, :], in_=msc[ci][rows, :, :]), mb, None)
                    V_(lambda h, mb=mb: h.tensor_tensor(out=mneg[:], in0=mb[:, 0, 0, :], in1=mb[:, 1, 0, :], op=ALU.min),
                       [mb], [mneg])
                    V_(lambda h, mb=mb: h.tensor_tensor(out=mneg[:], in0=mneg[:], in1=mb[:, 2, 0, :], op=ALU.min),
                       [mb, mneg], [mneg])
                    for ci in range(3):
                        V_(lambda h, ci=ci, mb=mb: h.tensor_tensor(out=w3[:, ci, :], in0=mneg[:], in1=mb[:, ci, 0, :],
                                                                  op=ALU.subtract), [mneg, mb], [w3])
                    A_(lambda h: h.activation(out=w3[:], in_=w3[:], func=AF.Exp), [w3], [w3])
                    V_(lambda h, mb=mb: h.tensor_tensor(out=ws3[:], in0=w3[:], in1=mb[:, :, 1, :], op=ALU.mult),
                       [w3, mb], [ws3])
                    V_(lambda h: h.tensor_tensor(out=den[:], in0=ws3[:, 0, :], in1=ws3[:, 1, :], op=ALU.add), [ws3], [den])
                    V_(lambda h: h.tensor_tensor(out=den[:], in0=den[:], in1=ws3[:, 2, :], op=ALU.add), [ws3, den], [den])
                    V_(lambda h: h.reciprocal(out=den[:], in_=den[:]), [den], [den])
                    for ci in range(3):
                        V_(lambda h, ci=ci: h.tensor_tensor(out=w3[:, ci, :], in0=w3[:, ci, :], in1=den[:], op=ALU.mult),
                           [w3, den], [w3])

                    def bc(ci):
                        return w3[:, ci, :].unsqueeze(2).broadcast_to([128, DH, 128])

                    def v3(b_):
                        return b_[:].rearrange("p (a e) -> p a e", a=DH)
                    V_(lambda h, Ob=Ob: h.tensor_tensor(out=v3(acc), in0=v3(Ob[0]), in1=bc(0), op=ALU.mult),
                       [Ob[0], w3], [acc])
                    G_(lambda h, Ob=Ob: h.tensor_tensor(out=v3(tmp), in0=v3(Ob[1]), in1=bc(1), op=ALU.mult),
                       [Ob[1], w3], [tmp])
                    V_(lambda h: h.tensor_tensor(out=acc[:], in0=acc[:], in1=tmp[:], op=ALU.add), [acc, tmp], [acc])
                    G_(lambda h, Ob=Ob: h.tensor_tensor(out=v3(tmp), in0=v3(Ob[2]), in1=bc(2), op=ALU.mult),
                       [Ob[2], w3], [tmp])
                    V_(lambda h: h.tensor_tensor(out=od[:], in0=acc[:], in1=tmp[:], op=ALU.add), [acc, tmp], [od])
                    transpose_out(od, 16, 16, t, ptb2, mT2)
                S.barrier(release=[mT2] + O3[0] + O3[1] + ms3)

        def qg_rows(qg):
            return (1920, 128) if qg == 0 else (2048 + (qg - 1) * 512, 512)
        QGS = P.opts.get("qgs", [0, 1, 2, 3, 4])
        NORM_Q_GROUPS = [[15]] + [[16 + 4 * i + j for j in range(4)] for i in range(4)]

        def resid_gemm(tag, xT_scr, kcx, w_scr, castbuf, res_src, dst):
            with ExitStack() as st:
                rs = [P.sb(st, tag + "_rs%d" % i, [128, 4, 512], F32) for i in range(2)]
                so = [P.sb(st, tag + "_so%d" % i, [128, 4, 512], F32) for i in range(2)]
                rr = [0]

                def mk(c0):
                    state = {}

                    def epi(pb, m, g, blk):
                        r0, T = qg_rows(g)
                        if m == 0:
                            state["rs"], state["so"] = rs[rr[0] % 2], so[rr[0] % 2]
                            rr[0] += 1
                            rsb = state["rs"]
                            srcv = res_src[r0:r0 + T, c0:c0 + 512].rearrange("(m p) c -> p m c", p=128)
                            S.dma(sp, lambda h: h.dma_start(out=rsb[:, 0:T // 128, :], in_=srcv), rsb, None)
                        rsb, sob = state["rs"], state["so"]
                        V_(lambda h: h.tensor_tensor(out=sob[:, m, :], in0=pb[:, :], in1=rsb[:, m, :], op=ALU.add),
                           [pb, rsb], [sob])

                    def fin(g, blk):
                        r0, T = qg_rows(g)
                        sob = state["so"]
                        dv_ = dst[r0:r0 + T, c0:c0 + 512].rearrange("(m p) c -> p m c", p=128)
                        S.dma(act, lambda h: h.dma_start(out=dv_, in_=sob[:, 0:T // 128, :]), None, sob)
                    return (w_scr, c0, 512, "A", epi, fin)
                blocks = [mk(c0) for c0 in range(0, D, 512)]
                gemm_stage(tag, [(g, qg_rows(g)[1]) for g in QGS], lambda g: xT_scr[g][:, :, 0:qg_rows(g)[1]], kcx, 512,
                           castbuf, lambda g: blocks, extra_bufs=rs + so)

        if want("wout"):
            resid_gemm("wo", mixT, KC, wout_bf, CB["mid"], x_loc, h1)

        if want("norm2"):
            norm_stage("n2", lambda t: h1[t * 128:(t + 1) * 128, :], NORM_Q_GROUPS, norm_xattn_g,
                       dstT=lambda g: hn2T[qg_of_tile(g[0])[0]][:, :, 0:len(g) * 128])
            norm_stage("nm", lambda t: mem[t * 128:(t + 1) * 128, :], [[0, 1]], mem_norm_g,
                       dstT=lambda g: memT[0][:, :, 0:256])

        if want("xattn"):
            with ExitStack() as st:
                xstB = [P.sb(st, "xa_stB%d" % i, [128, 4, 512], BF16) for i in range(2)]
                xrr = [0]

                def mkx(w_scr, mode, dst_fn):
                    state = {}

                    def epi(pb, i, g, blk):
                        T = 256 if g == "mem" else qg_rows(g)[1]
                        if i == 0:
                            state["stg"] = xstB[xrr[0] % 2]
                            xrr[0] += 1
                        stg = state["stg"]
                        if mode == "B":
                            evac(lambda: (stg[:, i, 0:T], pb[:, 0:T]), pb, [stg])
                        else:
                            evac(lambda: (stg[:, i, :], pb[:, 0:512]), pb, [stg])

                    def fin(g, blk):
                        stg = state["stg"]
                        T = 256 if g == "mem" else qg_rows(g)[1]
                        if mode == "B":
                            S.dma(act, lambda h: h.dma_start(out=dst_fn(g), in_=stg[:, :, 0:T]), None, stg)
                        else:
                            S.dma(act, lambda h: h.dma_start(out=dst_fn(g), in_=stg[:, 0:T // 128, :]), None, stg)
                    return (w_scr, 0, 512, mode, epi, fin)
                gemm_stage("xkv", [("mem", 256)], lambda g: memT[0][:, :, 0:256], KC, 512, CB["mid"],
                           lambda g: [mkx(wxk_bf, "B", lambda g: kxT[:, :, :]),
                                      mkx(wxv_bf, "A", lambda g: vx[:, :, :])], extra_bufs=[])
                gemm_stage("xq", [(g, qg_rows(g)[1]) for g in QGS], lambda g: hn2T[g][:, :, 0:qg_rows(g)[1]], KC, 512,
                           CB["mid"], lambda g: [mkx(wxq_bf, "B", lambda g: qxT[g][:, :, 0:qg_rows(g)[1]])],
                           extra_bufs=xstB)
            with ExitStack() as st:
                kx = P.sb(st, "xa_kx", [128, 4, 256], BF16)
                vxs = P.sb(st, "xa_vx", [128, 2, 512], BF16)
                S.dma(sp, lambda h: h.dma_start(out=kx[:], in_=kxT[:, :, :]), kx, None)
                S.dma(sp, lambda h: h.dma_start(out=vxs[:], in_=vx[:, :, :]), vxs, None)
                qx = [P.sb(st, "xa_qx%d" % i, [128, 4, 512], BF16) for i in range(2)]
                xPb = [P.sb(st, "xa_P%d" % i, [128, 256], BF16) for i in range(2)]
                xPTb = [P.sb(st, "xa_PT%d" % i, [128, 256], BF16) for i in range(2)]
                st4 = [P.sb(st, "xa_st%d" % i, [128, 4], F32) for i in range(2)]
                oxb = [P.sb(st, "xa_ox%d" % i, [128, 512], BF16) for i in range(2)]
                oxT_sb = [P.sb(st, "xa_oxT%d" % i, [128, 4, 512], BF16) for i in range(2)]
                xSps = [P.ps(st, "xa_Sps%d" % i, [128, 512], F32) for i in range(2)]
                xptp = [P.ps(st, "xa_ptp%d" % i, [128, 1024], BF16) for i in range(2)]
                xops = [P.ps(st, "xa_ops%d" % i, [128, 512], F32) for i in range(2)]
                xSC = DE ** -0.5
                xcnt = [0]

                def xtile(gi, g, m):
                    qb, oT = qx[gi % 2], oxT_sb[gi % 2]
                    ob = oxb[xcnt[0] % 2]
                    tsl = slice(m * 128, (m + 1) * 128)
                    for hh in range(4):
                        u = xcnt[0] * 4 + hh
                        sps, pb_, ptb_, stt, ptp_, ops_ = xSps[u % 2], xPb[u % 2], xPTb[u % 2], st4[u % 2], xptp[u % 2], xops[u % 2]

                        def one(hh=hh, sps=sps, pb_=pb_, ptb_=ptb_, stt=stt, ptp_=ptp_, ops_=ops_):
                            T_(lambda h: h.matmul(sps[:, 0:256], lhsT=qb[:, hh, tsl], rhs=kx[:, hh, :], start=True, stop=True),
                               [qb, kx], [sps])
                            V_(lambda h: h.tensor_reduce(out=stt[:, 0:1], in_=sps[:, 0:256], axis=AX.X, op=ALU.max,
                                                         negate=True), [sps], [stt])
                            V_(lambda h: h.tensor_scalar(out=stt[:, 1:2], in0=stt[:, 0:1], scalar1=xSC, scalar2=None,
                                                         op0=ALU.mult), [stt], [stt])
                            A_(lambda h: h.activation(out=pb_[:], in_=sps[:, 0:256], func=AF.Exp, bias=stt[:, 1:2], scale=xSC,
                                                      accum_out=stt[:, 2:3]), [sps, stt], [pb_, stt])
                            V_(lambda h: h.reciprocal(out=stt[:, 3:4], in_=stt[:, 2:3]), [stt], [stt])
                            for c in range(2):
                                T_(lambda h, c=c: h.transpose(out=ptp_[:, c * 128:(c + 1) * 128],
                                                              in_=pb_[:, c * 128:(c + 1) * 128], identity=ident_b[:]),
                                   [pb_, ident_b], [ptp_], sig=(c == 1))
                            evac(lambda: (ptb_[:], ptp_[:, 0:256]), ptp_, [ptb_])
                            for c in range(2):
                                T_(lambda h, c=c: h.matmul(ops_[:, 0:128], lhsT=ptb_[:, c * 128:(c + 1) * 128],
                                                           rhs=vxs[:, c, hh * 128:(hh + 1) * 128], start=(c == 0), stop=(c == 1)),
                                   [ptb_, vxs], [ops_], sig=(c == 1))
                            V_(lambda h: h.tensor_scalar(out=ob[:, hh * 128:(hh + 1) * 128], in0=ops_[:, 0:128],
                                                         scalar1=stt[:, 3:4], scalar2=None, op0=ALU.mult), [ops_, stt], [ob])
                        one()
                    pt2 = xptp[xcnt[0] % 2]
                    for k in range(4):
                        T_(lambda h, k=k: h.transpose(out=pt2[:, k * 128:(k + 1) * 128], in_=ob[:, k * 128:(k + 1) * 128],
                                                      identity=ident_b[:]), [ob, ident_b], [pt2], sig=(k == 3))
                    evac(lambda: (oT[:, :, tsl], pt2[:, 0:512].rearrange("p (k t) -> p k t", k=4)), pt2, [oT])
                    xcnt[0] += 1

                for gi, g in enumerate(QGS):
                    r0, T = qg_rows(g)
                    qb, oT = qx[gi % 2], oxT_sb[gi % 2]
                    S.dma(sp, lambda h, qb=qb, g=g, T=T: h.dma_start(out=qb[:, :, 0:T], in_=qxT[g][:, :, 0:T]), qb, None)
                    for m in range(T // 128):
                        xtile(gi, g, m)
                    S.dma(act, lambda h, oT=oT, g=g, T=T: h.dma_start(out=oxT[g][:, :, 0:T], in_=oT[:, :, 0:T]), None, oT)
                S.barrier(release=[kx, vxs] + qx + oxT_sb)
            resid_gemm("xo", oxT, 4, wxo_bf, CB["mid"], h1, h2)

        if want("norm3"):
            norm_stage("n3", lambda t: h2[t * 128:(t + 1) * 128, :], NORM_Q_GROUPS, norm_ffn_g,
                       dstT=lambda g: hn3T[qg_of_tile(g[0])[0]][:, :, 0:len(g) * 128])

        if want("ffn"):
            with ExitStack() as st:
                f_cwrow = P.sb(st, "f_cwrow", [FC, 4, 128], F32)
                f_cw = P.sb(st, "f_cw", [128, 4, FC], F32)
                f_flags = P.sb(st, "f_flags", [128, 2], F32)
                f_gprev = P.sb(st, "f_gprev", [128, FC, 2], F32)
                f_xT = P.sb(st, "f_xT", [128, KC, 512], BF16)
                f_xh = P.sb(st, "f_xh", [128, KC, 128], BF16)
                f_act = P.sb(st, "f_act", [128, 43, 512], BF16)
                f_wb = [P.sb(st, "f_wb%d" % i, [128, KC, 256], BF16) for i in range(3)]
                f_wd = [P.sb(st, "f_wd%d" % i, [128, 8, 512], BF16) for i in range(4)]
                f_gx = [P.sb(st, "f_gx%d" % i, [128, 514], F32) for i in range(2)]
                f_acc = [P.sb(st, "f_acc%d" % i, [128, 512], F32) for i in range(2)]
                f_sg = [P.sb(st, "f_sg%d" % i, [128, 2, 512], F32) for i in range(2)]
                f_res = [P.sb(st, "f_res%d" % i, [128, 512], F32) for i in range(4)]
                f_so = [P.sb(st, "f_so%d" % i, [128, 512], F32) for i in range(4)]
                f_ps = [P.ps(st, "f_ps%d" % i, [128, 512], F32) for i in range(3)]
                f_psh = P.ps(st, "f_psh", [128, 512], F32)
                f_psd = [P.ps(st, "f_psd%d" % i, [128, 512], F32) for i in range(4)]
                S.dma(sp, lambda h: h.dma_start(out=f_cwrow[:, 0:3, :], in_=ffn_conv_w.rearrange("i (j p) -> j i p", p=128)),
                      f_cwrow, None)
                S.dma(sp, lambda h: h.dma_start(out=f_cwrow[:, 3:4, :], in_=ffn_conv_b.rearrange("o (j p) -> j o p", p=128)),
                      f_cwrow, None)
                S.dma(sp, lambda h: h.dma_start(out=f_flags[:], in_=c_flags), f_flags, None)
                for i in range(4):
                    T_(lambda h, i=i: h.transpose(out=f_ps[0][:, i * 128:i * 128 + FC], in_=f_cwrow[:, i, :],
                                                  identity=ident_f[0:FC, 0:FC]), [f_cwrow, ident_f], [f_ps[0]], sig=(i == 3))
                V_(lambda h: h.tensor_copy(out=f_cw[:], in_=f_ps[0][:].rearrange("p (i j) -> p i j", i=4)[:, :, 0:FC]),
                   [f_ps[0]], [f_cw])
                G_(lambda h: h.memset(f_gprev[:], 0.0), [], [f_gprev])
                fc = {"w": 0, "p": 0, "c": 0, "d": 0, "e": 0}
                ffn_tgs = P.opts.get("ffn_tgs", [1, 2, 3, 4])
                h3b = {tg: Buf("h3b%d" % tg) for tg in ffn_tgs}

                def load_wblk(w_scr, c0, ncols):
                    wbb = f_wb[fc["w"] % 3]
                    fc["w"] += 1
                    src = w_scr[:, c0:c0 + ncols].rearrange("(kc p) n -> p kc n", p=128)
                    S.dma(sp, lambda h: h.dma_start(out=wbb[:, :, 0:ncols], in_=src), wbb, CB["gu"])
                    return wbb

                def mm_chunk(wbb, c):
                    pb = f_ps[fc["p"] % 3]
                    fc["p"] += 1
                    for kc in range(KC):
                        T_(lambda h, kc=kc: h.matmul(pb[:, :], lhsT=wbb[:, kc, c * 128:(c + 1) * 128], rhs=f_xT[:, kc, :],
                                                     start=(kc == 0), stop=(kc == KC - 1)), [wbb, f_xT], [pb], sig=(kc == KC - 1))
                    return pb

                def gate_chunk(tg, wbb, c, j, sgb):
                    pb = mm_chunk(wbb, c)
                    if tg == 1:
                        for kc in range(KC):
                            T_(lambda h, kc=kc: h.matmul(f_psh[:, 0:2], lhsT=wbb[:, kc, c * 128:(c + 1) * 128], rhs=f_xh[:, kc, 126:128],
                                                         start=(kc == 0), stop=(kc == KC - 1)), [wbb, f_xh], [f_psh],
                               sig=(kc == KC - 1))
                        V_(lambda h: h.tensor_scalar(out=f_gprev[:, j, :], in0=f_psh[:, 0:2], scalar1=f_flags[:, 1:2],
                                                     scalar2=None, op0=ALU.mult), [f_psh, f_flags], [f_gprev])
                    gx, acc = f_gx[fc["c"] % 2], f_acc[fc["c"] % 2]
                    fc["c"] += 1
                    A_(lambda h: h.copy(out=gx[:, 2:514], in_=pb[:, :]), [pb], [gx])
                    G_(lambda h: h.tensor_copy(out=gx[:, 0:2], in_=f_gprev[:, j, :]), [f_gprev], [gx])
                    G_(lambda h: h.tensor_copy(out=f_gprev[:, j, :], in_=gx[:, 512:514]), [gx], [f_gprev])
                    V_(lambda h: h.tensor_scalar(out=acc[:], in0=gx[:, 0:512], scalar1=f_cw[:, 0, j:j + 1],
                                                 scalar2=f_cw[:, 3, j:j + 1], op0=ALU.mult, op1=ALU.add), [gx, f_cw], [acc])
                    V_(lambda h: h.scalar_tensor_tensor(out=acc[:], in0=gx[:, 1:513], scalar=f_cw[:, 1, j:j + 1], in1=acc[:],
                                                        op0=ALU.mult, op1=ALU.add), [gx, f_cw, acc], [acc])
                    V_(lambda h: h.scalar_tensor_tensor(out=acc[:], in0=gx[:, 2:514], scalar=f_cw[:, 2, j:j + 1], in1=acc[:],
                                                        op0=ALU.mult, op1=ALU.add), [gx, f_cw, acc], [acc])
                    A_(lambda h: h.activation(out=sgb[:, c, :], in_=acc[:], func=AF.Silu), [acc], [sgb])

                def up_chunk(wbb, c, jl, sgb):
                    pb = mm_chunk(wbb, c)
                    V_(lambda h: h.tensor_tensor(out=f_act[:, jl, :], in0=pb[:, :], in1=sgb[:, c, :], op=ALU.mult),
                       [pb, sgb], [f_act])

                def down_phase(tg, half):
                    r0, T = qg_rows(tg)
                    j0 = half * 43
                    src_h = h2 if half == 0 else h3
                    for nb in range(8):
                        cs = slice(nb * 512, (nb + 1) * 512)
                        psd = f_psd if nb % 2 == 0 else [f_ps[0], f_ps[1], f_ps[2], f_psh]
                        for jg in range(0, 43, 8):
                            nj = min(8, 43 - jg)
                            wdp = f_wd[fc["d"] % 4]
                            fc["d"] += 1
                            rows = slice((j0 + jg) * 128, (j0 + jg + nj) * 128)
                            srcw = wd_bf[rows, cs].rearrange("(jj p) c -> p jj c", p=128)
                            S.dma(sp, lambda h, wdp=wdp, srcw=srcw, nj=nj: h.dma_start(out=wdp[:, 0:nj, :], in_=srcw),
                                  wdp, CB["wd"])
                            for jj in range(nj):
                                jl = jg + jj
                                for m in range(4):
                                    T_(lambda h, wdp=wdp, jj=jj, jl=jl, m=m, nj=nj, psd=psd: h.matmul(
                                        psd[m][:, :], lhsT=f_act[:, jl, m * 128:(m + 1) * 128], rhs=wdp[:, jj, :],
                                        start=(jl == 0), stop=(jl == 42)), [f_act, wdp], [psd[m]], sig=(jl == 42 or (jj == nj - 1 and m == 3)))
                        for m in range(4):
                            rr_ = slice(r0 + m * 128, r0 + (m + 1) * 128)
                            S.dma(sp, lambda h, m=m, rr_=rr_, cs=cs: h.dma_start(out=f_res[m][:], in_=src_h[rr_, cs]), f_res[m],
                                  h3b[tg] if half == 1 else None)
                        for m in range(4):
                            rb, sob = f_res[m], f_so[m]
                            rr_ = slice(r0 + m * 128, r0 + (m + 1) * 128)
                            V_(lambda h, rb=rb, sob=sob, m=m, psd=psd: h.tensor_tensor(out=sob[:], in0=psd[m][:, :], in1=rb[:], op=ALU.add),
                               [psd[m], rb], [sob])
                            S.dma(act, lambda h, sob=sob, rr_=rr_, cs=cs: h.dma_start(out=h3[rr_, cs], in_=sob[:]), h3b[tg], sob)

                for tg in ffn_tgs:
                    S.dma(sp, lambda h, tg=tg: h.dma_start(out=f_xT[:], in_=hn3T[tg]), f_xT, None)
                    if tg == 1:
                        S.dma(sp, lambda h: h.dma_start(out=f_xh[:], in_=hn3T[0][:, :, 0:128]), f_xh, None)
                    for half in range(2):
                        j0 = half * 43
                        for b0 in range(0, 43, 2):
                            nch = min(2, 43 - b0)
                            c0 = (j0 + b0) * 128
                            sgb = f_sg[(b0 // 2) % 2]
                            wbb = load_wblk(wg_bf, c0, nch * 128)
                            for c in range(nch):
                                gate_chunk(tg, wbb, c, j0 + b0 + c, sgb)
                            wbb = load_wblk(wu_bf, c0, nch * 128)
                            for c in range(nch):
                                up_chunk(wbb, c, b0 + c, sgb)
                        down_phase(tg, half)
                S.barrier(release=[f_cwrow, f_flags, f_xT, f_xh] + f_wb + f_wd + f_res + f_so + list(h3b.values()))

        if want("final"):
            norm_stage("nf", lambda t: h3[t * 128:(t + 1) * 128, :], [[t] for t in P.opts.get("final_tiles", list(range(16, 32)))],
                       final_norm_g, dst_tok=lambda t: out[(t - 16) * 128:(t - 15) * 128, :])

        S.barrier()
        with nc.Block() as block:
            S.emit(block)
    return nc, P, S


def host_consts():
    c = {}
    c["c_ident"] = np.eye(128, dtype=np.float32)
    j = np.arange(128)[:, None]
    cc = np.arange(128)[None, :]
    same = (j // 64) == (cc // 64)
    tri = (same & (j <= cc)).astype(np.float32)
    m2 = (same & (j > cc)).astype(np.float32)
    mid = (same & ((j % 64) <= 32)).astype(np.float32)
    m3 = tri - mid
    c["c_gmats"] = np.ascontiguousarray(np.stack([m2, tri, m3], axis=1)).astype(np.float32)
    c["c_attmask"] = tri.copy()
    kk = np.arange(256)[:, None]
    kj = np.arange(256)[None, :]
    J = (kk == 255 - kj).astype(np.float32)
    c["c_antiI"] = np.ascontiguousarray(J.reshape(2, 128, 256).transpose(1, 0, 2))
    onehot = np.zeros((32, 3, 384), np.float32)
    bandneg = np.zeros((DH, 3, 384), np.float32)
    for ci, d in enumerate((1, 4, 16)):
        for i in range(384):
            rel = i - 127
            if 0 <= rel <= 128:
                dist = rel * d
                if dist < 16:
                    b = dist
                else:
                    b = 16 + int(np.float32(np.log(np.float32(max(dist, 1)) / np.float32(16)) /
                                            np.float32(np.log(2048 / 16)) * np.float32(16)))
                    b = min(b, 31)
                onehot[b, ci, i] = 1.0
            else:
                bandneg[:, ci, i] = NEG
    c["c_onehot"] = onehot
    c["c_bandneg"] = bandneg
    return c


def make_in_maps(inputs):
    consts = host_consts()
    x = np.asarray(inputs["x"], dtype=np.float32)
    maps = []
    for c in range(8):
        b, half = c // 2, c % 2
        m = {}
        if half == 0:
            xl = np.zeros((SEQ, D), np.float32)
            xl[2048:] = x[b, :2048]
        else:
            xl = np.ascontiguousarray(x[b])
        m["x_loc"] = xl
        m["mem"] = np.ascontiguousarray(inputs["mem"][b], dtype=np.float32)
        m["rel_bias"] = np.ascontiguousarray(inputs["rel_bias"], dtype=np.float32)
        for k in ("norm_mix_g", "gla_b_gate", "gla_norm_g", "norm_xattn_g", "mem_norm_g", "norm_ffn_g", "ffn_conv_b"):
            m[k] = np.ascontiguousarray(np.asarray(inputs[k], dtype=np.float32).reshape(1, -1))
        m["final_norm_g"] = np.ascontiguousarray(np.asarray(inputs["final_norm_g"], dtype=np.float32).reshape(1, -1))
        for k in ("w_in", "gla_w_gate2", "w_out", "w_xq", "w_xk", "w_xv", "w_xo", "w_ffn_gate", "w_ffn_up",
                  "ffn_conv_w", "w_ffn_down"):
            m[k] = np.ascontiguousarray(np.asarray(inputs[k], dtype=np.float32)[0])
        m.update(consts)
        fl = np.zeros((128, 2), np.float32)
        fl[:, 0] = NEG if half == 0 else 0.0
        fl[:, 1] = 0.0 if half == 0 else 1.0
        m["c_flags"] = fl
        maps.append(m)
    return maps


def kernel(**inputs):
    nc, P, S = build()
    maps = make_in_maps(inputs)
    res = run_bass_kernel_spmd(nc, maps, core_ids=list(range(8)))
    outp = np.empty((4, SEQ, D), np.float32)
    for c in range(8):
        b, half = c // 2, c % 2
        outp[b, half * 2048:(half + 1) * 2048] = res.results[c]["out"]
    return outp
```

```python
import numpy as np
from contextlib import ExitStack
import concourse.bass as bass
import concourse.mybir as mybir
from concourse.bass_utils import run_bass_kernel_spmd

F32 = mybir.dt.float32
BF16 = mybir.dt.bfloat16
AF = mybir.ActivationFunctionType
ALU = mybir.AluOpType
AX = mybir.AxisListType

D = 4096
SEQ = 4096
LT = 32
QT0 = 15
NQT = LT - QT0
KC = D // 128
GH, GDK, GDV = 4, 256, 512
DH, DE = 16, 128
DFF = 11008
FC = DFF // 128
XW = 512
MEM = 256
NCOL = 12304
C_GQ, C_GK, C_GV, C_GLR, C_GR, C_DQ, C_DK, C_DV = 0, 1024, 2048, 4096, 4112, 6160, 8208, 10256
EPS = 1e-6
NEG = -30000.0


class Buf:
    def __init__(self, name, t=None):
        self.name = name
        self.t = t
        self.lw = None
        self.rd = []
        self.sem = None
        self.cnt = 0

    def __getitem__(self, k):
        return self.t[k]


class Eng:
    def __init__(self, name, same_raw):
        self.name = name
        self.ops = []
        self.cnt = 0
        self.waited = {}
        self.sem = None
        self.same_raw = same_raw


class Sched:
    def __init__(self, nc, es):
        self.nc = nc
        self.es = es
        self.pe = Eng("pe", False)
        self.act = Eng("act", True)
        self.dve = Eng("dve", True)
        self.pool = Eng("pool", True)
        self.sp = Eng("sp", False)
        self.engs = [self.pe, self.act, self.dve, self.pool, self.sp]
        for e in self.engs[:4]:
            e.sem = es.enter_context(nc.semaphore("sem_" + e.name))
        self.dsems = []
        self.free_dsems = []
        self.nsem = 0

    def dma_sem(self, barrier=True):
        if barrier and self.free_dsems:
            return self.free_dsems.pop()
        s = self.es.enter_context(self.nc.semaphore("dsem%d" % self.nsem))
        self.nsem += 1
        rec = [s, 0]
        if barrier:
            self.dsems.append(rec)
        return rec

    def _waits(self, eng, deps):
        ws = []
        for d in deps:
            if d is None:
                continue
            sem, val = d
            if sem is eng.sem and not eng.same_raw:
                continue
            k = id(sem)
            if eng.waited.get(k, 0) < val:
                eng.waited[k] = val
                ws.append((sem, val))
        return ws

    def op(self, eng, fn, reads=(), writes=(), signal=True):
        deps = []
        for b in reads:
            deps.append(b.lw)
        for b in writes:
            if b.lw is not None and b.lw[0] is not eng.sem:
                deps.append(b.lw)
            for r in b.rd:
                if r[0] is not eng.sem:
                    deps.append(r)
        ws = self._waits(eng, deps)
        val = eng.cnt + 1
        if signal:
            eng.cnt = val
            eng.ops.append((ws, fn, (eng.sem, 1)))
        else:
            eng.ops.append((ws, fn, None))
        for b in reads:
            b.rd.append((eng.sem, val))
        for b in writes:
            b.lw = (eng.sem, val)
            b.rd = []

    def dma(self, eng, fn, out, in_):
        deps = []
        if in_ is not None:
            deps.append(in_.lw)
        if out is not None:
            deps.append(out.lw)
            deps.extend(out.rd)
        ws = self._waits(eng, deps)
        tgt = out if out is not None else in_
        if tgt.sem is None:
            tgt.sem = self.dma_sem()
        tgt.sem[1] += 16
        key = (tgt.sem[0], tgt.sem[1])
        eng.ops.append((ws, fn, (tgt.sem[0], 16)))
        if out is not None:
            out.lw = key
            out.rd = []
        if in_ is not None:
            in_.rd.append(key)

    def barrier(self, release=()):
        for e in self.engs:
            deps = []
            for o in self.engs[:4]:
                if o is not e and o.cnt > 0:
                    deps.append((o.sem, o.cnt))
            for rec in self.dsems:
                if rec[1] > 0:
                    deps.append((rec[0], rec[1]))
            ws = self._waits(e, deps)
            if ws:
                e.ops.append((ws, None, None))
        for b in release:
            if b.sem is not None:
                self.free_dsems.append(b.sem)
                b.sem = None

    def emit(self, block):
        def run(eng, h):
            for ws, fn, inc in eng.ops:
                for sem, val in ws:
                    h.wait_ge(sem, val)
                if fn is not None:
                    ins = fn(h)
                    if inc is not None:
                        ins.then_inc(inc[0], inc[1])

        @block.tensor
        def _(h):
            run(self.pe, h)

        @block.scalar
        def _(h):
            run(self.act, h)

        @block.vector
        def _(h):
            run(self.dve, h)

        @block.gpsimd
        def _(h):
            run(self.pool, h)

        @block.sync
        def _(h):
            run(self.sp, h)


class Prog:
    def __init__(self, debug_outs=(), stages=None):
        self.nc = bass.Bass("TRN2", target_bir_lowering=False)
        self.es = ExitStack()
        self.debug_outs = set(debug_outs)
        self.stages = stages
        self.ins = {}
        self.scr = {}
        self.npsum = 0

    def inp(self, name, shape, dt=F32):
        t = self.nc.dram_tensor(name, list(shape), dt, kind="ExternalInput")
        self.ins[name] = t
        return t.ap()

    def scratch(self, name, shape, dt):
        kind = "ExternalOutput" if name in self.debug_outs else "Internal"
        t = self.nc.dram_tensor(name, list(shape), dt, kind=kind)
        self.scr[name] = t
        return t.ap()

    def sb(self, st, name, shape, dt):
        t = st.enter_context(self.nc.sbuf_tensor(name, list(shape), dt))
        return Buf(name, t)

    def ps(self, st, name, shape, dt):
        t = st.enter_context(self.nc.psum_tensor(name, list(shape), dt))
        return Buf(name, t)


def bcast_row(ap_row, nparts):
    return bass.AP(ap_row.tensor, ap_row.offset, [[0, nparts]] + [list(x) for x in ap_row.ap[1:]])


def build(debug_outs=(), stages=None, opts=None):
    P = Prog(debug_outs, stages)
    P.opts = opts or {}
    nc = P.nc
    es = P.es
    S = Sched(nc, es)
    pe, act, dve, pool, sp = S.pe, S.act, S.dve, S.pool, S.sp

    def want(name):
        return stages is None or name in stages

    x_loc = P.inp("x_loc", [SEQ, D])
    mem = P.inp("mem", [MEM, D])
    rel_bias = P.inp("rel_bias", [32, DH])
    norm_mix_g = P.inp("norm_mix_g", [1, D])
    w_in = P.inp("w_in", [D, NCOL])
    gla_w_gate2 = P.inp("gla_w_gate2", [16, 1024])
    gla_b_gate = P.inp("gla_b_gate", [1, 1024])
    gla_norm_g = P.inp("gla_norm_g", [1, GDV])
    w_out = P.inp("w_out", [D, D])
    norm_xattn_g = P.inp("norm_xattn_g", [1, D])
    mem_norm_g = P.inp("mem_norm_g", [1, D])
    w_xq = P.inp("w_xq", [D, XW])
    w_xk = P.inp("w_xk", [D, XW])
    w_xv = P.inp("w_xv", [D, XW])
    w_xo = P.inp("w_xo", [XW, D])
    norm_ffn_g = P.inp("norm_ffn_g", [1, D])
    w_ffn_gate = P.inp("w_ffn_gate", [D, DFF])
    w_ffn_up = P.inp("w_ffn_up", [D, DFF])
    ffn_conv_w = P.inp("ffn_conv_w", [3, DFF])
    ffn_conv_b = P.inp("ffn_conv_b", [1, DFF])
    w_ffn_down = P.inp("w_ffn_down", [DFF, D])
    final_norm_g = P.inp("final_norm_g", [1, D])
    c_ident = P.inp("c_ident", [128, 128])
    c_gmats = P.inp("c_gmats", [128, 3, 128])
    c_attmask = P.inp("c_attmask", [128, 128])
    c_antiI = P.inp("c_antiI", [128, 2, 256])
    c_onehot = P.inp("c_onehot", [32, 3, 384])
    c_bandneg = P.inp("c_bandneg", [DH, 3, 384])
    c_flags = P.inp("c_flags", [128, 2])
    out = nc.dram_tensor("out", [2048, D], F32, kind="ExternalOutput").ap()

    win_bf = P.scratch("win_bf", [D, NCOL], BF16)
    wout_bf = P.scratch("wout_bf", [D, D], BF16)
    wxq_bf = P.scratch("wxq_bf", [D, XW], BF16)
    wxk_bf = P.scratch("wxk_bf", [D, XW], BF16)
    wxv_bf = P.scratch("wxv_bf", [D, XW], BF16)
    wxo_bf = P.scratch("wxo_bf", [XW, D], BF16)
    wg_bf = P.scratch("wg_bf", [D, DFF], BF16)
    wu_bf = P.scratch("wu_bf", [D, DFF], BF16)
    wd_bf = P.scratch("wd_bf", [DFF, D], BF16)
    hnT = P.scratch("hnT", [8, 128, KC, 512], BF16)
    gqT = P.scratch("gqT", [1024, SEQ], BF16)
    gkT = P.scratch("gkT", [1024, SEQ], BF16)
    gv = P.scratch("gv", [SEQ, 2048], BF16)
    glrT = P.scratch("glrT", [16, SEQ], F32)
    gr = P.scratch("gr", [SEQ, 2048], F32)
    dq = P.scratch("dq", [SEQ, 2048], BF16)
    dk = P.scratch("dk", [SEQ, 2048], BF16)
    dv = P.scratch("dv", [SEQ, 2048], BF16)
    fext_d = P.scratch("fext_d", [3, DH, 384], F32)
    oc = [P.scratch("oc%d" % i, [SEQ, 2048], F32) for i in range(3)]
    msc = [P.scratch("msc%d" % i, [SEQ, 2, DH], F32) for i in range(3)]
    h1 = P.scratch("h1", [SEQ, D], F32)
    h2 = P.scratch("h2", [SEQ, D], F32)
    h3 = P.scratch("h3", [SEQ, D], F32)
    hn2T = P.scratch("hn2T", [5, 128, KC, 512], BF16)
    hn3T = P.scratch("hn3T", [5, 128, KC, 512], BF16)
    memT = P.scratch("memT", [1, 128, KC, 512], BF16)
    kxT = P.scratch("kxT", [128, 4, 256], BF16)
    vx = P.scratch("vx", [128, 2, 512], BF16)
    qxT = P.scratch("qxT", [5, 128, 4, 512], BF16)
    oxT = P.scratch("oxT", [5, 128, 4, 512], BF16)
    hnT32 = P.scratch("hnT32", [128, KC, 128], F32)
    qk32T = P.scratch("qk32T", [128, 16, 128], F32)
    mixT = P.scratch("mixT", [5, 128, KC, 512], BF16)

    with es:
        ident_f = P.sb(es, "ident_f", [128, 128], F32)
        ident_b = P.sb(es, "ident_b", [128, 128], BF16)
        S.dma(sp, lambda h: h.dma_start(out=ident_f[:], in_=c_ident), ident_f, None)
        S.op(dve, lambda h: h.tensor_copy(out=ident_b[:], in_=ident_f[:]), [ident_f], [ident_b])

        def cast_w(cb, dst, src, rows, rstep):
            for r0 in range(0, rows, rstep):
                r1 = min(rows, r0 + rstep)
                S.dma(pool, lambda h, r0=r0, r1=r1: h.dma_start(out=dst[r0:r1, :], in_=src[r0:r1, :],
                                                                max_dma_last_dim=8192), cb, None)
        CB = {}
        for nm in ("win", "mid", "gu", "wd"):
            CB[nm] = Buf("cast_" + nm)
            CB[nm].sem = S.dma_sem(barrier=False)
        if want("cast"):
            cast_w(CB["win"], win_bf, w_in, D, 512)
            cast_w(CB["mid"], wout_bf, w_out, D, 1024)
            cast_w(CB["mid"], wxq_bf, w_xq, D, 4096)
            cast_w(CB["mid"], wxk_bf, w_xk, D, 4096)
            cast_w(CB["mid"], wxv_bf, w_xv, D, 4096)
            cast_w(CB["mid"], wxo_bf, w_xo, XW, 512)

        def norm_stage(tag, src_rows, tiles, gain_row, dstT=None, dst_tok=None, dst_tok_dt=F32, f32T=None):
            with ExitStack() as st:
                gain = P.sb(st, tag + "_gain", [128, D], F32)
                S.dma(sp, lambda h: h.dma_start(out=gain[:], in_=bcast_row(gain_row, 128)), gain, None)
                xt = [P.sb(st, tag + "_x%d" % i, [128, D], F32) for i in range(2)]
                junk = P.sb(st, tag + "_junk", [128, D], BF16)
                hn = [P.sb(st, tag + "_hn%d" % i, [128, D], BF16 if dstT is not None else dst_tok_dt) for i in range(2)]
                ss = [P.sb(st, tag + "_ss%d" % i, [128, 4], F32) for i in range(2)]
                mhalf = P.sb(st, tag + "_mh", [128, 1], F32)
                S.op(dve, lambda h: h.memset(mhalf[:], -0.5), [], [mhalf])
                ng = 4
                grp = [P.sb(st, tag + "_grp%d" % i, [128, KC, 512], BF16) for i in range(2)] if dstT is not None else None
                pst = [P.ps(st, tag + "_pt%d" % i, [128, 1024], BF16) for i in range(2)] if dstT is not None else None
                allb = [gain, junk, mhalf] + xt + hn + ss + (grp or [])
                if f32T is not None:
                    hn32 = P.sb(st, tag + "_hn32", [128, D], F32)
                    hT32 = P.sb(st, tag + "_hT32", [128, KC, 128], F32)
                    ps32 = P.ps(st, tag + "_ps32", [128, 512], F32)
                    allb += [hn32, hT32]
                groups = tiles if (len(tiles) > 0 and isinstance(tiles[0], list)) else \
                    [tiles[i:i + ng] for i in range(0, len(tiles), ng)]
                it = 0
                for gi, g in enumerate(groups):
                    gb = grp[gi % 2] if grp else None
                    for ti, t in enumerate(g):
                        xb, hb, sb_ = xt[it % 2], hn[it % 2], ss[it % 2]
                        S.dma(sp, lambda h, xb=xb, t=t: h.dma_start(out=xb[:], in_=src_rows(t)), xb, None)
                        S.op(dve, lambda h, xb=xb, sb_=sb_: h.scalar_tensor_tensor(
                            out=junk[:], in0=xb[:], scalar=1.0, in1=xb[:], op0=ALU.mult, op1=ALU.mult,
                            accum_out=sb_[:, 0:1]), [xb], [junk, sb_])
                        S.op(dve, lambda h, sb_=sb_: h.tensor_scalar(out=sb_[:, 1:2], in0=sb_[:, 0:1], scalar1=1.0 / D,
                                                                     scalar2=EPS, op0=ALU.mult, op1=ALU.add), [sb_], [sb_])
                        S.op(act, lambda h, sb_=sb_: h.activation(out=sb_[:, 2:3], in_=sb_[:, 1:2], func=AF.Ln), [sb_], [sb_])
                        S.op(act, lambda h, sb_=sb_: h.activation(out=sb_[:, 2:3], in_=sb_[:, 2:3], func=AF.Exp, scale=-0.5),
                             [sb_], [sb_])
                        S.op(dve, lambda h, xb=xb, hb=hb, sb_=sb_: h.scalar_tensor_tensor(
                            out=hb[:], in0=xb[:], scalar=sb_[:, 2:3], in1=gain[:], op0=ALU.mult, op1=ALU.mult),
                            [xb, sb_, gain], [hb])
                        if f32T is not None and t == f32T[0]:
                            S.op(dve, lambda h, xb=xb, sb_=sb_: h.scalar_tensor_tensor(
                                out=hn32[:], in0=xb[:], scalar=sb_[:, 2:3], in1=gain[:], op0=ALU.mult, op1=ALU.mult),
                                [xb, sb_, gain], [hn32])
                            for kc4 in range(0, KC, 4):
                                for k in range(4):
                                    kc = kc4 + k
                                    S.op(pe, lambda h, kc=kc, k=k: h.transpose(
                                        out=ps32[:, k * 128:(k + 1) * 128], in_=hn32[:, kc * 128:(kc + 1) * 128],
                                        identity=ident_f[:]), [hn32, ident_f], [ps32], signal=(k == 3))
                                S.op(act, lambda h, kc4=kc4: h.copy(out=hT32[:, kc4:kc4 + 4, :],
                                                                   in_=ps32[:].rearrange("p (k t) -> p k t", k=4)),
                                     [ps32], [hT32])
                            S.dma(act, lambda h: h.dma_start(out=f32T[1], in_=hT32[:]), None, hT32)
                        if dst_tok is not None:
                            S.dma(act, lambda h, hb=hb, t=t: h.dma_start(out=dst_tok(t), in_=hb[:]), None, hb)
                        if dstT is not None:
                            for kc8 in range(0, KC, 8):
                                pb = pst[(kc8 // 8) % 2]
                                for k in range(8):
                                    kc = kc8 + k
                                    S.op(pe, lambda h, pb=pb, hb=hb, kc=kc, k=k: h.transpose(
                                        out=pb[:, k * 128:(k + 1) * 128], in_=hb[:, kc * 128:(kc + 1) * 128],
                                        identity=ident_b[:]), [hb, ident_b], [pb], signal=(k == 7))
                                eng = act if (kc8 // 8) % 2 == 0 else dve
                                if eng is act:
                                    S.op(act, lambda h, pb=pb, gb=gb, kc8=kc8, ti=ti: h.copy(
                                        out=gb[:, kc8:kc8 + 8, ti * 128:(ti + 1) * 128],
                                        in_=pb[:].rearrange("p (k t) -> p k t", k=8)), [pb], [gb])
                                else:
                                    S.op(dve, lambda h, pb=pb, gb=gb, kc8=kc8, ti=ti: h.tensor_copy(
                                        out=gb[:, kc8:kc8 + 8, ti * 128:(ti + 1) * 128],
                                        in_=pb[:].rearrange("p (k t) -> p k t", k=8)), [pb], [gb])
                        it += 1
                    if dstT is not None:
                        nt = len(g) * 128
                        S.dma(act, lambda h, gb=gb, g=g, nt=nt: h.dma_start(out=dstT(g), in_=gb[:, :, 0:nt]), None, gb)
                S.barrier(release=allb)

        if want("norm1"):
            norm_stage("n1", lambda t: x_loc[t * 128:(t + 1) * 128, :], list(range(LT)), norm_mix_g,
                       dstT=lambda g: hnT[g[0] // 4], f32T=(16, hnT32))

        def gemm_stage(tag, groups, xT_of, kcx, T, castbuf, blocks_of, extra_bufs=()):
            with ExitStack() as st:
                xT = [P.sb(st, tag + "_xT%d" % i, [128, kcx, T], BF16) for i in range(2)]
                wb = [P.sb(st, tag + "_wb%d" % i, [128, kcx, 512], BF16) for i in range(2)]
                psb = [P.ps(st, tag + "_ps%d" % i, [128, 512], F32) for i in range(6)]
                pi = 0
                wi = 0
                Tmax = T
                for gi, g in enumerate(groups):
                    if isinstance(g, tuple):
                        g, T = g
                    else:
                        T = Tmax
                    xb = xT[gi % 2]
                    S.dma(sp, lambda h, xb=xb, g=g, T=T: h.dma_start(out=xb[:, :, 0:T], in_=xT_of(g)), xb, None)
                    for blk in blocks_of(g):
                        w_dram, c0, ncols, mode, epi, fin = blk
                        wbb = wb[wi % 2]
                        wi += 1
                        src = w_dram[:, c0:c0 + ncols].rearrange("(kc p) n -> p kc n", p=128)
                        S.dma(sp, lambda h, wbb=wbb, src=src, ncols=ncols: h.dma_start(
                            out=wbb[:, :, 0:ncols], in_=src), wbb, castbuf)
                        if mode == "B":
                            nj = (ncols + 127) // 128
                            for j in range(nj):
                                mcols = min(128, ncols - j * 128)
                                pb = psb[pi % 6]
                                pi += 1
                                for kc in range(kcx):
                                    S.op(pe, lambda h, pb=pb, wbb=wbb, xb=xb, kc=kc, j=j, mcols=mcols, T=T: h.matmul(
                                        pb[0:mcols, 0:T], lhsT=wbb[:, kc, j * 128:j * 128 + mcols], rhs=xb[:, kc, 0:T],
                                        start=(kc == 0), stop=(kc == kcx - 1)), [wbb, xb], [pb], signal=(kc == kcx - 1))
                                epi(pb, j, g, blk)
                        else:
                            for m in range(T // 128):
                                pb = psb[pi % 6]
                                pi += 1
                                for kc in range(kcx):
                                    S.op(pe, lambda h, pb=pb, wbb=wbb, xb=xb, kc=kc, m=m, ncols=ncols: h.matmul(
                                        pb[:, 0:ncols], lhsT=xb[:, kc, m * 128:(m + 1) * 128], rhs=wbb[:, kc, 0:ncols],
                                        start=(kc == 0), stop=(kc == kcx - 1)), [wbb, xb], [pb], signal=(kc == kcx - 1))
                                epi(pb, m, g, blk)
                        if fin is not None:
                            fin(g, blk)
                S.barrier(release=xT + wb + list(extra_bufs))

        evac_rr = [0]

        def evac(out_ap_fn, pb, writes):
            evac_rr[0] += 1
            if evac_rr[0] % 2 == 0:
                S.op(act, lambda h: h.copy(out=out_ap_fn()[0], in_=out_ap_fn()[1]), [pb], writes)
            else:
                S.op(dve, lambda h: h.tensor_copy(out=out_ap_fn()[0], in_=out_ap_fn()[1]), [pb], writes)

        if want("proj"):
            with ExitStack() as st:
                stB = [P.sb(st, "pj_stB%d" % i, [128, 4, 512], BF16) for i in range(2)]
                stF = [P.sb(st, "pj_stF%d" % i, [128, 4, 512], F32) for i in range(2)]
                stL = P.sb(st, "pj_stL", [16, 512], F32)
                rr = {"B": 0, "F": 0}

                def mk_block(c0, ncols, mode, dst, dcol0, dt):
                    state = {}

                    def epi(pb, i, g, blk):
                        if i == 0:
                            key = "F" if dt is F32 else "B"
                            pool_ = stF if dt is F32 else stB
                            state["stg"] = pool_[rr[key] % 2]
                            rr[key] += 1
                        stg = state["stg"]
                        if mode == "B":
                            evac(lambda: (stg[:, i, :], pb[:, 0:512]), pb, [stg])
                        else:
                            evac(lambda: (stg[:, i, 0:ncols], pb[:, 0:ncols]), pb, [stg])

                    def fin(g, blk):
                        stg = state["stg"]
                        if mode == "B":
                            d = dst[dcol0:dcol0 + ncols, g * 512:(g + 1) * 512].rearrange("(j p) t -> p j t", p=128)
                            S.dma(act, lambda h: h.dma_start(out=d, in_=stg[:, 0:ncols // 128, :]), None, stg)
                        else:
                            d = dst[g * 512:(g + 1) * 512, dcol0:dcol0 + ncols].rearrange("(m p) c -> p m c", p=128)
                            S.dma(act, lambda h: h.dma_start(out=d, in_=stg[:, :, 0:ncols]), None, stg)
                    return (win_bf, c0, ncols, mode, epi, fin)

                def glr_block():
                    def epi(pb, i, g, blk):
                        S.op(act, lambda h: h.copy(out=stL[:, :], in_=pb[0:16, 0:512]), [pb], [stL])

                    def fin(g, blk):
                        S.dma(act, lambda h: h.dma_start(out=glrT[:, g * 512:(g + 1) * 512], in_=stL[:, :]), None, stL)
                    return (win_bf, C_GLR, 16, "B", epi, fin)

                kv_blocks = []
                q_blocks = []
                for b in range(2):
                    kv_blocks.append(mk_block(C_GK + b * 512, 512, "B", gkT, b * 512, BF16))
                    q_blocks.append(mk_block(C_GQ + b * 512, 512, "B", gqT, b * 512, BF16))
                kv_blocks.append(glr_block())
                for b in range(4):
                    kv_blocks.append(mk_block(C_GV + b * 512, 512, "A", gv, b * 512, BF16))
                    kv_blocks.append(mk_block(C_DK + b * 512, 512, "A", dk, b * 512, BF16))
                    kv_blocks.append(mk_block(C_DV + b * 512, 512, "A", dv, b * 512, BF16))
                    q_blocks.append(mk_block(C_GR + b * 512, 512, "A", gr, b * 512, F32))
                    q_blocks.append(mk_block(C_DQ + b * 512, 512, "A", dq, b * 512, BF16))

                pj_groups = P.opts.get("pj_groups", list(range(8)))
                gemm_stage("pj", pj_groups, lambda g: hnT[g], KC, 512, CB["win"],
                           lambda g: kv_blocks + (q_blocks if g >= 3 else []),
                           extra_bufs=stB + stF + [stL])

        def A_(fn, r, w):
            S.op(act, fn, r, w)

        def V_(fn, r, w):
            S.op(dve, fn, r, w)

        def G_(fn, r, w):
            S.op(dve, fn, r, w)

        def rstd_act(dst_fn, src_fn, r, w):
            S.op(act, lambda h: h.activation(out=dst_fn(), in_=src_fn(), func=AF.Ln), r, w)
            S.op(act, lambda h: h.activation(out=dst_fn(), in_=dst_fn(), func=AF.Exp, scale=-0.5), w, w)

        def T_(fn, r, w, sig=True):
            S.op(pe, fn, r, w, signal=sig)

        def qg_of_tile(t):
            if t == QT0:
                return 0, 0
            return 1 + (t - 16) // 4, ((t - 16) % 4) * 128

        def transpose_out(src, nfeat_chunks, kc0, t, ptb, mT):
            qg, toff = qg_of_tile(t)
            for c8 in range(0, nfeat_chunks, 8):
                for k in range(8):
                    c = c8 + k
                    T_(lambda h, c=c, k=k: h.transpose(out=ptb[:, k * 128:(k + 1) * 128],
                                                       in_=src[:, c * 128:(c + 1) * 128], identity=ident_b[:]),
                       [src, ident_b], [ptb], sig=(k == 7))
                evac(lambda c8=c8: (mT[:, c8:c8 + 8, :], ptb[:].rearrange("p (k t) -> p k t", k=8)), ptb, [mT])
            S.dma(act, lambda h: h.dma_start(out=mixT[qg][:, kc0:kc0 + nfeat_chunks, toff:toff + 128],
                                              in_=mT[:, 0:nfeat_chunks, :]), None, mT)

        if want("proj32"):
            with ExitStack() as st:
                p32_x = P.sb(st, "p32_x", [128, KC, 128], F32)
                p32_w = [P.sb(st, "p32_w%d" % i, [128, KC, 256], F32) for i in range(2)]
                p32_o = P.sb(st, "p32_o", [128, 16, 128], F32)
                p32_ps = [P.ps(st, "p32_ps%d" % i, [128, 512], F32) for i in range(2)]
                S.dma(sp, lambda h: h.dma_start(out=p32_x[:], in_=hnT32), p32_x, None)
                for blk in range(8):
                    wbb = p32_w[blk % 2]
                    src = w_in[:, blk * 256:(blk + 1) * 256].rearrange("(kc p) n -> p kc n", p=128)
                    S.dma(sp, lambda h, wbb=wbb, src=src: h.dma_start(out=wbb[:], in_=src), wbb, None)
                    for j in range(2):
                        pb = p32_ps[j]
                        for kc in range(KC):
                            T_(lambda h, wbb=wbb, pb=pb, kc=kc, j=j: h.matmul(
                                pb[:, 0:128], lhsT=wbb[:, kc, j * 128:(j + 1) * 128], rhs=p32_x[:, kc, :],
                                start=(kc == 0), stop=(kc == KC - 1)), [wbb, p32_x], [pb], sig=(kc == KC - 1))
                        evac(lambda blk=blk, j=j, pb=pb: (p32_o[:, blk * 2 + j, :], pb[:, 0:128]), pb, [p32_o])
                S.dma(act, lambda h: h.dma_start(out=qk32T, in_=p32_o[:]), None, p32_o)
                S.barrier(release=[p32_x, p32_o] + p32_w)

        if want("cast"):
            cast_w(CB["gu"], wg_bf, w_ffn_gate, D, 512)
            cast_w(CB["gu"], wu_bf, w_ffn_up, D, 512)
            cast_w(CB["wd"], wd_bf, w_ffn_down, DFF, 1376)

        if want("gla"):
            with ExitStack() as st:
                W2 = P.sb(st, "gl_W2", [16, 1024], F32)
                b2 = P.sb(st, "gl_b2", [1, 1024], F32)
                ones1 = P.sb(st, "gl_ones", [1, 128], F32)
                gm = P.sb(st, "gl_gm", [128, 3, 128], F32)
                amask = P.sb(st, "gl_amask", [128, 128], F32)
                gng = P.sb(st, "gl_gng", [128, GDV], F32)
                mhalf = P.sb(st, "gl_mh", [128, 1], F32)
                S.dma(sp, lambda h: h.dma_start(out=W2[:], in_=gla_w_gate2), W2, None)
                S.dma(sp, lambda h: h.dma_start(out=b2[:], in_=gla_b_gate), b2, None)
                S.dma(sp, lambda h: h.dma_start(out=gm[:], in_=c_gmats), gm, None)
                S.dma(sp, lambda h: h.dma_start(out=amask[:], in_=c_attmask), amask, None)
                S.dma(sp, lambda h: h.dma_start(out=gng[:], in_=bcast_row(gla_norm_g, 128)), gng, None)
                G_(lambda h: h.memset(ones1[:], 1.0), [], [ones1])
                G_(lambda h: h.memset(mhalf[:], -0.5), [], [mhalf])
                kTg = [P.sb(st, "gl_kT%d" % i, [128, 8, 512], BF16) for i in range(2)]
                qTg = [P.sb(st, "gl_qT%d" % i, [128, 8, 512], BF16) for i in range(2)]
                vt = [P.sb(st, "gl_v%d" % i, [128, 2048], BF16) for i in range(2)]
                grt = [P.sb(st, "gl_gr%d" % i, [128, 2048], F32) for i in range(2)]
                glr = [P.sb(st, "gl_glr%d" % i, [16, 128], F32) for i in range(2)]
                e_sb = P.sb(st, "gl_e", [128, 1024], F32)
                sp_ = P.sb(st, "gl_sp", [128, 1024], F32)
                kes = P.sb(st, "gl_kes", [128, 1024], F32)
                eb = P.sb(st, "gl_eb", [128, 8, 128], F32)
                e3 = P.sb(st, "gl_e3", [128, 8, 128], F32)
                e3n = P.sb(st, "gl_e3n", [128, 8, 128], F32)
                qsT = P.sb(st, "gl_qsT", [128, 8, 128], BF16)
                qdT = P.sb(st, "gl_qdT", [128, 8, 128], BF16)
                kdT = P.sb(st, "gl_kdT", [128, 8, 128], BF16)
                kend = P.sb(st, "gl_kend", [128, 1024], BF16)
                att_sb = P.sb(st, "gl_att", [128, 128], BF16)
                qk32 = P.sb(st, "gl_qk32", [128, 16, 128], F32)
                qd32 = P.sb(st, "gl_qd32", [128, 8, 128], F32)
                kd32 = P.sb(st, "gl_kd32", [128, 8, 128], F32)
                Sst = [P.sb(st, "gl_S%d" % i, [128, 512], F32) for i in range(8)]
                Sbf = [P.sb(st, "gl_Sb%d" % i, [128, 512], BF16) for i in range(8)]
                ysb = P.sb(st, "gl_y", [128, 512], F32)
                junk = P.sb(st, "gl_junk", [128, 512], BF16)
                nst = P.sb(st, "gl_nst", [128, 4], F32)
                sg = P.sb(st, "gl_sg", [128, 2048], F32)
                og = P.sb(st, "gl_og", [128, 2048], BF16)
                mT = P.sb(st, "gl_mT", [128, 16, 128], BF16)
                zb = P.ps(st, "gl_zb", [128, 1024], F32)
                ktp = P.ps(st, "gl_ktp", [128, 1024], BF16)
                attp = P.ps(st, "gl_attp", [128, 512], F32)
                op_ = P.ps(st, "gl_op", [128, 512], F32)
                kvps = [P.ps(st, "gl_kvp%d" % i, [128, 512], F32) for i in range(3)]
                kvc = [0]
                for i in range(8):
                    G_(lambda h, i=i: h.memset(Sst[i][:], 0.0), [], [Sst[i]])
                    G_(lambda h, i=i: h.memset(Sbf[i][:], 0.0), [], [Sbf[i]])
                gkT_v = gkT.rearrange("(c p) s -> p c s", p=128)
                gqT_v = gqT.rearrange("(c p) s -> p c s", p=128)
                gla_tiles = P.opts.get("gla_tiles", list(range(LT)))

                def gla_tile(t):
                    g4, ti = t // 4, t % 4
                    isq = t >= QT0
                    kg, qg_ = kTg[g4 % 2], qTg[g4 % 2]
                    vb, grb, lrb = vt[t % 2], grt[t % 2], glr[t % 2]
                    tsl = slice(ti * 128, (ti + 1) * 128)
                    if ti == 0 or t == gla_tiles[0]:
                        S.dma(sp, lambda h: h.dma_start(out=kg[:], in_=gkT_v[:, :, g4 * 512:(g4 + 1) * 512]), kg, None)
                        if g4 >= 3:
                            S.dma(sp, lambda h: h.dma_start(out=qg_[:], in_=gqT_v[:, :, g4 * 512:(g4 + 1) * 512]), qg_, None)
                    S.dma(sp, lambda h: h.dma_start(out=vb[:], in_=gv[t * 128:(t + 1) * 128, :]), vb, None)
                    S.dma(sp, lambda h: h.dma_start(out=lrb[:], in_=glrT[:, t * 128:(t + 1) * 128]), lrb, None)
                    if isq:
                        S.dma(sp, lambda h: h.dma_start(out=grb[:], in_=gr[t * 128:(t + 1) * 128, :]), grb, None)
                    for hf in range(2):
                        cs = slice(hf * 512, (hf + 1) * 512)
                        T_(lambda h, cs=cs: h.matmul(zb[:, cs], lhsT=lrb[:, :], rhs=W2[:, cs], start=True, stop=False),
                           [lrb, W2], [zb], sig=False)
                        T_(lambda h, cs=cs: h.matmul(zb[:, cs], lhsT=ones1[:, :], rhs=b2[:, cs], start=False, stop=True),
                           [ones1, b2], [zb])
                    A_(lambda h: h.activation(out=e_sb[:], in_=zb[:], func=AF.Exp, scale=-1.0), [zb], [e_sb])
                    A_(lambda h: h.activation(out=sp_[:], in_=e_sb[:], func=AF.Ln, bias=1.0, scale=1.0), [e_sb], [sp_])
                    for hf in range(2):
                        cs = slice(hf * 512, (hf + 1) * 512)
                        T_(lambda h, cs=cs: h.matmul(zb[:, cs], lhsT=gm[:, 0, :], rhs=sp_[:, cs], start=True, stop=True),
                           [gm, sp_], [zb])
                    A_(lambda h: h.activation(out=kes[:], in_=zb[:], func=AF.Exp, scale=-1.0 / 16), [zb], [kes])
                    for half in range(2):
                        for k in range(4):
                            dc = half * 4 + k
                            T_(lambda h, dc=dc, k=k: h.matmul(zb[:, k * 256:(k + 1) * 256], lhsT=sp_[:, dc * 128:(dc + 1) * 128],
                                                              rhs=gm[:, 1:3, :], start=True, stop=True),
                               [sp_, gm], [zb], sig=(k == 3))
                        ds = slice(half * 4, half * 4 + 4)
                        pEv = zb[:].rearrange("p (k c) -> p k c", k=4)
                        A_(lambda h, ds=ds, pEv=pEv: h.activation(out=eb[:, ds, :], in_=pEv[:, :, 0:128], func=AF.Exp,
                                                          scale=-1.0 / 16), [zb], [eb])
                        if isq:
                            A_(lambda h, ds=ds, pEv=pEv: h.activation(out=e3[:, ds, :], in_=pEv[:, :, 128:256], func=AF.Exp,
                                                              scale=-1.0 / 16), [zb], [e3])
                            A_(lambda h, ds=ds, pEv=pEv: h.activation(out=e3n[:, ds, :], in_=pEv[:, :, 128:256], func=AF.Exp,
                                                              scale=1.0 / 16), [zb], [e3n])
                    if isq:
                        V_(lambda h: h.scalar_tensor_tensor(out=qsT[:], in0=qg_[:, :, tsl], scalar=1.0 / 16, in1=eb[:],
                                                            op0=ALU.mult, op1=ALU.mult), [qg_, eb], [qsT])
                        V_(lambda h: h.scalar_tensor_tensor(out=qdT[:], in0=qg_[:, :, tsl], scalar=1.0 / 16, in1=e3[:],
                                                            op0=ALU.mult, op1=ALU.mult), [qg_, e3], [qdT])
                        V_(lambda h: h.tensor_tensor(out=kdT[:], in0=kg[:, :, tsl], in1=e3n[:], op=ALU.mult),
                           [kg, e3n], [kdT])
                    hp = (t == 16) and want("proj32")
                    if hp:
                        S.dma(sp, lambda h: h.dma_start(out=qk32[:], in_=qk32T), qk32, None)
                        V_(lambda h: h.scalar_tensor_tensor(out=qd32[:], in0=qk32[:, 0:8, :], scalar=1.0 / 16, in1=e3[:],
                                                            op0=ALU.mult, op1=ALU.mult), [qk32, e3], [qd32])
                        V_(lambda h: h.tensor_tensor(out=kd32[:], in0=qk32[:, 8:16, :], in1=e3n[:], op=ALU.mult),
                           [qk32, e3n], [kd32])
                    for dc in range(8):
                        T_(lambda h, dc=dc: h.transpose(out=ktp[:, dc * 128:(dc + 1) * 128], in_=kg[:, dc, tsl],
                                                        identity=ident_b[:]), [kg, ident_b], [ktp], sig=(dc == 7))
                    V_(lambda h: h.tensor_tensor(out=kend[:], in0=ktp[:], in1=kes[:], op=ALU.mult), [ktp, kes], [kend])

                    def head(hh):
                        es_ = slice(hh * 512, (hh + 1) * 512)
                        if isq:
                            for i, dc in enumerate((2 * hh, 2 * hh + 1)):
                                if hp:
                                    T_(lambda h, dc=dc, i=i: h.matmul(attp[:, 0:128], lhsT=kd32[:, dc, :], rhs=qd32[:, dc, :],
                                                                      start=(i == 0), stop=(i == 1)),
                                       [kd32, qd32], [attp], sig=(i == 1))
                                else:
                                    T_(lambda h, dc=dc, i=i: h.matmul(attp[:, 0:128], lhsT=kdT[:, dc, :], rhs=qdT[:, dc, :],
                                                                      start=(i == 0), stop=(i == 1)),
                                       [kdT, qdT], [attp], sig=(i == 1))
                            V_(lambda h: h.tensor_tensor(out=att_sb[:], in0=attp[:, 0:128], in1=amask[:], op=ALU.mult),
                               [attp, amask], [att_sb])
                            T_(lambda h: h.matmul(op_[:, :], lhsT=att_sb[:, :], rhs=vb[:, es_], start=True, stop=False),
                               [att_sb, vb], [op_], sig=False)
                        for ch in range(2):
                            ps_ = slice(ch * 64, (ch + 1) * 64)
                            if isq:
                                for i, dc in enumerate((2 * hh, 2 * hh + 1)):
                                    last = (ch == 1 and i == 1)
                                    T_(lambda h, dc=dc, last=last, ps_=ps_: h.matmul(
                                        op_[ps_, :], lhsT=qsT[:, dc, ps_], rhs=Sbf[dc][:, :], start=False, stop=last),
                                       [qsT, Sbf[dc]], [op_], sig=last)
                            if t == gla_tiles[-1] and ch == 1:
                                continue
                            for dc in (2 * hh, 2 * hh + 1):
                                kvp = kvps[kvc[0] % 3]
                                kvc[0] += 1
                                T_(lambda h, dc=dc, ps_=ps_, kvp=kvp: h.matmul(kvp[:, :], lhsT=kend[ps_, dc * 128:(dc + 1) * 128],
                                                                      rhs=vb[ps_, es_], start=True, stop=True),
                                   [kend, vb], [kvp])
                                V_(lambda h, dc=dc, ch=ch, kvp=kvp: h.scalar_tensor_tensor(
                                    out=Sst[dc][:], in0=Sst[dc][:], scalar=eb[:, dc, ch * 64 + 63:ch * 64 + 64],
                                    in1=kvp[:], op0=ALU.mult, op1=ALU.add), [Sst[dc], eb, kvp], [Sst[dc]])
                                A_(lambda h, dc=dc: h.copy(out=Sbf[dc][:], in_=Sst[dc][:]), [Sst[dc]], [Sbf[dc]])
                        if isq:
                            A_(lambda h: h.activation(out=junk[:], in_=op_[:], func=AF.Square, accum_out=nst[:, 0:1]),
                               [op_], [junk, nst])
                            V_(lambda h: h.tensor_scalar(out=nst[:, 1:2], in0=nst[:, 0:1], scalar1=1.0 / GDV, scalar2=EPS,
                                                         op0=ALU.mult, op1=ALU.add), [nst], [nst])
                            A_(lambda h: h.activation(out=nst[:, 2:3], in_=nst[:, 1:2], func=AF.Ln), [nst], [nst])
                            A_(lambda h: h.activation(out=nst[:, 2:3], in_=nst[:, 2:3], func=AF.Exp, scale=-0.5), [nst], [nst])
                            V_(lambda h: h.scalar_tensor_tensor(out=ysb[:], in0=op_[:], scalar=nst[:, 2:3], in1=gng[:],
                                                                op0=ALU.mult, op1=ALU.mult), [op_, nst, gng], [ysb])
                            V_(lambda h: h.tensor_tensor(out=og[:, es_], in0=ysb[:], in1=sg[:, es_], op=ALU.mult),
                               [ysb, sg], [og])
                    if isq:
                        A_(lambda h: h.activation(out=sg[:], in_=grb[:], func=AF.Silu), [grb], [sg])
                    for hh in range(GH):
                        head(hh)
                    if isq:
                        transpose_out(og, 16, 0, t, ktp, mT)

                for t in gla_tiles:
                    gla_tile(t)
                S.barrier(release=[W2, b2, gm, amask, gng, qk32] + kTg + qTg + vt + grt + glr + [mT])

        DCFG = (1, 4, 16)
        if want("dil"):
            with ExitStack() as st:
                relb = P.sb(st, "dl_relb", [32, DH], F32)
                oneh = P.sb(st, "dl_oneh", [32, 3, 384], F32)
                bneg = P.sb(st, "dl_bneg", [DH, 3, 384], F32)
                antiI = P.sb(st, "dl_antiI", [128, 2, 256], F32)
                flags = P.sb(st, "dl_flags", [128, 2], F32)
                negc = P.sb(st, "dl_negc", [128, 1], F32)
                S.dma(sp, lambda h: h.dma_start(out=relb[:], in_=rel_bias), relb, None)
                S.dma(sp, lambda h: h.dma_start(out=oneh[:], in_=c_onehot), oneh, None)
                S.dma(sp, lambda h: h.dma_start(out=bneg[:], in_=c_bandneg), bneg, None)
                S.dma(sp, lambda h: h.dma_start(out=antiI[:], in_=c_antiI), antiI, None)
                S.dma(sp, lambda h: h.dma_start(out=flags[:], in_=c_flags), flags, None)
                G_(lambda h: h.memset(negc[:], NEG), [], [negc])
                fext_sb = P.sb(st, "dl_fext", [DH, 384], F32)
                Hk = P.sb(st, "dl_Hk", [128, 2, DH, 128], F32)
                tbl = P.sb(st, "dl_tbl", [128, DH, 256], F32)
                Qb = [P.sb(st, "dl_Q%d" % i, [128, 2048], BF16) for i in range(2)]
                Kb = [P.sb(st, "dl_K%d" % i, [128, 2048], BF16) for i in range(2)]
                Vb = [P.sb(st, "dl_V%d" % i, [128, 2048], BF16) for i in range(3)]
                QT = [P.sb(st, "dl_QT%d" % i, [128, DH, 128], BF16) for i in range(2)]
                KT = [P.sb(st, "dl_KT%d" % i, [128, DH, 128], BF16) for i in range(3)]
                Sp = [P.sb(st, "dl_Sp%d" % i, [128, 2, 256], F32) for i in range(2)]
                Pb = [P.sb(st, "dl_P%d" % i, [128, 256], BF16) for i in range(3)]
                PT = [P.sb(st, "dl_PT%d" % i, [128, 4, 256], BF16) for i in range(2)]
                Oall = [P.sb(st, "dl_O%d" % i, [128, 2048], F32) for i in range(2)]
                msb = [P.sb(st, "dl_ms%d" % i, [128, 2, DH], F32) for i in range(2)]
                nmx = [P.sb(st, "dl_nm%d" % i, [128, 2], F32) for i in range(2)]
                trp = [P.ps(st, "dl_trp%d" % i, [128, 1024], BF16) for i in range(2)]
                Sps = [P.ps(st, "dl_Sps%d" % i, [128, 512], F32) for i in range(2)]
                ptp = P.ps(st, "dl_ptp", [128, 1024], BF16)
                ops = [P.ps(st, "dl_ops%d" % i, [128, 512], F32) for i in range(2)]
                fextb = Buf("fext_d")
                ctr = {"u": 0, "v": 0, "k": 0}
                SCALE = DE ** -0.5
                dil_cfgs = P.opts.get("dil_cfgs", [0, 1, 2])

                def build_tables(ci):
                    T_(lambda h: h.matmul(Sps[0][0:DH, 0:384], lhsT=relb[:, :], rhs=oneh[:, ci, :], start=True, stop=True),
                       [relb, oneh], [Sps[0]])
                    V_(lambda h: h.tensor_tensor(out=fext_sb[:], in0=Sps[0][0:DH, 0:384], in1=bneg[:, ci, :], op=ALU.add),
                       [Sps[0], bneg], [fext_sb])
                    S.dma(act, lambda h: h.dma_start(out=fext_d[ci], in_=fext_sb[:]), fextb, fext_sb)
                    for c in range(2):
                        src = bass.AP(fext_d.tensor, ci * DH * 384 + c * 128, [[1, 128], [384, DH], [1, 128]])
                        S.dma(sp, lambda h, c=c, src=src: h.dma_start(out=Hk[:, c, :, :], in_=src), Hk, fextb)
                    for hh in range(DH):
                        pb = Sps[hh % 2]
                        for c in range(2):
                            T_(lambda h, hh=hh, c=c, pb=pb: h.matmul(pb[:, 0:256], lhsT=Hk[:, c, hh, :], rhs=antiI[:, c, :],
                                                                    start=(c == 0), stop=(c == 1)),
                               [Hk, antiI], [pb], sig=(c == 1))
                        evac(lambda hh=hh, pb=pb: (tbl[:, hh, :], pb[:, 0:256]), pb, [tbl])

                def load_rows(buf, src, d, r, b):
                    start = d * 128 * b + r
                    S.dma(sp, lambda h: h.dma_start(out=buf[:], in_=src[start:start + d * 127 + 1:d, :]), buf, None)

                def transpose_all(srcb, dstT):
                    for half in range(2):
                        pb = trp[half]
                        for k in range(8):
                            hh = half * 8 + k
                            T_(lambda h, hh=hh, k=k, pb=pb: h.transpose(out=pb[:, k * 128:(k + 1) * 128],
                                                                        in_=srcb[:, hh * 128:(hh + 1) * 128],
                                                                        identity=ident_b[:]),
                               [srcb, ident_b], [pb], sig=(k == 7))
                        evac(lambda half=half, pb=pb: (dstT[:, half * 8:(half + 1) * 8, :],
                                                       pb[:].rearrange("p (k t) -> p k t", k=8)), pb, [dstT])

                def unit(ci, d, r, b, kt_prev, v_prev, kt_cur, v_cur, cvar):
                    u = ctr["u"]
                    ctr["u"] += 1
                    qb, qt = Qb[u % 2], QT[u % 2]
                    ob, mb = Oall[u % 2], msb[u % 2]
                    load_rows(qb, dq, d, r, b)
                    transpose_all(qb, qt)
                    def pairA(hp):
                        sps = Sps[hp % 2]
                        for i in range(2):
                            hh = hp * 2 + i
                            T_(lambda h, hh=hh, i=i: h.matmul(sps[:, i * 256:i * 256 + 128], lhsT=qt[:, hh, :],
                                                              rhs=kt_prev[:, hh, :], start=True, stop=True),
                               [qt, kt_prev], [sps], sig=False)
                            T_(lambda h, hh=hh, i=i: h.matmul(sps[:, i * 256 + 128:i * 256 + 256], lhsT=qt[:, hh, :],
                                                              rhs=kt_cur[:, hh, :], start=True, stop=True),
                               [qt, kt_cur], [sps], sig=(i == 1))

                    def pairB(hp):
                        sps, spb, nm = Sps[hp % 2], Sp[hp % 2], nmx[hp % 2]
                        V_(lambda h, hp=hp: h.scalar_tensor_tensor(
                            out=spb[:], in0=sps[:].rearrange("p (i k) -> p i k", i=2), scalar=SCALE,
                            in1=tbl[:, 2 * hp:2 * hp + 2, :], op0=ALU.mult, op1=ALU.add), [sps, tbl], [spb])
                        if cvar is not None:
                            V_(lambda h: h.tensor_scalar(out=spb[:, :, 0:128], in0=spb[:, :, 0:128], scalar1=cvar,
                                                         scalar2=None, op0=ALU.add), [spb, flags, negc], [spb])
                        V_(lambda h: h.tensor_reduce(out=nm[:], in_=spb[:], axis=AX.X, op=ALU.max, negate=True),
                           [spb], [nm])
                        V_(lambda h, hp=hp: h.tensor_copy(out=mb[:, 0, 2 * hp:2 * hp + 2], in_=nm[:]), [nm], [mb])
                        for i in range(2):
                            hh = hp * 2 + i
                            pbuf = Pb[hh % 3]
                            A_(lambda h, hh=hh, i=i, pbuf=pbuf: h.activation(
                                out=pbuf[:], in_=spb[:, i, :], func=AF.Exp, bias=nm[:, i:i + 1], scale=1.0,
                                accum_out=mb[:, 1, hh:hh + 1]), [spb, nm], [pbuf, mb])
                            q4 = hh % 4
                            for c in range(2):
                                T_(lambda h, pbuf=pbuf, q4=q4, c=c: h.transpose(
                                    out=ptp[:, q4 * 256 + c * 128:q4 * 256 + (c + 1) * 128],
                                    in_=pbuf[:, c * 128:(c + 1) * 128], identity=ident_b[:]),
                                   [pbuf, ident_b], [ptp], sig=(c == 1))
                        if hp % 2 == 1:
                            g4 = hp // 2
                            ptb, opb = PT[g4 % 2], ops[g4 % 2]
                            evac(lambda ptb=ptb: (ptb[:], ptp[:].rearrange("p (a k) -> p a k", a=4)), ptp, [ptb])
                            for q4 in range(4):
                                hh = g4 * 4 + q4
                                T_(lambda h, hh=hh, q4=q4: h.matmul(opb[:, q4 * 128:(q4 + 1) * 128], lhsT=ptb[:, q4, 0:128],
                                                                    rhs=v_prev[:, hh * 128:(hh + 1) * 128],
                                                                    start=True, stop=False),
                                   [ptb, v_prev], [opb], sig=False)
                                T_(lambda h, hh=hh, q4=q4: h.matmul(opb[:, q4 * 128:(q4 + 1) * 128], lhsT=ptb[:, q4, 128:256],
                                                                    rhs=v_cur[:, hh * 128:(hh + 1) * 128],
                                                                    start=False, stop=True),
                                   [ptb, v_cur], [opb], sig=(q4 == 3))
                            evac(lambda g4=g4, opb=opb: (ob[:, g4 * 512:(g4 + 1) * 512], opb[:]), opb, [ob])
                    pairA(0)
                    for hp_ in range(DH // 2):
                        if hp_ + 1 < DH // 2:
                            pairA(hp_ + 1)
                        pairB(hp_)
                    start = d * 128 * b + r
                    rows = slice(start, start + d * 127 + 1, d)
                    S.dma(act, lambda h: h.dma_start(out=oc[ci][rows, :], in_=ob[:]), None, ob)
                    S.dma(act, lambda h: h.dma_start(out=msc[ci][rows, :, :], in_=mb[:]), None, mb)

                def load_kv(d, r, b):
                    kb = Kb[ctr["k"] % 2]
                    ktb = KT[ctr["k"] % 3]
                    vb = Vb[ctr["k"] % 3]
                    ctr["k"] += 1
                    load_rows(kb, dk, d, r, b)
                    load_rows(vb, dv, d, r, b)
                    transpose_all(kb, ktb)
                    return ktb, vb

                for ci in dil_cfgs:
                    d = DCFG[ci]
                    build_tables(ci)
                    nfirst = 16 // d
                    for r in range(d):
                        if d == 1:
                            qbs = list(range(15, 32))
                        elif d == 4:
                            qbs = list(range(3, 8)) if r >= 2 else list(range(4, 8))
                        else:
                            qbs = [0, 1] if r >= 14 else [1]
                        prev = None
                        for b in qbs:
                            if prev is None and b > 0:
                                prev = load_kv(d, r, b - 1)
                            cur = load_kv(d, r, b)
                            if b == 0:
                                cvar = negc[:, 0:1]
                                pk, pv = cur
                            else:
                                pk, pv = prev
                                cvar = flags[:, 0:1] if (b - 1) < nfirst else None
                            unit(ci, d, r, b, pk, pv, cur[0], cur[1], cvar)
                            prev = cur
                S.barrier(release=[relb, oneh, bneg, antiI, flags, Hk, fextb] + Qb + Kb + Vb + Oall + msb + [fext_sb])

        if want("dilc"):
            with ExitStack() as st:
                O3 = [[P.sb(st, "dc_O%d_%d" % (i, j), [128, 2048], F32) for j in range(3)] for i in range(2)]
                ms3 = [P.sb(st, "dc_ms%d" % i, [128, 3, 2, DH], F32) for i in range(2)]
                mneg = P.sb(st, "dc_mneg", [128, DH], F32)
                w3 = P.sb(st, "dc_w3", [128, 3, DH], F32)
                ws3 = P.sb(st, "dc_ws3", [128, 3, DH], F32)
                den = P.sb(st, "dc_den", [128, DH], F32)
                acc = P.sb(st, "dc_acc", [128, 2048], F32)
                tmp = P.sb(st, "dc_tmp", [128, 2048], F32)
                od = P.sb(st, "dc_od", [128, 2048], BF16)
                mT2 = P.sb(st, "dc_mT", [128, 16, 128], BF16)
                ptb2 = P.ps(st, "dc_ptb", [128, 1024], BF16)
                for it, t in enumerate(P.opts.get("dilc_tiles", list(range(QT0, LT)))):
                    Ob, mb = O3[it % 2], ms3[it % 2]
                    rows = slice(t * 128, (t + 1) * 128)
                    for ci in range(3):
                        S.dma(sp, lambda h, ci=ci, Ob=Ob, rows=rows: h.dma_start(out=Ob[ci][:], in_=oc[ci][rows, :]), Ob[ci], None)
                        S.dma(sp, lambda h, ci=ci, mb=mb, rows=rows: h.dma_start(out=mb[:, ci, :, :], in_=msc[ci][rows, :, :]), mb, None)
                    V_(lambda h, mb=mb: h.tensor_tensor(out=mneg[:], in0=mb[:, 0, 0, :], in1=mb[:, 1, 0, :], op=ALU.min),
                       [mb], [mneg])
                    V_(lambda h, mb=mb: h.tensor_tensor(out=mneg[:], in0=mneg[:], in1=mb[:, 2, 0, :], op=ALU.min),
                       [mb, mneg], [mneg])
                    for ci in range(3):
                        V_(lambda h, ci=ci, mb=mb: h.tensor_tensor(out=w3[:, ci, :], in0=mneg[:], in1=mb[:, ci, 0, :],
                                                                  op=ALU.subtract), [mneg, mb], [w3])
                    A_(lambda h: h.activation(out=w3[:], in_=w3[:], func=AF.Exp), [w3], [w3])
                    V_(lambda h, mb=mb: h.tensor_tensor(out=ws3[:], in0=w3[:], in1=mb[:, :, 1, :], op=ALU.mult),
                       [w3, mb], [ws3])
                    V_(lambda h: h.tensor_tensor(out=den[:], in0=ws3[:, 0, :], in1=ws3[:, 1, :], op=ALU.add), [ws3], [den])
                    V_(lambda h: h.tensor_tensor(out=den[:], in0=den[:], in1=ws3[:, 2, :], op=ALU.add), [ws3, den], [den])
                    V_(lambda h: h.reciprocal(out=den[:], in_=den[:]), [den], [den])
                    for ci in range(3):
                        V_(lambda h, ci=ci: h.tensor_tensor(out=w3[:, ci, :], in0=w3[:, ci, :], in1=den[:], op=ALU.mult),
                           [w3, den], [w3])

                    def bc(ci):
                        return w3[:, ci, :].unsqueeze(2).broadcast_to([128, DH, 128])

                    def v3(b_):
                        return b_[:].rearrange("p (a e) -> p a e", a=DH)
                    V_(lambda h, Ob=Ob: h.tensor_tensor(out=v3(acc), in0=v3(Ob[0]), in1=bc(0), op=ALU.mult),
                       [Ob[0], w3], [acc])
                    G_(lambda h, Ob=Ob: h.tensor_tensor(out=v3(tmp), in0=v3(Ob[1]), in1=bc(1), op=ALU.mult),
                       [Ob[1], w3], [tmp])
                    V_(lambda h: h.tensor_tensor(out=acc[:], in0=acc[:], in1=tmp[:], op=ALU.add), [acc, tmp], [acc])
                    G_(lambda h, Ob=Ob: h.tensor_tensor(out=v3(tmp), in0=v3(Ob[2]), in1=bc(2), op=ALU.mult),
                       [Ob[2], w3], [tmp])
                    V_(lambda h: h.tensor_tensor(out=od[:], in0=acc[:], in1=tmp[:], op=ALU.add), [acc, tmp], [od])
                    transpose_out(od, 16, 16, t, ptb2, mT2)
                S.barrier(release=[mT2] + O3[0] + O3[1] + ms3)

        def qg_rows(qg):
            return (1920, 128) if qg == 0 else (2048 + (qg - 1) * 512, 512)
        QGS = P.opts.get("qgs", [0, 1, 2, 3, 4])
        NORM_Q_GROUPS = [[15]] + [[16 + 4 * i + j for j in range(4)] for i in range(4)]

        def resid_gemm(tag, xT_scr, kcx, w_scr, castbuf, res_src, dst):
            with ExitStack() as st:
                rs = [P.sb(st, tag + "_rs%d" % i, [128, 4, 512], F32) for i in range(2)]
                so = [P.sb(st, tag + "_so%d" % i, [128, 4, 512], F32) for i in range(2)]
                rr = [0]

                def mk(c0):
                    state = {}

                    def epi(pb, m, g, blk):
                        r0, T = qg_rows(g)
                        if m == 0:
                            state["rs"], state["so"] = rs[rr[0] % 2], so[rr[0] % 2]
                            rr[0] += 1
                            rsb = state["rs"]
                            srcv = res_src[r0:r0 + T, c0:c0 + 512].rearrange("(m p) c -> p m c", p=128)
                            S.dma(sp, lambda h: h.dma_start(out=rsb[:, 0:T // 128, :], in_=srcv), rsb, None)
                        rsb, sob = state["rs"], state["so"]
                        V_(lambda h: h.tensor_tensor(out=sob[:, m, :], in0=pb[:, :], in1=rsb[:, m, :], op=ALU.add),
                           [pb, rsb], [sob])

                    def fin(g, blk):
                        r0, T = qg_rows(g)
                        sob = state["so"]
                        dv_ = dst[r0:r0 + T, c0:c0 + 512].rearrange("(m p) c -> p m c", p=128)
                        S.dma(act, lambda h: h.dma_start(out=dv_, in_=sob[:, 0:T // 128, :]), None, sob)
                    return (w_scr, c0, 512, "A", epi, fin)
                blocks = [mk(c0) for c0 in range(0, D, 512)]
                gemm_stage(tag, [(g, qg_rows(g)[1]) for g in QGS], lambda g: xT_scr[g][:, :, 0:qg_rows(g)[1]], kcx, 512,
                           castbuf, lambda g: blocks, extra_bufs=rs + so)

        if want("wout"):
            resid_gemm("wo", mixT, KC, wout_bf, CB["mid"], x_loc, h1)

        if want("norm2"):
            norm_stage("n2", lambda t: h1[t * 128:(t + 1) * 128, :], NORM_Q_GROUPS, norm_xattn_g,
                       dstT=lambda g: hn2T[qg_of_tile(g[0])[0]][:, :, 0:len(g) * 128])
            norm_stage("nm", lambda t: mem[t * 128:(t + 1) * 128, :], [[0, 1]], mem_norm_g,
                       dstT=lambda g: memT[0][:, :, 0:256])

        if want("xattn"):
            with ExitStack() as st:
                xstB = [P.sb(st, "xa_stB%d" % i, [128, 4, 512], BF16) for i in range(2)]
                xrr = [0]

                def mkx(w_scr, mode, dst_fn):
                    state = {}

                    def epi(pb, i, g, blk):
                        T = 256 if g == "mem" else qg_rows(g)[1]
                        if i == 0:
                            state["stg"] = xstB[xrr[0] % 2]
                            xrr[0] += 1
                        stg = state["stg"]
                        if mode == "B":
                            evac(lambda: (stg[:, i, 0:T], pb[:, 0:T]), pb, [stg])
                        else:
                            evac(lambda: (stg[:, i, :], pb[:, 0:512]), pb, [stg])

                    def fin(g, blk):
                        stg = state["stg"]
                        T = 256 if g == "mem" else qg_rows(g)[1]
                        if mode == "B":
                            S.dma(act, lambda h: h.dma_start(out=dst_fn(g), in_=stg[:, :, 0:T]), None, stg)
                        else:
                            S.dma(act, lambda h: h.dma_start(out=dst_fn(g), in_=stg[:, 0:T // 128, :]), None, stg)
                    return (w_scr, 0, 512, mode, epi, fin)
                gemm_stage("xkv", [("mem", 256)], lambda g: memT[0][:, :, 0:256], KC, 512, CB["mid"],
                           lambda g: [mkx(wxk_bf, "B", lambda g: kxT[:, :, :]),
                                      mkx(wxv_bf, "A", lambda g: vx[:, :, :])], extra_bufs=[])
                gemm_stage("xq", [(g, qg_rows(g)[1]) for g in QGS], lambda g: hn2T[g][:, :, 0:qg_rows(g)[1]], KC, 512,
                           CB["mid"], lambda g: [mkx(wxq_bf, "B", lambda g: qxT[g][:, :, 0:qg_rows(g)[1]])],
                           extra_bufs=xstB)
            with ExitStack() as st:
                kx = P.sb(st, "xa_kx", [128, 4, 256], BF16)
                vxs = P.sb(st, "xa_vx", [128, 2, 512], BF16)
                S.dma(sp, lambda h: h.dma_start(out=kx[:], in_=kxT[:, :, :]), kx, None)
                S.dma(sp, lambda h: h.dma_start(out=vxs[:], in_=vx[:, :, :]), vxs, None)
                qx = [P.sb(st, "xa_qx%d" % i, [128, 4, 512], BF16) for i in range(2)]
                xPb = [P.sb(st, "xa_P%d" % i, [128, 256], BF16) for i in range(2)]
                xPTb = [P.sb(st, "xa_PT%d" % i, [128, 256], BF16) for i in range(2)]
                st4 = [P.sb(st, "xa_st%d" % i, [128, 4], F32) for i in range(2)]
                oxb = [P.sb(st, "xa_ox%d" % i, [128, 512], BF16) for i in range(2)]
                oxT_sb = [P.sb(st, "xa_oxT%d" % i, [128, 4, 512], BF16) for i in range(2)]
                xSps = [P.ps(st, "xa_Sps%d" % i, [128, 512], F32) for i in range(2)]
                xptp = [P.ps(st, "xa_ptp%d" % i, [128, 1024], BF16) for i in range(2)]
                xops = [P.ps(st, "xa_ops%d" % i, [128, 512], F32) for i in range(2)]
                xSC = DE ** -0.5
                xcnt = [0]

                def xtile(gi, g, m):
                    qb, oT = qx[gi % 2], oxT_sb[gi % 2]
                    ob = oxb[xcnt[0] % 2]
                    tsl = slice(m * 128, (m + 1) * 128)
                    for hh in range(4):
                        u = xcnt[0] * 4 + hh
                        sps, pb_, ptb_, stt, ptp_, ops_ = xSps[u % 2], xPb[u % 2], xPTb[u % 2], st4[u % 2], xptp[u % 2], xops[u % 2]

                        def one(hh=hh, sps=sps, pb_=pb_, ptb_=ptb_, stt=stt, ptp_=ptp_, ops_=ops_):
                            T_(lambda h: h.matmul(sps[:, 0:256], lhsT=qb[:, hh, tsl], rhs=kx[:, hh, :], start=True, stop=True),
                               [qb, kx], [sps])
                            V_(lambda h: h.tensor_reduce(out=stt[:, 0:1], in_=sps[:, 0:256], axis=AX.X, op=ALU.max,
                                                         negate=True), [sps], [stt])
                            V_(lambda h: h.tensor_scalar(out=stt[:, 1:2], in0=stt[:, 0:1], scalar1=xSC, scalar2=None,
                                                         op0=ALU.mult), [stt], [stt])
                            A_(lambda h: h.activation(out=pb_[:], in_=sps[:, 0:256], func=AF.Exp, bias=stt[:, 1:2], scale=xSC,
                                                      accum_out=stt[:, 2:3]), [sps, stt], [pb_, stt])
                            V_(lambda h: h.reciprocal(out=stt[:, 3:4], in_=stt[:, 2:3]), [stt], [stt])
                            for c in range(2):
                                T_(lambda h, c=c: h.transpose(out=ptp_[:, c * 128:(c + 1) * 128],
                                                              in_=pb_[:, c * 128:(c + 1) * 128], identity=ident_b[:]),
                                   [pb_, ident_b], [ptp_], sig=(c == 1))
                            evac(lambda: (ptb_[:], ptp_[:, 0:256]), ptp_, [ptb_])
                            for c in range(2):
                                T_(lambda h, c=c: h.matmul(ops_[:, 0:128], lhsT=ptb_[:, c * 128:(c + 1) * 128],
                                                           rhs=vxs[:, c, hh * 128:(hh + 1) * 128], start=(c == 0), stop=(c == 1)),
                                   [ptb_, vxs], [ops_], sig=(c == 1))
                            V_(lambda h: h.tensor_scalar(out=ob[:, hh * 128:(hh + 1) * 128], in0=ops_[:, 0:128],
                                                         scalar1=stt[:, 3:4], scalar2=None, op0=ALU.mult), [ops_, stt], [ob])
                        one()
                    pt2 = xptp[xcnt[0] % 2]
                    for k in range(4):
                        T_(lambda h, k=k: h.transpose(out=pt2[:, k * 128:(k + 1) * 128], in_=ob[:, k * 128:(k + 1) * 128],
                                                      identity=ident_b[:]), [ob, ident_b], [pt2], sig=(k == 3))
                    evac(lambda: (oT[:, :, tsl], pt2[:, 0:512].rearrange("p (k t) -> p k t", k=4)), pt2, [oT])
                    xcnt[0] += 1

                for gi, g in enumerate(QGS):
                    r0, T = qg_rows(g)
                    qb, oT = qx[gi % 2], oxT_sb[gi % 2]
                    S.dma(sp, lambda h, qb=qb, g=g, T=T: h.dma_start(out=qb[:, :, 0:T], in_=qxT[g][:, :, 0:T]), qb, None)
                    for m in range(T // 128):
                        xtile(gi, g, m)
                    S.dma(act, lambda h, oT=oT, g=g, T=T: h.dma_start(out=oxT[g][:, :, 0:T], in_=oT[:, :, 0:T]), None, oT)
                S.barrier(release=[kx, vxs] + qx + oxT_sb)
            resid_gemm("xo", oxT, 4, wxo_bf, CB["mid"], h1, h2)

        if want("norm3"):
            norm_stage("n3", lambda t: h2[t * 128:(t + 1) * 128, :], NORM_Q_GROUPS, norm_ffn_g,
                       dstT=lambda g: hn3T[qg_of_tile(g[0])[0]][:, :, 0:len(g) * 128])

        if want("ffn"):
            with ExitStack() as st:
                f_cwrow = P.sb(st, "f_cwrow", [FC, 4, 128], F32)
                f_cw = P.sb(st, "f_cw", [128, 4, FC], F32)
                f_flags = P.sb(st, "f_flags", [128, 2], F32)
                f_gprev = P.sb(st, "f_gprev", [128, FC, 2], F32)
                f_xT = P.sb(st, "f_xT", [128, KC, 512], BF16)
                f_xh = P.sb(st, "f_xh", [128, KC, 128], BF16)
                f_act = P.sb(st, "f_act", [128, 43, 512], BF16)
                f_wb = [P.sb(st, "f_wb%d" % i, [128, KC, 256], BF16) for i in range(3)]
                f_wd = [P.sb(st, "f_wd%d" % i, [128, 8, 512], BF16) for i in range(4)]
                f_gx = [P.sb(st, "f_gx%d" % i, [128, 514], F32) for i in range(2)]
                f_acc = [P.sb(st, "f_acc%d" % i, [128, 512], F32) for i in range(2)]
                f_sg = [P.sb(st, "f_sg%d" % i, [128, 2, 512], F32) for i in range(2)]
                f_res = [P.sb(st, "f_res%d" % i, [128, 512], F32) for i in range(4)]
                f_so = [P.sb(st, "f_so%d" % i, [128, 512], F32) for i in range(4)]
                f_ps = [P.ps(st, "f_ps%d" % i, [128, 512], F32) for i in range(3)]
                f_psh = P.ps(st, "f_psh", [128, 512], F32)
                f_psd = [P.ps(st, "f_psd%d" % i, [128, 512], F32) for i in range(4)]
                S.dma(sp, lambda h: h.dma_start(out=f_cwrow[:, 0:3, :], in_=ffn_conv_w.rearrange("i (j p) -> j i p", p=128)),
                      f_cwrow, None)
                S.dma(sp, lambda h: h.dma_start(out=f_cwrow[:, 3:4, :], in_=ffn_conv_b.rearrange("o (j p) -> j o p", p=128)),
                      f_cwrow, None)
                S.dma(sp, lambda h: h.dma_start(out=f_flags[:], in_=c_flags), f_flags, None)
                for i in range(4):
                    T_(lambda h, i=i: h.transpose(out=f_ps[0][:, i * 128:i * 128 + FC], in_=f_cwrow[:, i, :],
                                                  identity=ident_f[0:FC, 0:FC]), [f_cwrow, ident_f], [f_ps[0]], sig=(i == 3))
                V_(lambda h: h.tensor_copy(out=f_cw[:], in_=f_ps[0][:].rearrange("p (i j) -> p i j", i=4)[:, :, 0:FC]),
                   [f_ps[0]], [f_cw])
                G_(lambda h: h.memset(f_gprev[:], 0.0), [], [f_gprev])
                fc = {"w": 0, "p": 0, "c": 0, "d": 0, "e": 0}
                ffn_tgs = P.opts.get("ffn_tgs", [1, 2, 3, 4])
                h3b = {tg: Buf("h3b%d" % tg) for tg in ffn_tgs}

                def load_wblk(w_scr, c0, ncols):
                    wbb = f_wb[fc["w"] % 3]
                    fc["w"] += 1
                    src = w_scr[:, c0:c0 + ncols].rearrange("(kc p) n -> p kc n", p=128)
                    S.dma(sp, lambda h: h.dma_start(out=wbb[:, :, 0:ncols], in_=src), wbb, CB["gu"])
                    return wbb

                def mm_chunk(wbb, c):
                    pb = f_ps[fc["p"] % 3]
                    fc["p"] += 1
                    for kc in range(KC):
                        T_(lambda h, kc=kc: h.matmul(pb[:, :], lhsT=wbb[:, kc, c * 128:(c + 1) * 128], rhs=f_xT[:, kc, :],
                                                     start=(kc == 0), stop=(kc == KC - 1)), [wbb, f_xT], [pb], sig=(kc == KC - 1))
                    return pb

                def gate_chunk(tg, wbb, c, j, sgb):
                    pb = mm_chunk(wbb, c)
                    if tg == 1:
                        for kc in range(KC):
                            T_(lambda h, kc=kc: h.matmul(f_psh[:, 0:2], lhsT=wbb[:, kc, c * 128:(c + 1) * 128], rhs=f_xh[:, kc, 126:128],
                                                         start=(kc == 0), stop=(kc == KC - 1)), [wbb, f_xh], [f_psh],
                               sig=(kc == KC - 1))
                        V_(lambda h: h.tensor_scalar(out=f_gprev[:, j, :], in0=f_psh[:, 0:2], scalar1=f_flags[:, 1:2],
                                                     scalar2=None, op0=ALU.mult), [f_psh, f_flags], [f_gprev])
                    gx, acc = f_gx[fc["c"] % 2], f_acc[fc["c"] % 2]
                    fc["c"] += 1
                    A_(lambda h: h.copy(out=gx[:, 2:514], in_=pb[:, :]), [pb], [gx])
                    G_(lambda h: h.tensor_copy(out=gx[:, 0:2], in_=f_gprev[:, j, :]), [f_gprev], [gx])
                    G_(lambda h: h.tensor_copy(out=f_gprev[:, j, :], in_=gx[:, 512:514]), [gx], [f_gprev])
                    V_(lambda h: h.tensor_scalar(out=acc[:], in0=gx[:, 0:512], scalar1=f_cw[:, 0, j:j + 1],
                                                 scalar2=f_cw[:, 3, j:j + 1], op0=ALU.mult, op1=ALU.add), [gx, f_cw], [acc])
                    V_(lambda h: h.scalar_tensor_tensor(out=acc[:], in0=gx[:, 1:513], scalar=f_cw[:, 1, j:j + 1], in1=acc[:],
                                                        op0=ALU.mult, op1=ALU.add), [gx, f_cw, acc], [acc])
                    V_(lambda h: h.scalar_tensor_tensor(out=acc[:], in0=gx[:, 2:514], scalar=f_cw[:, 2, j:j + 1], in1=acc[:],
                                                        op0=ALU.mult, op1=ALU.add), [gx, f_cw, acc], [acc])
                    A_(lambda h: h.activation(out=sgb[:, c, :], in_=acc[:], func=AF.Silu), [acc], [sgb])

                def up_chunk(wbb, c, jl, sgb):
                    pb = mm_chunk(wbb, c)
                    V_(lambda h: h.tensor_tensor(out=f_act[:, jl, :], in0=pb[:, :], in1=sgb[:, c, :], op=ALU.mult),
                       [pb, sgb], [f_act])

                def down_phase(tg, half):
                    r0, T = qg_rows(tg)
                    j0 = half * 43
                    src_h = h2 if half == 0 else h3
                    for nb in range(8):
                        cs = slice(nb * 512, (nb + 1) * 512)
                        psd = f_psd if nb % 2 == 0 else [f_ps[0], f_ps[1], f_ps[2], f_psh]
                        for jg in range(0, 43, 8):
                            nj = min(8, 43 - jg)
                            wdp = f_wd[fc["d"] % 4]
                            fc["d"] += 1
                            rows = slice((j0 + jg) * 128, (j0 + jg + nj) * 128)
                            srcw = wd_bf[rows, cs].rearrange("(jj p) c -> p jj c", p=128)
                            S.dma(sp, lambda h, wdp=wdp, srcw=srcw, nj=nj: h.dma_start(out=wdp[:, 0:nj, :], in_=srcw),
                                  wdp, CB["wd"])
                            for jj in range(nj):
                                jl = jg + jj
                                for m in range(4):
                                    T_(lambda h, wdp=wdp, jj=jj, jl=jl, m=m, nj=nj, psd=psd: h.matmul(
                                        psd[m][:, :], lhsT=f_act[:, jl, m * 128:(m + 1) * 128], rhs=wdp[:, jj, :],
                                        start=(jl == 0), stop=(jl == 42)), [f_act, wdp], [psd[m]], sig=(jl == 42 or (jj == nj - 1 and m == 3)))
                        for m in range(4):
                            rr_ = slice(r0 + m * 128, r0 + (m + 1) * 128)
                            S.dma(sp, lambda h, m=m, rr_=rr_, cs=cs: h.dma_start(out=f_res[m][:], in_=src_h[rr_, cs]), f_res[m],
                                  h3b[tg] if half == 1 else None)
                        for m in range(4):
                            rb, sob = f_res[m], f_so[m]
                            rr_ = slice(r0 + m * 128, r0 + (m + 1) * 128)
                            V_(lambda h, rb=rb, sob=sob, m=m, psd=psd: h.tensor_tensor(out=sob[:], in0=psd[m][:, :], in1=rb[:], op=ALU.add),
                               [psd[m], rb], [sob])
                            S.dma(act, lambda h, sob=sob, rr_=rr_, cs=cs: h.dma_start(out=h3[rr_, cs], in_=sob[:]), h3b[tg], sob)

                for tg in ffn_tgs:
                    S.dma(sp, lambda h, tg=tg: h.dma_start(out=f_xT[:], in_=hn3T[tg]), f_xT, None)
                    if tg == 1:
                        S.dma(sp, lambda h: h.dma_start(out=f_xh[:], in_=hn3T[0][:, :, 0:128]), f_xh, None)
                    for half in range(2):
                        j0 = half * 43
                        for b0 in range(0, 43, 2):
                            nch = min(2, 43 - b0)
                            c0 = (j0 + b0) * 128
                            sgb = f_sg[(b0 // 2) % 2]
                            wbb = load_wblk(wg_bf, c0, nch * 128)
                            for c in range(nch):
                                gate_chunk(tg, wbb, c, j0 + b0 + c, sgb)
                            wbb = load_wblk(wu_bf, c0, nch * 128)
                            for c in range(nch):
                                up_chunk(wbb, c, b0 + c, sgb)
                        down_phase(tg, half)
                S.barrier(release=[f_cwrow, f_flags, f_xT, f_xh] + f_wb + f_wd + f_res + f_so + list(h3b.values()))

        if want("final"):
            norm_stage("nf", lambda t: h3[t * 128:(t + 1) * 128, :], [[t] for t in P.opts.get("final_tiles", list(range(16, 32)))],
                       final_norm_g, dst_tok=lambda t: out[(t - 16) * 128:(t - 15) * 128, :])

        S.barrier()
        with nc.Block() as block:
            S.emit(block)
    return nc, P, S


def host_consts():
    c = {}
    c["c_ident"] = np.eye(128, dtype=np.float32)
    j = np.arange(128)[:, None]
    cc = np.arange(128)[None, :]
    same = (j // 64) == (cc // 64)
    tri = (same & (j <= cc)).astype(np.float32)
    m2 = (same & (j > cc)).astype(np.float32)
    mid = (same & ((j % 64) <= 32)).astype(np.float32)
    m3 = tri - mid
    c["c_gmats"] = np.ascontiguousarray(np.stack([m2, tri, m3], axis=1)).astype(np.float32)
    c["c_attmask"] = tri.copy()
    kk = np.arange(256)[:, None]
    kj = np.arange(256)[None, :]
    J = (kk == 255 - kj).astype(np.float32)
    c["c_antiI"] = np.ascontiguousarray(J.reshape(2, 128, 256).transpose(1, 0, 2))
    onehot = np.zeros((32, 3, 384), np.float32)
    bandneg = np.zeros((DH, 3, 384), np.float32)
    for ci, d in enumerate((1, 4, 16)):
        for i in range(384):
            rel = i - 127
            if 0 <= rel <= 128:
                dist = rel * d
                if dist < 16:
                    b = dist
                else:
                    b = 16 + int(np.float32(np.log(np.float32(max(dist, 1)) / np.float32(16)) /
                                            np.float32(np.log(2048 / 16)) * np.float32(16)))
                    b = min(b, 31)
                onehot[b, ci, i] = 1.0
            else:
                bandneg[:, ci, i] = NEG
    c["c_onehot"] = onehot
    c["c_bandneg"] = bandneg
    return c


def make_in_maps(inputs):
    consts = host_consts()
    x = np.asarray(inputs["x"], dtype=np.float32)
    maps = []
    for c in range(8):
        b, half = c // 2, c % 2
        m = {}
        if half == 0:
            xl = np.zeros((SEQ, D), np.float32)
            xl[2048:] = x[b, :2048]
        else:
            xl = np.ascontiguousarray(x[b])
        m["x_loc"] = xl
        m["mem"] = np.ascontiguousarray(inputs["mem"][b], dtype=np.float32)
        m["rel_bias"] = np.ascontiguousarray(inputs["rel_bias"], dtype=np.float32)
        for k in ("norm_mix_g", "gla_b_gate", "gla_norm_g", "norm_xattn_g", "mem_norm_g", "norm_ffn_g", "ffn_conv_b"):
            m[k] = np.ascontiguousarray(np.asarray(inputs[k], dtype=np.float32).reshape(1, -1))
        m["final_norm_g"] = np.ascontiguousarray(np.asarray(inputs["final_norm_g"], dtype=np.float32).reshape(1, -1))
        for k in ("w_in", "gla_w_gate2", "w_out", "w_xq", "w_xk", "w_xv", "w_xo", "w_ffn_gate", "w_ffn_up",
                  "ffn_conv_w", "w_ffn_down"):
            m[k] = np.ascontiguousarray(np.asarray(inputs[k], dtype=np.float32)[0])
        m.update(consts)
        fl = np.zeros((128, 2), np.float32)
        fl[:, 0] = NEG if half == 0 else 0.0
        fl[:, 1] = 0.0 if half == 0 else 1.0
        m["c_flags"] = fl
        maps.append(m)
    return maps


def kernel(**inputs):
    nc, P, S = build()
    maps = make_in_maps(inputs)
    res = run_bass_kernel_spmd(nc, maps, core_ids=list(range(8)))
    outp = np.empty((4, SEQ, D), np.float32)
    for c in range(8):
        b, half = c // 2, c % 2
        outp[b, half * 2048:(half + 1) * 2048] = res.results[c]["out"]
    return outp
```
